# Optimizing a Trainium2 kernel written in Bass

```python
import math
import jax, jax.numpy as jnp
from jax import lax
import numpy as np

D_MODEL = 1024
BATCH = 4
SEQ = 8192
DEPTH = 2

N_MIXERS = 2
N_META = 16
RMS_EPS = 1e-6
SB_HEADS = 16
SB_HEAD_DIM = D_MODEL // SB_HEADS
SB_BLOCK = 128
POOL_WINDOWS = (2, 4, 8, 16)
POOL_GROUP = D_MODEL // len(POOL_WINDOWS)
PEER_HEADS = 8
PEER_N_KEYS = 128
PEER_N_EXPERTS = PEER_N_KEYS * PEER_N_KEYS
PEER_TOPK = 16
PEER_QUERY_DIM = 256
PEER_HALF = PEER_QUERY_DIM // 2
PEER_CHUNK = 256

N_A_LAYERS = (DEPTH + N_MIXERS - 1) // N_MIXERS
N_B_LAYERS = DEPTH // N_MIXERS

kernel_name = "sb_pool_peer_hybrid"


def rms_norm(x, gain):
    x32 = x.astype(jnp.float32)
    y = x32 * lax.rsqrt(jnp.mean(x32 * x32, axis=-1, keepdims=True) + RMS_EPS)
    return (y * gain.astype(jnp.float32)).astype(x.dtype)


def stick_breaking_attention(xn, w_qkv, q_gain, k_gain, w_o):
    B, L, D = xn.shape
    qkv = xn @ w_qkv
    q, k, v = jnp.split(qkv, 3, axis=-1)
    q = rms_norm(q.reshape(B, L, SB_HEADS, SB_HEAD_DIM), q_gain)
    k = rms_norm(k.reshape(B, L, SB_HEADS, SB_HEAD_DIM), k_gain)
    v = v.reshape(B, L, SB_HEADS, SB_HEAD_DIM)
    pad = SB_BLOCK - N_META
    padf = lambda a: jnp.pad(a, ((0, 0), (pad, 0), (0, 0), (0, 0)))
    q, k, v = padf(q), padf(k), padf(v)
    Lp = L + pad
    nb = Lp // SB_BLOCK
    kh = k.transpose(0, 2, 1, 3)
    vh = v.transpose(0, 2, 1, 3)
    qb = q.reshape(B, nb, SB_BLOCK, SB_HEADS, SB_HEAD_DIM).transpose(1, 0, 3, 2, 4)
    key_pos = jnp.arange(Lp)
    scale = 1.0 / math.sqrt(SB_HEAD_DIM)

    def block(args):
        bi, qblk = args
        q_pos = bi * SB_BLOCK + jnp.arange(SB_BLOCK)
        z = jnp.einsum('bhqd,bhkd->bhqk', qblk, kh).astype(jnp.float32) * scale
        mask = (key_pos[None, :] < q_pos[:, None]) & (key_pos[None, :] >= pad)
        log_keep = jnp.where(mask, jax.nn.log_sigmoid(-z), 0.0)
        log_after = lax.cumsum(log_keep, axis=3, reverse=True) - log_keep
        a = jnp.where(mask, jnp.exp(jax.nn.log_sigmoid(z) + log_after), 0.0)
        return jnp.einsum('bhqk,bhkd->bhqd', a.astype(vh.dtype), vh)

    o = lax.map(block, (jnp.arange(nb), qb))
    o = o.transpose(1, 0, 3, 2, 4).reshape(B, Lp, D)[:, pad:]
    return o @ w_o


def pool_mixer(xn, w_grp, scale):
    B, L, D = xn.shape
    x32 = xn.astype(jnp.float32)
    c = jnp.concatenate([jnp.zeros((B, 1, D), jnp.float32), jnp.cumsum(x32, axis=1)], axis=1)
    t = jnp.arange(L)
    outs = []
    for g, w in enumerate(POOL_WINDOWS):
        lo, hi = g * POOL_GROUP, (g + 1) * POOL_GROUP
        cg = c[:, :, lo:hi]
        shifted = jnp.concatenate([jnp.zeros((B, w - 1, POOL_GROUP), jnp.float32), cg[:, :L - w + 1]], axis=1)
        cnt = jnp.minimum(t + 1, w).astype(jnp.float32)
        mean = (cg[:, 1:] - shifted) / cnt[None, :, None]
        outs.append(mean - x32[:, :, lo:hi])
    pooled = jnp.stack(outs, axis=2).astype(xn.dtype)
    y = jnp.einsum('blgc,gcd->blgd', pooled, w_grp).reshape(B, L, D)
    return y * scale


def peer_ffn(xn, w_q, subkeys, u_tab, v_tab):
    B, L, D = xn.shape
    T = B * L
    nc = -(-T // PEER_CHUNK)
    xt = jnp.pad(xn.reshape(T, D), ((0, nc * PEER_CHUNK - T), (0, 0))).reshape(nc, PEER_CHUNK, D)

    def chunk(xc):
        C = xc.shape[0]
        q = (xc @ w_q).reshape(C, PEER_HEADS, 2, PEER_HALF)
        s = jnp.einsum('chpd,pnd->chpn', q, subkeys).astype(jnp.float32)
        s_top, i_top = lax.top_k(s, PEER_TOPK)
        cand = (s_top[:, :, 0, :, None] + s_top[:, :, 1, None, :]).reshape(C, PEER_HEADS, PEER_TOPK * PEER_TOPK)
        cidx = (i_top[:, :, 0, :, None] * PEER_N_KEYS + i_top[:, :, 1, None, :]).reshape(C, PEER_HEADS, PEER_TOPK * PEER_TOPK)
        sc, pos = lax.top_k(cand, PEER_TOPK)
        idx = jnp.take_along_axis(cidx, pos, axis=-1)
        g = jax.nn.softmax(sc, axis=-1)
        u = jnp.take(u_tab, idx, axis=0)
        act = jax.nn.gelu(jnp.einsum('chkd,cd->chk', u, xc).astype(jnp.float32), approximate=False)
        v = jnp.take(v_tab, idx, axis=0)
        return jnp.einsum('chk,chkd->cd', (g * act).astype(xc.dtype), v)

    out = lax.map(chunk, xt).reshape(nc * PEER_CHUNK, D)[:T]
    return out.reshape(B, L, D)


def setup_inputs(seed: int = 0) -> dict:
    key = jax.random.key(seed)
    ks = jax.random.split(key, 16)
    D = D_MODEL
    nrm = lambda k, shape, s: jax.random.normal(k, shape, jnp.float32) * s
    return {
        "x": nrm(ks[0], (BATCH, SEQ, D), 1.0),
        "meta": nrm(ks[1], (N_META, D), 1.0),
        "norm_mix": 1.0 + nrm(ks[2], (DEPTH, D), 0.05),
        "norm_ffn": 1.0 + nrm(ks[3], (DEPTH, D), 0.05),
        "sb_w_qkv": nrm(ks[4], (N_A_LAYERS, D, 3 * D), D ** -0.5),
        "sb_q_gain": 1.0 + nrm(ks[5], (N_A_LAYERS, SB_HEAD_DIM), 0.05),
        "sb_k_gain": 1.0 + nrm(ks[6], (N_A_LAYERS, SB_HEAD_DIM), 0.05),
        "sb_w_o": nrm(ks[7], (N_A_LAYERS, D, D), D ** -0.5),
        "pool_w": nrm(ks[8], (N_B_LAYERS, len(POOL_WINDOWS), POOL_GROUP, POOL_GROUP), POOL_GROUP ** -0.5),
        "pool_scale": 1.0 + nrm(ks[9], (N_B_LAYERS, D), 0.05),
        "peer_w_q": nrm(ks[10], (DEPTH, D, PEER_HEADS * PEER_QUERY_DIM), D ** -0.5),
        "peer_subkeys": nrm(ks[11], (DEPTH, 2, PEER_N_KEYS, PEER_HALF), PEER_HALF ** -0.5),
        "peer_u": nrm(ks[12], (DEPTH, PEER_N_EXPERTS, D), D ** -0.5),
        "peer_v": nrm(ks[13], (DEPTH, PEER_N_EXPERTS, D), 0.2),
    }


def reference(x, meta, norm_mix, norm_ffn, sb_w_qkv, sb_q_gain, sb_k_gain, sb_w_o,
              pool_w, pool_scale, peer_w_q, peer_subkeys, peer_u, peer_v):
    B = x.shape[0]
    h = jnp.concatenate([jnp.broadcast_to(meta[None].astype(x.dtype), (B, N_META, x.shape[-1])), x], axis=1)
    for i in range(DEPTH):
        j = i // N_MIXERS
        hn = rms_norm(h, norm_mix[i])
        if i % N_MIXERS == 0:
            h = h + stick_breaking_attention(hn, sb_w_qkv[j], sb_q_gain[j], sb_k_gain[j], sb_w_o[j])
        else:
            h = h + pool_mixer(hn, pool_w[j], pool_scale[j])
        hn = rms_norm(h, norm_ffn[i])
        h = h + peer_ffn(hn, peer_w_q[i], peer_subkeys[i], peer_u[i], peer_v[i])
    return h[:, N_META:]
```

```python
import numpy as np
import ml_dtypes
from contextlib import ExitStack
import concourse.bass as bass
import concourse.mybir as mybir
from concourse.bass_utils import run_bass_kernel_spmd

F32 = mybir.dt.float32
BF16 = mybir.dt.bfloat16
U32 = mybir.dt.uint32
AF = mybir.ActivationFunctionType
ALU = mybir.AluOpType
AX = mybir.AxisListType

D = 1024
NRUN = 8
RUNT = 512
NOWN = NRUN * RUNT
NTOK = NOWN + 128
NTILE = NTOK // 128
LP = 8320
NKB = LP // 128
NHB = 61
NEG = -30000.0
EPS = 1e-6


class Prog:
    ENG = ("pe", "act", "dve", "pool", "sp")

    def __init__(self, nc, es):
        self.nc = nc
        self.es = es
        self.eng = {"pe": nc.tensor, "act": nc.scalar, "dve": nc.vector, "pool": nc.gpsimd, "sp": nc.sync}
        self.esem = {e: es.enter_context(nc.semaphore("prog_" + e)) for e in ("pe", "act", "dve", "pool")}
        self.ecnt = {e: 0 for e in self.esem}
        self.dsem = {}
        self.lastw = {}
        self.readers = {}
        self.waited = {e: {} for e in self.ENG}
        self.nins = 0

    def _wait(self, eng, k, val):
        kind, key = k
        if kind == "d":
            val = max(val, self.dsem[key][1])
        d = self.waited[eng]
        if d.get(k, 0) >= val:
            return
        d[k] = val
        sem = self.esem[key] if kind == "e" else self.dsem[key][0]
        self.eng[eng].wait_ge(sem, val)
        self.nins += 1

    def op(self, eng, fn, reads=(), writes=(), sig=True, dma=False, nowaw=False):
        need = {}

        def add(dct):
            for k, v in dct.items():
                if need.get(k, 0) < v:
                    need[k] = v
        for r in reads:
            add(self.lastw.get(r, {}))
        for w in writes:
            if not nowaw:
                add(self.lastw.get(w, {}))
            add(self.readers.get(w, {}))
        for k, v in need.items():
            if k[0] == "e" and k[1] == eng and eng == "pe" and not dma:
                continue
            self._wait(eng, k, v)
        ins = fn(self.eng[eng])
        self.nins += 1
        if dma:
            key = writes[0] + "|" + (reads[0] if reads else "")
            if key not in self.dsem:
                self.dsem[key] = [self.es.enter_context(self.nc.semaphore("d%d" % len(self.dsem))), 0]
            self.dsem[key][1] += 16
            ins.then_inc(self.dsem[key][0], 16)
            tok = (("d", key), self.dsem[key][1])
        elif sig:
            self.ecnt[eng] += 1
            ins.then_inc(self.esem[eng], 1)
            tok = (("e", eng), self.ecnt[eng])
        else:
            tok = (("e", eng), self.ecnt[eng] + 1)
        for r in reads:
            rd = self.readers.setdefault(r, {})
            rd[tok[0]] = max(rd.get(tok[0], 0), tok[1])
        for w in writes:
            if nowaw:
                lw = self.lastw.setdefault(w, {})
                lw[tok[0]] = max(lw.get(tok[0], 0), tok[1])
            else:
                self.lastw[w] = {tok[0]: tok[1]}
            self.readers[w] = {}
        return tok

    def dma(self, out, in_, reads=(), writes=(), eng="sp", nowaw=False):
        return self.op(eng, lambda e: e.dma_start(out=out, in_=in_), reads=reads, writes=writes, dma=True, nowaw=nowaw)

    def barrier(self):
        toks = [(("e", e), c) for e, c in self.ecnt.items() if c > 0]
        toks += [(("d", k), v[1]) for k, v in self.dsem.items()]
        for eng in self.ENG:
            for k, v in toks:
                self._wait(eng, k, v)
        self.lastw = {}
        self.readers = {}


def bview(ap, shape):
    return ap.to_broadcast(shape)


def emit_norm_T(P, tag, x_sb, x_res, gain_bc, ident_bf, hn_sb, ss_sb, rstd_sb, junk_sb, psT, dst, dst_res):
    P.op("act", lambda e: e.activation(out=junk_sb, in_=x_sb, func=AF.Square, accum_out=ss_sb),
         reads=[x_res], writes=["junk", "ss"])
    P.op("act", lambda e: e.activation(out=ss_sb, in_=ss_sb, func=AF.Sqrt, bias=EPS, scale=1.0 / D),
         reads=["ss"], writes=["ss"])
    P.op("dve", lambda e: e.reciprocal(out=rstd_sb, in_=ss_sb), reads=["ss"], writes=["rstd"])
    P.op("dve", lambda e: e.scalar_tensor_tensor(out=hn_sb, in0=x_sb, scalar=rstd_sb, in1=gain_bc, op0=ALU.mult, op1=ALU.mult),
         reads=[x_res, "rstd", "consts"], writes=["hn"])
    for kc in range(8):
        P.op("pe", lambda e, kc=kc: e.transpose(psT[:, kc, :], hn_sb[:, kc * 128:(kc + 1) * 128], ident_bf),
             reads=["hn", "consts"], writes=["psT"], sig=(kc == 7))
    P.op("act", lambda e: e.copy(out=dst, in_=psT), reads=["psT"], writes=[dst_res])


def build_program(dbg=None):
    nc = bass.Bass("TRN2", target_bir_lowering=False)
    dram_in = {}

    def din(name, shape, dt=F32):
        dram_in[name] = nc.dram_tensor(name, list(shape), dt, kind="ExternalInput").ap()
        return dram_in[name]

    xq = din("xq", [NTOK, D])
    xall = din("xall", [LP, D])
    wqkv = din("wqkv", [D, 3 * D])
    wo = din("wo", [D, D])
    gains = din("gains", [6, D])
    poolw = din("poolw", [4, 256, 256])
    wq_p = din("wq_p", [2, D, 2048])
    skT = din("skT", [2, 2, 128, 128])
    uS = din("uS", [2, 2048, 8192])
    vS = din("vS", [2, 16384, D])
    ident_bf_d = din("ident_bf", [128, 128], BF16)
    ident_f_d = din("ident_f", [128, 128])
    ntri_d = din("ntri", [128, 128], BF16)
    nones_d = din("nones", [128, 128], BF16)
    iota_d = din("iota", [128, 128])
    iotab_d = din("iota_bf", [128, 128], BF16)
    maskd_d = din("maskd", [128, 9, 512], BF16)
    hmask_d = din("hmask", [128, NHB, 128], BF16)
    out = nc.dram_tensor("out", [NOWN, D], F32, kind="ExternalOutput").ap()
    dbg_out = None
    if dbg:
        dbg_out = nc.dram_tensor("dbg", [NTOK, D], F32, kind="ExternalOutput").ap()

    KT = nc.dram_tensor("KT", [16, 64, LP], BF16).ap()
    VS = nc.dram_tensor("VSc", [16, 128, NKB, 64], BF16).ap()
    QT = nc.dram_tensor("QT", [16, 64, NTOK], BF16).ap()
    OT = nc.dram_tensor("OT", [16, 64, NTOK], BF16).ap()
    H = nc.dram_tensor("H", [NTOK, D], F32).ap()
    UB = nc.dram_tensor("UB", [2, 128, 128, 1024], BF16).ap()
    VB = nc.dram_tensor("VB", [2, 16384, D], BF16).ap()
    WQB = nc.dram_tensor("WQB", [2, 128, 8, 2048], BF16).ap()

    with ExitStack() as es:
        P = Prog(nc, es)
        sb = lambda name, shape, dt: es.enter_context(nc.sbuf_tensor(name, list(shape), dt))

        ident_bf = sb("ident_bf_s", [128, 128], BF16)
        ident_f = sb("ident_f_s", [128, 128], F32)
        ntri = sb("ntri_s", [128, 128], BF16)
        nones = sb("nones_s", [128, 128], BF16)
        iota = sb("iota_s", [128, 128], F32)
        iota_bf = sb("iota_bf_s", [128, 128], BF16)
        gain_bc = sb("gain_bc", [128, 6, D], F32)
        for dst, src in ((ident_bf, ident_bf_d), (ident_f, ident_f_d), (ntri, ntri_d), (nones, nones_d), (iota, iota_d), (iota_bf, iotab_d)):
            P.dma(dst[:], src, writes=["consts"], nowaw=True)
        for i in range(6):
            P.dma(gain_bc[:, i, :], gains[i:i + 1, :].partition_broadcast(128), writes=["consts"], nowaw=True)

        es2 = ExitStack()
        NCB = 2
        cb = [es2.enter_context(nc.sbuf_tensor("castbuf%d" % i, [128, 8192], BF16)) for i in range(NCB)]
        pre = []
        n = 0
        for l in range(2):
            usrc = uS[l]
            udst = UB[l].rearrange("i (q a) f -> (i q) (a f)", a=8)
            vsrc = vS[l].rearrange("(q a) d -> q (a d)", a=8)
            vdst = VB[l].rearrange("(q a) d -> q (a d)", a=8)
            for src, dst in ((usrc, udst), (vsrc, vdst)):
                for blk in range(16):
                    def th(src=src, dst=dst, blk=blk, n=n):
                        b, r = cb[n % NCB], "castbuf%d" % (n % NCB)
                        P.dma(b[:], src[blk * 128:(blk + 1) * 128, :], writes=[r], eng="pool")
                        P.dma(dst[blk * 128:(blk + 1) * 128, :], b[:], reads=[r], writes=["UVB"], nowaw=True)
                    pre.append(th)
                    n += 1
        for l in range(2):
            for hf in range(2):
                def th(l=l, hf=hf, n=n):
                    b, r = cb[n % NCB], "castbuf%d" % (n % NCB)
                    bv = b[:].rearrange("p (k n) -> p k n", k=8)
                    P.dma(bv, wq_p[l].rearrange("(k p) n -> p k n", p=128)[:, :, hf * 1024:(hf + 1) * 1024], writes=[r], eng="pool")
                    P.dma(WQB[l, :, :, hf * 1024:(hf + 1) * 1024], bv, reads=[r], writes=["UVB"], nowaw=True)
                pre.insert(0, th)
                n += 1
        pre_state = (pre, es2)

        C = dict(locals())
        emit_attention(nc, P, C)
        if dbg == "attn":
            emit_copy_rows(nc, P, H, dbg_out, NTILE)
            return nc
        emit_peer(nc, P, C, layer=0, ntiles=NTILE, dst=H)
        if dbg == "peer0":
            emit_copy_rows(nc, P, H, dbg_out, NTILE)
            return nc
        emit_pool(nc, P, C)
        if dbg == "pool":
            emit_copy_rows(nc, P, H, dbg_out, NTILE)
            return nc
        emit_peer(nc, P, C, layer=1, ntiles=NOWN // 128, dst=out)
        P.barrier()
        print("[kernel] instructions=%d dma_sems=%d" % (P.nins, len(P.dsem)))
    return nc


def emit_copy_rows(nc, P, src, dst, ntile):
    P.barrier()
    P.dma(dst, src, writes=["dbgout"])
    P.barrier()


def emit_attention(nc, P, C):
    xq, xall, wqkv, wo = C["xq"], C["xall"], C["wqkv"], C["wo"]
    KT, VS, QT, OT, H = C["KT"], C["VS"], C["QT"], C["OT"], C["H"]
    gain_bc, ident_bf, ntri, nones = C["gain_bc"], C["ident_bf"], C["ntri"], C["nones"]
    maskd_d, hmask_d = C["maskd_d"], C["hmask_d"]

    pre, es_pre = C["pre_state"]
    with ExitStack() as es:
        sb = lambda name, shape, dt: es.enter_context(nc.sbuf_tensor(name, list(shape), dt))
        ps = lambda name, shape, dt: es.enter_context(nc.psum_tensor(name, list(shape), dt))
        w_sb = sb("wqkv_s", [128, 8, 3072], BF16)
        P.dma(w_sb[:], wqkv.rearrange("(k p) n -> p k n", p=128), writes=["wqkv"], eng="pool")
        gq = sb("gq", [128, 64], F32)
        P.op("dve", lambda e: e.scalar_tensor_tensor(out=gq[:], in0=gain_bc[:, 5, 0:64], scalar=0.125,
                                                     in1=gain_bc[:, 5, 64:128], op0=ALU.mult, op1=ALU.mult),
             reads=["consts"], writes=["gq"])
        two = lambda name, shape, dt: [sb("%s%d" % (name, i), shape, dt) for i in range(2)]
        x_t = two("xt", [128, D], F32)
        hn = two("hn", [128, D], BF16)
        junk = two("junk", [128, D], BF16)
        ss = two("ss", [128, 1], F32)
        rstd = two("rstd", [128, 1], F32)
        hnT = two("hnT", [128, 8, 128], BF16)
        sq = two("sq", [128, D], F32)
        ssh = two("ssh", [128, 16], F32)
        rsh = two("rsh", [128, 16], F32)
        kn = two("kn", [128, D], BF16)
        knT = two("knT", [128, 8, 128], BF16)
        v_sb = two("v_sb", [128, D], BF16)
        psT_ = ps("psT", [128, 8, 128], BF16)
        psT2_ = ps("psT2", [128, 8, 128], BF16)
        psT = [psT_, psT_]
        psT2 = [psT2_, psT2_]
        psKs = [ps("psK%d" % i, [128, D], F32) for i in range(2)]
        psV = ps("psV", [128, D], F32)

        def proj(dst_ps, res, hT, hres, col0):
            for half in range(2):
                for kc in range(8):
                    P.op("pe", f_mm(dst_ps[:, half * 512:(half + 1) * 512], hT[:, kc, :],
                                    w_sb[:, kc, col0 + half * 512: col0 + (half + 1) * 512], kc == 0, kc == 7),
                         reads=[hres, "wqkv"], writes=[res], sig=(kc == 7))

        def headnorm_1(is_q, b):
            sfx = str(b)
            psK, pkr = psKs[b], "psK%d" % b
            hv = lambda ap: ap.rearrange("p (h d) -> p h d", d=64)
            P.op("act", f_act(sq[b][:], psK[:], AF.Square), reads=[pkr], writes=["sq" + sfx])
            P.op("dve", f_red(ssh[b][:], hv(sq[b][:])), reads=["sq" + sfx], writes=["ssh" + sfx])
            P.op("act", f_act(ssh[b][:], ssh[b][:], AF.Sqrt, bias=EPS, scale=1.0 / 64), reads=["ssh" + sfx], writes=["ssh" + sfx])
            P.op("dve", f_rcp(rsh[b][:], ssh[b][:]), reads=["ssh" + sfx], writes=["rsh" + sfx])
            rb = rsh[b][:].unsqueeze(2).to_broadcast([128, 16, 64])
            if is_q:
                P.op("dve", f_tt(hv(sq[b][:]), hv(psK[:]), rb, ALU.mult), reads=[pkr, "rsh" + sfx, "sq" + sfx], writes=["sq" + sfx])
                P.op("dve", f_tt(hv(kn[b][:]), hv(sq[b][:]), gq[:].unsqueeze(1).to_broadcast([128, 16, 64]), ALU.mult),
                     reads=["sq" + sfx, "gq"], writes=["kn" + sfx])
            else:
                P.op("dve", f_tt(hv(kn[b][:]), hv(psK[:]), rb, ALU.mult), reads=[pkr, "rsh" + sfx], writes=["kn" + sfx])

        def headnorm_2(n_):
            rows, is_q, t = jobs[n_]
            b = n_ % 2
            sfx = str(b)
            for pr in range(8):
                P.op("pe", f_tr(psT2[b][:, pr, :], kn[b][:, pr * 128:(pr + 1) * 128], ident_bf[:]),
                     reads=["kn" + sfx, "consts"], writes=["psT2"], sig=(pr == 7))
            P.op("dve", f_cp(knT[b][:], psT2[b][:]), reads=["psT2"], writes=["knT" + sfx])
            dstT = QT if is_q else KT
            for hh in range(2):
                P.dma(dstT[hh::2, :, t * 128:(t + 1) * 128].rearrange("h p t -> p h t"), knT[b][hh * 64:(hh + 1) * 64, :, :],
                      reads=["knT%d" % b], writes=["QT" if is_q else "KT"], nowaw=True)

        def load_x(src_rows, t):
            b = t % 2
            P.dma(x_t[b][:], src_rows, writes=["xt%d" % b])

        def norm_a(t):
            b = t % 2
            sfx = str(b)
            xb, xr = x_t[b], "xt" + sfx
            P.op("act", f_act(junk[b][:], xb[:], AF.Square, accum_out=ss[b][:]), reads=[xr], writes=["junk" + sfx, "ss" + sfx])
            P.op("act", f_act(ss[b][:], ss[b][:], AF.Sqrt, bias=EPS, scale=1.0 / D), reads=["ss" + sfx], writes=["ss" + sfx])
            P.op("dve", f_rcp(rstd[b][:], ss[b][:]), reads=["ss" + sfx], writes=["rstd" + sfx])
            P.op("dve", f_stt(hn[b][:], xb[:], rstd[b][:], gain_bc[:, 0, :], ALU.mult, ALU.mult),
                 reads=[xr, "rstd" + sfx, "consts"], writes=["hn" + sfx])

        def norm_b(t):
            b = t % 2
            sfx = str(b)
            for kc in range(8):
                P.op("pe", f_tr(psT[b][:, kc, :], hn[b][:, kc * 128:(kc + 1) * 128], ident_bf[:]),
                     reads=["hn" + sfx, "consts"], writes=["psT"], sig=(kc == 7))
            P.op("act", f_acp(hnT[b][:], psT[b][:]), reads=["psT"], writes=["hnT" + sfx])
            return hnT[b], "hnT" + sfx

        npop = [0]

        def pop_pre():
            if pre and npop[0] < 36:
                pre.pop(0)()
                npop[0] += 1

        jobs = [(xall[t * 128:(t + 1) * 128, :], False, t) for t in range(NKB)] + [(xq[t * 128:(t + 1) * 128, :], True, t) for t in range(NTILE)]
        NJ = len(jobs)
        load_x(jobs[0][0], 0)
        load_x(jobs[1][0], 1)
        norm_a(0)
        nxt = norm_b(0)
        for n_, (rows, is_q, t) in enumerate(jobs):
            b = n_ % 2
            hT, hres = nxt
            if n_ + 2 < NJ:
                load_x(jobs[n_ + 2][0], n_ + 2)
            if n_ + 1 < NJ:
                norm_a(n_ + 1)
            if not is_q:
                proj(psKs[b], "psK%d" % b, hT, hres, 1024)
                proj(psV, "psV", hT, hres, 2048)
            else:
                proj(psKs[b], "psK%d" % b, hT, hres, 0)
            if n_ >= 1:
                headnorm_2(n_ - 1)
            if n_ + 1 < NJ:
                nxt = norm_b(n_ + 1)
            if not is_q:
                P.op("act", f_acp(v_sb[b][:], psV[:]), reads=["psV"], writes=["v_sb%d" % b])
                P.dma(VS[:, :, t, :].rearrange("h p d -> p h d"), v_sb[b][:].rearrange("p (h d) -> p h d", d=64),
                      reads=["v_sb%d" % b], writes=["VS"], nowaw=True)
            headnorm_1(is_q, b)
            pop_pre()
        headnorm_2(NJ - 1)
        while pre and npop[0] < 36:
            pop_pre()
        P.barrier()

    with ExitStack() as es:
        sb = lambda name, shape, dt: es.enter_context(nc.sbuf_tensor(name, list(shape), dt))
        ps = lambda name, shape, dt: es.enter_context(nc.psum_tensor(name, list(shape), dt))
        maskd = sb("maskd_s", [128, 9, 512], BF16)
        hmask = sb("hmask_s", [128, NHB, 128], BF16)
        P.dma(maskd[:], maskd_d, writes=["masks"], nowaw=True)
        P.dma(hmask[:], hmask_d, writes=["masks"], nowaw=True)
        kt = [sb("kt%d" % i, [64, LP], BF16) for i in range(2)]
        vt = [sb("vt%d" % i, [128, NKB, 64], BF16) for i in range(2)]
        qt = [sb("qt%d" % i, [64, NTOK], BF16) for i in range(2)]
        ot = [sb("ot%d" % i, [64, NTOK], BF16) for i in range(2)]
        Eb = [sb("E%d" % i, [128, 512], BF16) for i in range(3)]
        SPb = [sb("SP%d" % i, [128, 512], BF16) for i in range(3)]
        Ab = [sb("A%d" % i, [128, 512], BF16) for i in range(3)]
        RSb = [sb("RS%d" % i, [128, 512], BF16) for i in range(2)]
        ps1 = [ps("ps1_%d" % i, [128, 512], F32) for i in range(4)]
        pso = [ps("pso%d" % i, [64, 512], F32) for i in range(2)]

        groups = [(r, r * RUNT, RUNT, 8 * r + 9) for r in range(NRUN)] + [(-1, NOWN, 128, NHB)]
        steps = []
        gidx = 0
        for h in range(16):
            for gi_, (r, off, N, nkb) in enumerate(groups):
                for s_, j in enumerate(range(nkb - 1, -1, -1)):
                    steps.append(dict(h=h, b=h % 2, r=r, off=off, N=N, nkb=nkb, s=s_, j=j, g=gidx,
                                      first=(gi_ == 0 and s_ == 0), last_of_head=(gi_ == len(groups) - 1 and s_ == nkb - 1)))
                gidx += 1

        def mask_for(d):
            r, j = d["r"], d["j"]
            if r < 0:
                return hmask[:, j, :]
            if j == 0:
                return maskd[:, 0, :]
            if j >= 8 * r + 1:
                return maskd[:, 1 + j - (8 * r + 1), :]
            return None

        def S1(G):
            d = steps[G]
            h, b, N, off, j = d["h"], d["b"], d["N"], d["off"], d["j"]
            if d["first"]:
                P.dma(kt[b][:], KT[h], reads=["KT"], writes=["kt%d" % b])
                P.dma(vt[b][:], VS[h], reads=["VS"], writes=["vt%d" % b])
                P.dma(qt[b][:], QT[h], reads=["QT"], writes=["qt%d" % b])
            if d["s"] == 0 and d["g"] % 4 == 0 and pre:
                pre.pop(0)()
            p1, pres = ps1[G % 4], "ps1_%d" % (G % 4)
            m = mask_for(d)
            P.op("pe", f_mm(p1[:, :N], kt[b][:, j * 128:(j + 1) * 128], qt[b][:, off:off + N], True, m is None),
                 reads=["kt%d" % b, "qt%d" % b], writes=[pres], sig=(m is None))
            if m is not None:
                P.op("pe", f_mm(p1[:, :N], ident_bf[:], m, False, True), reads=["masks", "consts"], writes=[pres])
            P.op("act", f_act(Eb[G % 3][:, :N], p1[:, :N], AF.Exp), reads=[pres], writes=["E%d" % (G % 3)])

        def S1b(G):
            N = steps[G]["N"]
            P.op("act", f_act(SPb[G % 3][:, :N], Eb[G % 3][:, :N], AF.Ln, bias=1.0), reads=["E%d" % (G % 3)], writes=["SP%d" % (G % 3)])

        def S2a(G):
            d = steps[G]
            N, s_, nkb = d["N"], d["s"], d["nkb"]
            p1, pres = ps1[G % 4], "ps1_%d" % (G % 4)
            P.op("pe", lambda e: e.matmul(p1[:, :N], lhsT=ntri[:], rhs=SPb[G % 3][:, :N], start=False, stop=(s_ == 0), skip_group_check=True),
                 reads=["SP%d" % (G % 3), "consts"], writes=[pres], sig=(s_ == 0))
            if s_ > 0:
                P.op("pe", lambda e: e.matmul(p1[:, :N], lhsT=nones[:], rhs=RSb[(G - 1) % 2][:, :N], start=False, stop=True, skip_group_check=True),
                     reads=["RS%d" % ((G - 1) % 2), "consts"], writes=[pres])
            if s_ < nkb - 1:
                if s_ == 0:
                    P.op("dve", f_cp(RSb[G % 2][:, :N], SPb[G % 3][:, :N]), reads=["SP%d" % (G % 3)], writes=["RS%d" % (G % 2)])
                else:
                    P.op("dve", f_tt(RSb[G % 2][:, :N], RSb[(G - 1) % 2][:, :N], SPb[G % 3][:, :N], ALU.add),
                         reads=["RS%d" % ((G - 1) % 2), "SP%d" % (G % 3)], writes=["RS%d" % (G % 2)])
            P.op("act", f_act(Ab[G % 3][:, :N], p1[:, :N], AF.Exp), reads=[pres], writes=["A%d" % (G % 3)])

        def S2b(G):
            d = steps[G]
            h, b, N, off, j, s_, nkb = d["h"], d["b"], d["N"], d["off"], d["j"], d["s"], d["nkb"]
            o_ps, ores = pso[d["g"] % 2], "pso%d" % (d["g"] % 2)
            P.op("pe", f_mm(o_ps[:, :N], vt[b][:, j, :], Ab[G % 3][:, :N], s_ == 0, s_ == nkb - 1),
                 reads=["A%d" % (G % 3), "vt%d" % b], writes=[ores], sig=(s_ == nkb - 1))
            if s_ == nkb - 1:
                P.op("dve", f_cp(ot[b][:, off:off + N], o_ps[:, :N]), reads=[ores], writes=["ot%d" % b])
            if d["last_of_head"]:
                P.dma(OT[h], ot[b][:], reads=["ot%d" % b], writes=["OT"], nowaw=True)

        NS = len(steps)
        for G in range(NS + 3):
            if G < NS:
                S1(G)
            if 0 <= G - 1 < NS:
                S1b(G - 1)
            if 0 <= G - 2 < NS:
                S2a(G - 2)
            if 0 <= G - 3 < NS:
                S2b(G - 3)
        while pre:
            pre.pop(0)()
        P.barrier()
    es_pre.close()

    with ExitStack() as es:
        sb = lambda name, shape, dt: es.enter_context(nc.sbuf_tensor(name, list(shape), dt))
        ps = lambda name, shape, dt: es.enter_context(nc.psum_tensor(name, list(shape), dt))
        wo_sb = sb("wo_s", [64, 16, D], BF16)
        P.dma(wo_sb[:], wo.rearrange("(h p) n -> p h n", p=64), writes=["wo"], eng="pool")
        ott = [sb("ott%d" % i, [64, 16, 128], BF16) for i in range(2)]
        xt = [sb("xc%d" % i, [128, D], F32) for i in range(2)]
        res = [sb("res%d" % i, [128, D], F32) for i in range(2)]
        psC = [ps("psC%d" % i, [128, D], F32) for i in range(2)]
        def loadC(t):
            b = t % 2
            P.dma(ott[b][:], OT[:, :, t * 128:(t + 1) * 128].rearrange("h p t -> p h t"), reads=["OT"], writes=["ott%d" % b])
            P.dma(xt[b][:], xq[t * 128:(t + 1) * 128, :], writes=["xc%d" % b])

        loadC(0)
        for t in range(NTILE):
            b = t % 2
            if t + 1 < NTILE:
                loadC(t + 1)
            for half in range(2):
                for h in range(16):
                    P.op("pe", lambda e, half=half, h=h: e.matmul(psC[b][:, half * 512:(half + 1) * 512], lhsT=ott[b][:, h, :],
                                                                  rhs=wo_sb[:, h, half * 512:(half + 1) * 512],
                                                                  start=(h == 0), stop=(h == 15)),
                         reads=["ott%d" % b, "wo"], writes=["psC%d" % b], sig=(h == 15))
            P.op("dve", lambda e: e.tensor_tensor(out=res[b][:], in0=psC[b][:], in1=xt[b][:], op=ALU.add),
                 reads=["psC%d" % b, "xc%d" % b], writes=["res%d" % b])
            P.dma(H[t * 128:(t + 1) * 128, :], res[b][:], reads=["res%d" % b], writes=["H"], nowaw=True)
        P.barrier()


def f_max(o, i): return lambda e: e.max(out=o, in_=i)
def f_mr(o, r, v): return lambda e: e.match_replace(out=o, in_to_replace=r, in_values=v, imm_value=-1e30)
def f_mi(o, m, v): return lambda e: e.max_index(out=o, in_max=m, in_values=v)
def f_tt(o, a, b, op): return lambda e: e.tensor_tensor(out=o, in0=a, in1=b, op=op)
def f_cp(o, i): return lambda e: e.tensor_copy(out=o, in_=i)
def f_acp(o, i): return lambda e: e.copy(out=o, in_=i)
def f_red(o, i): return lambda e: e.tensor_reduce(out=o, in_=i, axis=AX.X, op=ALU.add)
def f_mm(o, l, r, st, sp): return lambda e: e.matmul(o, lhsT=l, rhs=r, start=st, stop=sp)
def f_tr(o, i, ident): return lambda e: e.transpose(o, i, ident)
def f_act(o, i, func, **kw): return lambda e: e.activation(out=o, in_=i, func=func, **kw)
def f_stt(o, a, sc, b, op0, op1): return lambda e: e.scalar_tensor_tensor(out=o, in0=a, scalar=sc, in1=b, op0=op0, op1=op1)
def f_rcp(o, i): return lambda e: e.reciprocal(out=o, in_=i)


def emit_peer(nc, P, C, layer, ntiles, dst):
    H, WQB, skT, UB, VB = C["H"], C["WQB"], C["skT"], C["UB"], C["VB"]
    gain_bc, ident_bf, ident_f, iota, iota_bf = C["gain_bc"], C["ident_bf"], C["ident_f"], C["iota"], C["iota_bf"]
    NT = 2
    TM = NT * 128
    NB = 5
    with ExitStack() as es:
        sb = lambda name, shape, dt: es.enter_context(nc.sbuf_tensor("%s_L%d" % (name, layer), list(shape), dt))
        ps = lambda name, shape, dt: es.enter_context(nc.psum_tensor("%s_L%d" % (name, layer), list(shape), dt))
        big = sb("big", [128, 16384], BF16)
        bigf = big[:].bitcast(F32)
        S, S2, EQ, TMP = (bigf[:, i * 2048:(i + 1) * 2048] for i in range(4))
        JR = [(big[:, (2 * x) * 4096:(2 * x + 1) * 4096].rearrange("p (t j) -> p t j", j=128),
               big[:, (2 * x + 1) * 4096:(2 * x + 2) * 4096].rearrange("p (t j) -> p t j", j=128)) for x in range(2)]
        JRn = ["JmA", "RmA", "JmB", "RmB"]
        GT = sb("GT", [128, TM, 128], BF16)
        hnT = [sb("hnTp%d" % i, [128, 8, TM], BF16) for i in range(2)]
        qT = sb("qT", [128, 16, TM], BF16)
        wqs = [sb("wqs%d" % i, [128, 8, 128], BF16) for i in range(2)]
        skT_sb = sb("skT_s", [128, 2, 128], BF16)
        P.dma(skT_sb[:], skT[layer].rearrange("s d n -> d s n"), writes=["skT"], eng="pool")
        xt = [sb("xp%d" % i, [128, D], F32) for i in range(2)]
        hn = sb("hnp", [128, D], BF16)
        ss = sb("ssp", [128, 1], F32)
        rstd = sb("rstdp", [128, 1], F32)
        mx = sb("mx", [128, 16, 16], F32)
        ixu = sb("ixu", [128, 16, 16], U32)
        ixf = sb("ixf", [128, 16, 16], F32)
        posu = sb("posu", [128, 8, 16], U32)
        posf = sb("posf", [128, 8, 16], F32)
        aq = sb("aq", [128, 8, 16], F32)
        bq = sb("bq", [128, 8, 16], F32)
        ex3 = sb("ex3", [128, 8, 16], F32)
        Z = sb("Z", [128, 8], F32)
        rZ = sb("rZ", [128, 8], F32)
        selT = sb("selT", [128, 3, TM], F32)
        ub = [sb("ub%d" % i, [128, 1024], BF16) for i in range(NB)]
        vb = [sb("vb%d" % i, [128, 1024], BF16) for i in range(NB)]
        gl = [sb("gl%d" % i, [128, TM], BF16) for i in range(2)]
        wt = [sb("wt%d" % i, [128, TM], BF16) for i in range(3)]
        c16 = sb("c16", [128, 32], F32)
        P.op("dve", lambda e: e.tensor_single_scalar(out=c16[:], in_=iota[:, 0:32], scalar=16.0, op=ALU.mult), reads=["consts"], writes=["c16"])
        acc = [ps("acc%d" % i, [128, 512], F32) for i in range(4)]
        psA = [ps("psA%d" % i, [128, 512], F32) for i in range(2)]
        rt = [ps("rtb%d" % i, [128, 512], F32) for i in range(2)]
        psT = rt[0][:].bitcast(BF16)[:, 0:1024].rearrange("p (k t) -> p k t", k=8)
        psSel = rt[0][:, 0:384].rearrange("p (m t) -> p m t", m=3)
        gain = gain_bc[:, 2 + layer, :]

        tiles = list(range(ntiles))
        groups = [tiles[i:i + NT] for i in range(0, ntiles, NT)]
        B4 = [128, 8, 16, 16]
        B3 = [128, 8, 16]

        SX = [sb("SX%d" % i, [128, 2048], F32)[:] for i in range(2)]
        S3 = EQ
        bigb = big[:]
        EQb = bigb[:, 12288:14336].rearrange("p (h k a) -> p h k a", h=8, k=16)
        TMb = bigb[:, 14336:16384].rearrange("p (h k a) -> p h k a", h=8, k=16)
        scs = [sb("sc%d" % i, [128, 8, 16], F32) for i in range(2)]
        sels = [sb("sel%d" % i, [128, 3, 128], F32) for i in range(2)]

        bT = acc[0][:].bitcast(BF16)[:, 0:1024].rearrange("p (k t) -> p k t", k=8)

        def burst(gi):
            gt = groups[gi]
            T = len(gt) * 128
            hT, hres = hnT[gi % 2], "hnT%d" % (gi % 2)
            L = []
            R = lambda eng, fn, **kw: L.append(lambda: P.op(eng, fn, **kw))
            for ti, t in enumerate(gt):
                xb, xr = xt[ti % 2], "xp%d" % (ti % 2)
                L.append(lambda xb=xb, xr=xr, t=t: P.dma(xb[:], H[t * 128:(t + 1) * 128, :], reads=["H"], writes=[xr]))
                R("act", f_act(hn[:], xb[:], AF.Square, accum_out=ss[:]), reads=[xr], writes=["hn", "ss"])
                R("act", f_act(ss[:], ss[:], AF.Sqrt, bias=EPS, scale=1.0 / D), reads=["ss"], writes=["ss"])
                R("dve", f_rcp(rstd[:], ss[:]), reads=["ss"], writes=["rstd"])
                R("dve", f_stt(hn[:], xb[:], rstd[:], gain, ALU.mult, ALU.mult), reads=[xr, "rstd", "consts"], writes=["hn"])
                for kc in range(8):
                    R("pe", f_tr(bT[:, kc, :], hn[:, kc * 128:(kc + 1) * 128], ident_bf[:]), reads=["hn", "consts"], writes=["acc0"], sig=(kc == 7))
                R("act", f_acp(hT[:, :, ti * 128:(ti + 1) * 128], bT), reads=["acc0"], writes=[hres])
            for g in range(16):
                wb, wr = wqs[g % 2], "wqs%d" % (g % 2)
                L.append(lambda wb=wb, wr=wr, g=g: P.dma(wb[:], WQB[layer, :, :, g * 128:(g + 1) * 128], reads=["UVB"], writes=[wr]))
                pq, pr = acc[1 + g % 2], "acc%d" % (1 + g % 2)
                for kc in range(8):
                    R("pe", f_mm(pq[:, :T], wb[:, kc, :], hT[:, kc, :T], kc == 0, kc == 7), reads=[wr, hres], writes=[pr], sig=(kc == 7))
                R("act", f_acp(qT[:, g, :T], pq[:, :T]), reads=[pr], writes=["qT"])
            for ti, t in enumerate(gt):
                for q4 in range(4):
                    pq, pr = acc[1 + q4 % 2], "acc%d" % (1 + q4 % 2)
                    for gg in range(4):
                        g = q4 * 4 + gg
                        R("pe", f_mm(pq[:, gg * 128:(gg + 1) * 128], qT[:, g, ti * 128:(ti + 1) * 128], skT_sb[:, g % 2, :], True, True),
                          reads=["qT", "skT"], writes=[pr], sig=(gg == 3))
                    R("act", f_acp(SX[ti][:, q4 * 512:(q4 + 1) * 512], pq[:]), reads=[pr], writes=["SX%d" % ti])
            return L

        def routing_thunks(gi):
            gt = groups[gi]
            L = []
            R = lambda eng, fn, **kw: L.append(lambda: P.op(eng, fn, **kw))
            for ti, t in enumerate(gt):
                RT = dict(reads=["rt", "SX%d" % ti], writes=["rt", "SX%d" % ti] + (JRn if ti == 0 else []))
                RC = dict(reads=["rt", "consts", "c16", "SX%d" % ti], writes=["rt", "SX%d" % ti])
                Sx = SX[ti]
                sc, sel = scs[ti], sels[ti]
                sxn = "SX%d" % ti
                G16 = range(16)
                sl16 = [slice(g * 128, (g + 1) * 128) for g in G16]
                for g in G16:
                    R("dve", f_max(mx[:, g, 0:8], Sx[:, sl16[g]]), reads=[sxn, "rt"], writes=["mxa%d" % g])
                for g in G16:
                    R("dve", f_mr(S3[:, sl16[g]], mx[:, g, 0:8], Sx[:, sl16[g]]), reads=[sxn, "mxa%d" % g, "rt"], writes=["S3_%d" % g] + (JRn if ti == 0 else []))
                for g in G16:
                    R("dve", f_max(mx[:, g, 8:16], S3[:, sl16[g]]), reads=["S3_%d" % g], writes=["mxb%d" % g])
                for g in G16:
                    R("dve", f_mi(ixu[:, g, 0:8], mx[:, g, 0:8], Sx[:, sl16[g]]), reads=[sxn, "mxa%d" % g], writes=["ixa%d" % g])
                for g in G16:
                    R("dve", f_mi(ixu[:, g, 8:16], mx[:, g, 8:16], S3[:, sl16[g]]), reads=["S3_%d" % g, "mxb%d" % g], writes=["ixb%d" % g])
                allmx = ["mxa%d" % g for g in G16] + ["mxb%d" % g for g in G16]
                allix = ["ixa%d" % g for g in G16] + ["ixb%d" % g for g in G16]
                alls3 = ["S3_%d" % g for g in G16]
                R("dve", f_cp(ixf[:], ixu[:]), reads=allix + ["rt"], writes=["rt"])
                mxv = mx[:].rearrange("p (h two) k -> p h two k", two=2)
                ixv = ixf[:].rearrange("p (h two) k -> p h two k", two=2)
                cand4 = Sx.rearrange("p (h a b) -> p h a b", h=8, a=16)
                R("dve", f_tt(cand4, mxv[:, :, 0, :].unsqueeze(3).to_broadcast(B4), mxv[:, :, 1, :].unsqueeze(2).to_broadcast(B4), ALU.add),
                  reads=allmx + allix + [sxn, "rt"], writes=[sxn, "rt"])
                H8 = range(8)
                sl8 = [slice(h * 256, (h + 1) * 256) for h in H8]
                for h in H8:
                    R("dve", f_max(sc[:, h, 0:8], Sx[:, sl8[h]]), reads=[sxn, "rt"], writes=["sca%d" % h])
                for h in H8:
                    R("dve", f_mr(S3[:, sl8[h]], sc[:, h, 0:8], Sx[:, sl8[h]]), reads=[sxn, "sca%d" % h] + alls3[2 * h:2 * h + 2],
                      writes=alls3[2 * h:2 * h + 2])
                for h in H8:
                    R("dve", f_max(sc[:, h, 8:16], S3[:, sl8[h]]), reads=alls3[2 * h:2 * h + 2], writes=["scb%d" % h])
                for h in H8:
                    R("dve", f_mi(posu[:, h, 0:8], sc[:, h, 0:8], Sx[:, sl8[h]]), reads=[sxn, "sca%d" % h, "rt"], writes=["posa%d" % h])
                for h in H8:
                    R("dve", f_mi(posu[:, h, 8:16], sc[:, h, 8:16], S3[:, sl8[h]]), reads=alls3[2 * h:2 * h + 2] + ["scb%d" % h, "rt"], writes=["posb%d" % h])
                allpos = ["posa%d" % h for h in H8] + ["posb%d" % h for h in H8]
                allsc = ["sca%d" % h for h in H8] + ["scb%d" % h for h in H8]
                RT = dict(reads=["rt", sxn] + allpos + allsc + alls3 + allmx, writes=["rt", sxn] + allpos + alls3 + (JRn if ti == 0 else []))
                RC = dict(reads=RT["reads"] + ["consts", "c16"], writes=RT["writes"])
                R("dve", f_cp(posf[:], posu[:]), **RT)
                io4 = iota[:, 0:16].unsqueeze(1).unsqueeze(1).to_broadcast(B4)
                selv = lambda m, sel=sel: sel[:, m, :].rearrange("p (h k) -> p h k", h=8)
                R("dve", lambda e: e.tensor_single_scalar(out=aq[:], in_=posf[:], scalar=0.0625, op=ALU.mult), **RT)
                R("dve", f_cp(posu[:], aq[:]), **RT)
                R("dve", f_cp(aq[:], posu[:]), **RT)
                R("dve", lambda e: e.tensor_single_scalar(out=bq[:], in_=aq[:], scalar=16.0, op=ALU.mult), **RT)
                R("dve", f_tt(bq[:], bq[:], posf[:], ALU.is_gt), **RT)
                R("dve", f_tt(aq[:], aq[:], bq[:], ALU.subtract), **RT)
                R("dve", f_stt(bq[:], aq[:], -16.0, posf[:], ALU.mult, ALU.add), **RT)
                R("dve", f_tt(EQb, aq[:].unsqueeze(3).to_broadcast(B4), io4, ALU.is_equal), **RC)
                R("dve", f_tt(TMb, EQb, ixv[:, :, 0, :].unsqueeze(2).to_broadcast(B4), ALU.mult), **RT)
                R("dve", f_red(selv(0), TMb), **RT)
                R("dve", f_tt(EQb, bq[:].unsqueeze(3).to_broadcast(B4), io4, ALU.is_equal), **RC)
                R("dve", f_tt(TMb, EQb, ixv[:, :, 1, :].unsqueeze(2).to_broadcast(B4), ALU.mult), **RT)
                R("dve", f_red(selv(1), TMb), **RT)
            return L

        def rtail(gi):
            gt = groups[gi]
            L = []
            R = lambda eng, fn, **kw: L.append(lambda: P.op(eng, fn, **kw))
            for ti, t in enumerate(gt):
                sc, sel = scs[ti], sels[ti]
                RT = dict(reads=["rt"] + ["sca%d" % h for h in range(8)] + ["scb%d" % h for h in range(8)], writes=["rt"])
                selg = sel[:, 2, :].rearrange("p (h k) -> p h k", h=8)
                R("dve", f_tt(ex3[:], sc[:], sc[:, :, 0:1].to_broadcast(B3), ALU.subtract), **RT)
                R("act", f_act(ex3[:], ex3[:], AF.Exp), **RT)
                R("dve", f_red(Z[:], ex3[:]), **RT)
                R("dve", f_rcp(rZ[:], Z[:]), **RT)
                R("dve", f_tt(selg, ex3[:], rZ[:].unsqueeze(2).to_broadcast(B3), ALU.mult), **RT)
                for m in range(3):
                    R("pe", f_tr(psSel[:, m, :], sel[:, m, :], ident_f[:]), reads=["rt", "consts"], writes=["rt0"], sig=(m == 2))
                R("act", f_acp(selT[:, :, ti * 128:(ti + 1) * 128], psSel), reads=["rt0"], writes=["selT"])
            return L

        prebuilt = set()

        def gb_dve(gi, u):
            t0 = u * 32
            Jm, Rm = JR[u % 2]
            jn, rn = JRn[2 * (u % 2)], JRn[2 * (u % 2) + 1]
            io3 = iota[:].unsqueeze(1).to_broadcast([128, 32, 128])
            P.op("dve", f_tt(Jm, io3, selT[:, 1, t0:t0 + 32].unsqueeze(2).to_broadcast([128, 32, 128]), ALU.is_equal),
                 reads=["selT", "consts", "rt"], writes=[jn, "rt"] if u == 0 else [jn], nowaw=True)
            for tl in range(32):
                tk = t0 + tl
                P.op("dve", lambda e, tl=tl, tk=tk: e.tensor_scalar(out=Rm[:, tl, :], in0=iota_bf[:], scalar1=selT[:, 0, tk:tk + 1],
                                                                   scalar2=selT[:, 2, tk:tk + 1], op0=ALU.is_equal, op1=ALU.mult),
                     reads=["selT"], writes=[rn], nowaw=True)

        def gbuild(gi, extra=()):
            extra = list(extra)
            gt = groups[gi]
            T = len(gt) * 128
            nsub = T // 32
            per_sub = (len(extra) + nsub - 1) // nsub if extra else 0
            io3 = iota[:].unsqueeze(1).to_broadcast([128, 32, 128])
            nev = 0
            for u in range(T // 32):
                t0 = u * 32
                Jm, Rm = JR[u % 2]
                jn, rn = JRn[2 * (u % 2)], JRn[2 * (u % 2) + 1]
                if not (gi in prebuilt and u < 2):
                    gb_dve(gi, u)
                for t4 in range(8):
                    pb, pr = rt[nev % 2], "rt%d" % (nev % 2)
                    for t in range(4):
                        tok = t4 * 4 + t
                        P.op("pe", f_mm(pb[:, t * 128:(t + 1) * 128], Jm[:, tok, :], Rm[:, tok, :], True, True),
                             reads=[jn, rn], writes=[pr], sig=(t == 3))
                    dstv = GT[:, t0 + t4 * 4:t0 + t4 * 4 + 4, :].rearrange("p t i -> p (t i)")
                    P.op("act", f_acp(dstv, pb[:]), reads=[pr], writes=["GT"], nowaw=True)
                    nev += 1
                for _ in range(per_sub):
                    if extra:
                        extra.pop(0)()
            while extra:
                extra.pop(0)()

        def main_loop(gi, thunks, late=()):
            late = list(late)
            gt = groups[gi]
            nt = len(gt)
            T = nt * 128
            hT, hres = hnT[gi % 2], "hnT%d" % (gi % 2)
            per = (len(thunks) + 71) // 72 if thunks else 0
            pos = [0]

            def load(i):
                b = i % NB
                P.dma(ub[b][:], UB[layer, i], reads=["UVB"], writes=["ub%d" % b])
                P.dma(vb[b][:], VB[layer, i * 128:(i + 1) * 128, :], reads=["UVB"], writes=["vb%d" % b])

            def stA(i):
                b = i % NB
                for kc in range(8):
                    P.op("pe", f_mm(psA[i % 2][:, :T], ub[b][:, kc * 128:(kc + 1) * 128], hT[:, kc, :T], kc == 0, kc == 7),
                         reads=["ub%d" % b, hres], writes=["psA%d" % (i % 2)], sig=(kc == 7))

            def stB(i):
                P.op("act", f_act(gl[i % 2][:, :T], psA[i % 2][:, :T], AF.Gelu), reads=["psA%d" % (i % 2)], writes=["gl%d" % (i % 2)])
                P.op("pool", f_tt(wt[i % 3][:, :T], gl[i % 2][:, :T], GT[:, 0:T, i], ALU.mult),
                     reads=["gl%d" % (i % 2), "GT"], writes=["wt%d" % (i % 3)])

            def stV(i):
                b = i % NB
                for tt in range(nt):
                    for hf in range(2):
                        a = acc[tt * 2 + hf]
                        P.op("pe", f_mm(a[:], wt[i % 3][:, tt * 128:(tt + 1) * 128], vb[b][:, hf * 512:(hf + 1) * 512], i == 0, i == 127),
                             reads=["wt%d" % (i % 3), "vb%d" % b], writes=["acc%d" % (tt * 2 + hf)], sig=(tt == nt - 1 and hf == 1))

            for i in range(3):
                load(i)
            for i in range(130):
                if i < 128:
                    stA(i)
                if 0 <= i - 1 < 128:
                    stB(i - 1)
                if i - 2 >= 0:
                    stV(i - 2)
                if i + 3 < 128:
                    load(i + 3)
                if i == 118:
                    for ti, t in enumerate(gt):
                        P.dma(xt[ti % 2][:], H[t * 128:(t + 1) * 128, :], reads=["H"], writes=["xp%d" % (ti % 2)])
                for _ in range(per):
                    if pos[0] < len(thunks):
                        thunks[pos[0]]()
                        pos[0] += 1
                if i >= 110:
                    for _ in range(2):
                        if late and pos[0] >= len(thunks):
                            late.pop(0)()
            while pos[0] < len(thunks):
                thunks[pos[0]]()
                pos[0] += 1
            while late:
                late.pop(0)()

        def epilogue(gi):
            gt = groups[gi]
            for ti, t in enumerate(gt):
                xb, xr = xt[ti % 2], "xp%d" % (ti % 2)
                for hf in range(2):
                    P.op("act", f_acp(SX[ti][:, hf * 512:(hf + 1) * 512], acc[ti * 2 + hf][:]),
                         reads=["acc%d" % (ti * 2 + hf)], writes=["SX%d" % ti], nowaw=(hf == 1))
                P.op("pool", f_tt(xb[:], xb[:], SX[ti][:, 0:1024], ALU.add), reads=["SX%d" % ti, xr], writes=[xr])
                P.dma(dst[t * 128:(t + 1) * 128, :], xb[:], reads=[xr], writes=["H" if dst is H else "outd"], nowaw=True)

        for th in burst(0):
            th()
        for th in routing_thunks(0):
            th()
        for th in rtail(0):
            th()
        for gi in range(len(groups)):
            more = gi + 1 < len(groups)
            gbuild(gi, burst(gi + 1) if more else ())
            late = []
            if more:
                late = rtail(gi + 1)
                late.append(lambda g1=gi + 1: (gb_dve(g1, 0), gb_dve(g1, 1), prebuilt.add(g1)))
            main_loop(gi, routing_thunks(gi + 1) if more else [], late)
            epilogue(gi)
        P.barrier()


def emit_pool(nc, P, C):
    H, poolw = C["H"], C["poolw"]
    gain_bc, ident_f = C["gain_bc"], C["ident_f"]
    W = 16 + RUNT
    with ExitStack() as es:
        sb = lambda name, shape, dt: es.enter_context(nc.sbuf_tensor(name, list(shape), dt))
        ps = lambda name, shape, dt: es.enter_context(nc.psum_tensor(name, list(shape), dt))
        wp = sb("wp", [128, 4, 2, 256], BF16)
        P.dma(wp[:], poolw.rearrange("g (k p) d -> p g k d", p=128), writes=["wp"], eng="pool")
        xt = [sb("xq%d" % i, [128, D], F32) for i in range(8)]
        xh = sb("xh", [128, D], F32)
        hn = sb("hnq", [128, D], F32)
        ss = sb("ssq", [128, 1], F32)
        rstd = sb("rstdq", [128, 1], F32)
        haloT = sb("haloT", [128, 8, 128], F32)
        XT = sb("XT", [128, 8, W], F32)
        SAB = [sb("SAB%d" % i, [128, 2, W], F32) for i in range(2)]
        PT = sb("PT", [128, 8, RUNT], BF16)
        psTf = ps("psTf", [128, 8, 128], F32)
        psY = [ps("psY%d" % i, [128, D], F32) for i in range(2)]
        gain = gain_bc[:, 1, :]
        scale = gain_bc[:, 4, :]

        def norm_T(xb, xr, dstv, dres):
            P.op("act", lambda e: e.activation(out=hn[:], in_=xb[:], func=AF.Square, accum_out=ss[:]), reads=[xr], writes=["hn", "ss"])
            P.op("act", lambda e: e.activation(out=ss[:], in_=ss[:], func=AF.Sqrt, bias=EPS, scale=1.0 / D), reads=["ss"], writes=["ss"])
            P.op("dve", lambda e: e.reciprocal(out=rstd[:], in_=ss[:]), reads=["ss"], writes=["rstd"])
            P.op("dve", lambda e: e.scalar_tensor_tensor(out=hn[:], in0=xb[:], scalar=rstd[:], in1=gain, op0=ALU.mult, op1=ALU.mult),
                 reads=[xr, "rstd", "consts"], writes=["hn"])
            for kc in range(8):
                P.op("pe", lambda e, kc=kc: e.transpose(psTf[:, kc, :], hn[:, kc * 128:(kc + 1) * 128], ident_f[:]),
                     reads=["hn", "consts"], writes=["psTf"], sig=(kc == 7))
            P.op("act", lambda e: e.copy(out=dstv, in_=psTf[:]), reads=["psTf"], writes=[dres])

        P.dma(xh[:], H[NOWN:NOWN + 128, :], reads=["H"], writes=["xh"])
        norm_T(xh, "xh", haloT[:], "haloT")
        def load_run(r):
            for ti in range(4):
                t = 4 * r + ti
                bi = (r % 2) * 4 + ti
                P.dma(xt[bi][:], H[t * 128:(t + 1) * 128, :], reads=["H"], writes=["xq%d" % bi])

        load_run(0)
        for r in range(NRUN):
            if r + 1 < NRUN:
                load_run(r + 1)
            P.op("dve", lambda e: e.tensor_copy(out=XT[:, :, 0:16], in_=haloT[:, :, 16 * r:16 * r + 16]), reads=["haloT"], writes=["XT"])
            for ti in range(4):
                t = 4 * r + ti
                bi = (r % 2) * 4 + ti
                norm_T(xt[bi], "xq%d" % bi, XT[:, :, 16 + ti * 128:16 + (ti + 1) * 128], "XT")
            for g in range(4):
                w = 2 << g
                cur, cres = XT[:, 2 * g:2 * g + 2, :], "XT"
                s_, k = 1, 0
                while s_ < w:
                    nxt, nres = SAB[k % 2], "SAB%d" % (k % 2)
                    P.op("dve", lambda e: e.tensor_tensor(out=nxt[:, :, s_:W], in0=cur[:, :, s_:W], in1=cur[:, :, 0:W - s_], op=ALU.add),
                         reads=[cres], writes=[nres])
                    cur, cres = nxt[:], nres
                    s_ *= 2
                    k += 1
                P.op("dve", lambda e: e.scalar_tensor_tensor(out=PT[:, 2 * g:2 * g + 2, :], in0=cur[:, :, 16:W], scalar=1.0 / w,
                                                             in1=XT[:, 2 * g:2 * g + 2, 16:W], op0=ALU.mult, op1=ALU.subtract),
                     reads=[cres, "XT"], writes=["PT"])
            for ti in range(4):
                t = 4 * r + ti
                py, pr = psY[ti % 2], "psY%d" % (ti % 2)
                for g in range(4):
                    for k2 in range(2):
                        P.op("pe", lambda e, g=g, k2=k2: e.matmul(py[:, g * 256:(g + 1) * 256], lhsT=PT[:, 2 * g + k2, ti * 128:(ti + 1) * 128],
                                                                  rhs=wp[:, g, k2, :], start=(k2 == 0), stop=(k2 == 1)),
                             reads=["PT", "wp"], writes=[pr], sig=(g == 3 and k2 == 1))
                P.op("dve", lambda e: e.tensor_tensor(out=hn[:], in0=py[:], in1=scale, op=ALU.mult), reads=[pr, "consts"], writes=["hn"])
                bi = (r % 2) * 4 + ti
                P.op("dve", lambda e: e.tensor_tensor(out=xt[bi][:], in0=hn[:], in1=xt[bi][:], op=ALU.add),
                     reads=["hn", "xq%d" % bi], writes=["xq%d" % bi])
                P.dma(H[t * 128:(t + 1) * 128, :], xt[bi][:], reads=["xq%d" % bi], writes=["H"], nowaw=True)
        P.barrier()


def _consts():
    bf = ml_dtypes.bfloat16
    kk = np.arange(128)
    c = {}
    c["ident_bf"] = np.eye(128, dtype=np.float32).astype(bf)
    c["ident_f"] = np.eye(128, dtype=np.float32)
    c["ntri"] = (-(kk[:, None] >= kk[None, :]).astype(np.float32)).astype(bf)
    c["nones"] = (-np.ones((128, 128), np.float32)).astype(bf)
    c["iota"] = np.tile(np.arange(128, dtype=np.float32)[None, :], (128, 1))
    c["iota_bf"] = c["iota"].astype(bf)
    return c


def _masks(p):
    bf = ml_dtypes.bfloat16
    kk = np.arange(128)[:, None]
    q = np.arange(512)[None, :]
    maskd = np.zeros((128, 9, 512), np.float32)
    maskd[:112, 0, :] = NEG
    for jb in range(8):
        allowed = (128 * jb + kk) < (512 * p + q)
        maskd[:, 1 + jb, :] = np.where(allowed, 0.0, NEG)
    hmask = np.zeros((128, NHB, 128), np.float32)
    for g in range(8):
        qb = 8 * g + 4 * p
        for i in range(16):
            col = g * 16 + i
            for j in range(NHB):
                if j < qb:
                    hmask[:, j, col] = 0.0
                elif j == qb:
                    hmask[:, j, col] = np.where(np.arange(128) < 112 + i, 0.0, NEG)
                else:
                    hmask[:, j, col] = NEG
            hmask[:112, 0, col] = NEG
    return maskd.astype(bf), hmask.astype(bf)


def prepare_in_maps(inputs, cores=range(8)):
    x = np.asarray(inputs["x"], np.float32)
    meta = np.asarray(inputs["meta"], np.float32)
    consts = _consts()
    gains = np.zeros((6, D), np.float32)
    gains[0:2] = np.asarray(inputs["norm_mix"], np.float32)
    gains[2:4] = np.asarray(inputs["norm_ffn"], np.float32)
    gains[4] = np.asarray(inputs["pool_scale"], np.float32)[0]
    gains[5, 0:64] = np.asarray(inputs["sb_q_gain"], np.float32)[0]
    gains[5, 64:128] = np.asarray(inputs["sb_k_gain"], np.float32)[0]
    u = np.asarray(inputs["peer_u"], np.float32)
    uS = np.ascontiguousarray(u.reshape(2, 128, 128, 8, 128).transpose(0, 1, 4, 3, 2)).reshape(2, 2048, 8192)
    shared = dict(
        wqkv=np.ascontiguousarray(np.asarray(inputs["sb_w_qkv"], np.float32)[0]),
        wo=np.ascontiguousarray(np.asarray(inputs["sb_w_o"], np.float32)[0]),
        gains=gains,
        poolw=np.ascontiguousarray(np.asarray(inputs["pool_w"], np.float32)[0]),
        wq_p=np.ascontiguousarray(np.asarray(inputs["peer_w_q"], np.float32)),
        skT=np.ascontiguousarray(np.asarray(inputs["peer_subkeys"], np.float32).transpose(0, 1, 3, 2)),
        uS=uS,
        vS=np.ascontiguousarray(np.asarray(inputs["peer_v"], np.float32)),
        **consts,
    )
    masks = [_masks(0), _masks(1)]
    in_maps = []
    for c in cores:
        b, p = c // 2, c % 2
        xall = np.zeros((LP, D), np.float32)
        xall[112:128] = meta
        xall[128:] = x[b]
        xq = np.zeros((NTOK, D), np.float32)
        for r in range(NRUN):
            k = 2 * r + p
            xq[r * RUNT:(r + 1) * RUNT] = x[b, k * RUNT:(k + 1) * RUNT]
            xq[NOWN + 16 * r: NOWN + 16 * (r + 1)] = meta if k == 0 else x[b, k * RUNT - 16:k * RUNT]
        m = dict(shared)
        m.update(xq=xq, xall=xall, maskd=masks[p][0], hmask=masks[p][1])
        in_maps.append(m)
    return in_maps


def kernel(**inputs):
    nc = build_program()
    in_maps = prepare_in_maps(inputs)
    res = run_bass_kernel_spmd(nc, in_maps, core_ids=list(range(8)))
    out = np.zeros((4, 8192, D), np.float32)
    for c in range(8):
        b, p = c // 2, c % 2
        o = np.asarray(res.results[c]["out"], np.float32)
        for r in range(NRUN):
            k = 2 * r + p
            out[b, k * RUNT:(k + 1) * RUNT] = o[r * RUNT:(r + 1) * RUNT]
    return out
```

```python
import numpy as np
import ml_dtypes
from contextlib import ExitStack
import concourse.bass as bass
import concourse.mybir as mybir
from concourse.bass_utils import run_bass_kernel_spmd

F32 = mybir.dt.float32
BF16 = mybir.dt.bfloat16
U32 = mybir.dt.uint32
AF = mybir.ActivationFunctionType
ALU = mybir.AluOpType
AX = mybir.AxisListType

D = 1024
NRUN = 8
RUNT = 512
NOWN = NRUN * RUNT
NTOK = NOWN + 128
NTILE = NTOK // 128
LP = 8320
NKB = LP // 128
NHB = 61
NEG = -30000.0
EPS = 1e-6


class Prog:
    ENG = ("pe", "act", "dve", "pool", "sp")

    def __init__(self, nc, es):
        self.nc = nc
        self.es = es
        self.eng = {"pe": nc.tensor, "act": nc.scalar, "dve": nc.vector, "pool": nc.gpsimd, "sp": nc.sync}
        self.esem = {e: es.enter_context(nc.semaphore("prog_" + e)) for e in ("pe", "act", "dve", "pool")}
        self.ecnt = {e: 0 for e in self.esem}
        self.dsem = {}
        self.lastw = {}
        self.readers = {}
        self.waited = {e: {} for e in self.ENG}
        self.nins = 0

    def _wait(self, eng, k, val):
        kind, key = k
        if kind == "d":
            val = max(val, self.dsem[key][1])
        d = self.waited[eng]
        if d.get(k, 0) >= val:
            return
        d[k] = val
        sem = self.esem[key] if kind == "e" else self.dsem[key][0]
        self.eng[eng].wait_ge(sem, val)
        self.nins += 1

    def op(self, eng, fn, reads=(), writes=(), sig=True, dma=False, nowaw=False):
        need = {}

        def add(dct):
            for k, v in dct.items():
                if need.get(k, 0) < v:
                    need[k] = v
        for r in reads:
            add(self.lastw.get(r, {}))
        for w in writes:
            if not nowaw:
                add(self.lastw.get(w, {}))
            add(self.readers.get(w, {}))
        for k, v in need.items():
            if k[0] == "e" and k[1] == eng and eng == "pe" and not dma:
                continue
            self._wait(eng, k, v)
        ins = fn(self.eng[eng])
        self.nins += 1
        if dma:
            key = writes[0] + "|" + (reads[0] if reads else "")
            if key not in self.dsem:
                self.dsem[key] = [self.es.enter_context(self.nc.semaphore("d%d" % len(self.dsem))), 0]
            self.dsem[key][1] += 16
            ins.then_inc(self.dsem[key][0], 16)
            tok = (("d", key), self.dsem[key][1])
        elif sig:
            self.ecnt[eng] += 1
            ins.then_inc(self.esem[eng], 1)
            tok = (("e", eng), self.ecnt[eng])
        else:
            tok = (("e", eng), self.ecnt[eng] + 1)
        for r in reads:
            rd = self.readers.setdefault(r, {})
            rd[tok[0]] = max(rd.get(tok[0], 0), tok[1])
        for w in writes:
            if nowaw:
                lw = self.lastw.setdefault(w, {})
                lw[tok[0]] = max(lw.get(tok[0], 0), tok[1])
            else:
                self.lastw[w] = {tok[0]: tok[1]}
            self.readers[w] = {}
        return tok

    def dma(self, out, in_, reads=(), writes=(), eng="sp", nowaw=False):
        return self.op(eng, lambda e: e.dma_start(out=out, in_=in_), reads=reads, writes=writes, dma=True, nowaw=nowaw)

    def barrier(self):
        toks = [(("e", e), c) for e, c in self.ecnt.items() if c > 0]
        toks += [(("d", k), v[1]) for k, v in self.dsem.items()]
        for eng in self.ENG:
            for k, v in toks:
                self._wait(eng, k, v)
        self.lastw = {}
        self.readers = {}


def bview(ap, shape):
    return ap.to_broadcast(shape)


def emit_norm_T(P, tag, x_sb, x_res, gain_bc, ident_bf, hn_sb, ss_sb, rstd_sb, junk_sb, psT, dst, dst_res):
    P.op("act", lambda e: e.activation(out=junk_sb, in_=x_sb, func=AF.Square, accum_out=ss_sb),
         reads=[x_res], writes=["junk", "ss"])
    P.op("act", lambda e: e.activation(out=ss_sb, in_=ss_sb, func=AF.Sqrt, bias=EPS, scale=1.0 / D),
         reads=["ss"], writes=["ss"])
    P.op("dve", lambda e: e.reciprocal(out=rstd_sb, in_=ss_sb), reads=["ss"], writes=["rstd"])
    P.op("dve", lambda e: e.scalar_tensor_tensor(out=hn_sb, in0=x_sb, scalar=rstd_sb, in1=gain_bc, op0=ALU.mult, op1=ALU.mult),
         reads=[x_res, "rstd", "consts"], writes=["hn"])
    for kc in range(8):
        P.op("pe", lambda e, kc=kc: e.transpose(psT[:, kc, :], hn_sb[:, kc * 128:(kc + 1) * 128], ident_bf),
             reads=["hn", "consts"], writes=["psT"], sig=(kc == 7))
    P.op("act", lambda e: e.copy(out=dst, in_=psT), reads=["psT"], writes=[dst_res])


def build_program(dbg=None):
    nc = bass.Bass("TRN2", target_bir_lowering=False)
    dram_in = {}

    def din(name, shape, dt=F32):
        dram_in[name] = nc.dram_tensor(name, list(shape), dt, kind="ExternalInput").ap()
        return dram_in[name]

    xq = din("xq", [NTOK, D])
    xall = din("xall", [LP, D])
    wqkv = din("wqkv", [D, 3 * D])
    wo = din("wo", [D, D])
    gains = din("gains", [6, D])
    poolw = din("poolw", [4, 256, 256])
    wq_p = din("wq_p", [2, D, 2048])
    skT = din("skT", [2, 2, 128, 128])
    uS = din("uS", [2, 2048, 8192])
    vS = din("vS", [2, 16384, D])
    ident_bf_d = din("ident_bf", [128, 128], BF16)
    ident_f_d = din("ident_f", [128, 128])
    ntri_d = din("ntri", [128, 128], BF16)
    nones_d = din("nones", [128, 128], BF16)
    iota_d = din("iota", [128, 128])
    iotab_d = din("iota_bf", [128, 128], BF16)
    maskd_d = din("maskd", [128, 9, 512], BF16)
    hmask_d = din("hmask", [128, NHB, 128], BF16)
    out = nc.dram_tensor("out", [NOWN, D], F32, kind="ExternalOutput").ap()
    dbg_out = None
    if dbg:
        dbg_out = nc.dram_tensor("dbg", [NTOK, D], F32, kind="ExternalOutput").ap()

    KT = nc.dram_tensor("KT", [16, 64, LP], BF16).ap()
    VS = nc.dram_tensor("VSc", [16, 128, NKB, 64], BF16).ap()
    QT = nc.dram_tensor("QT", [16, 64, NTOK], BF16).ap()
    OT = nc.dram_tensor("OT", [16, 64, NTOK], BF16).ap()
    H = nc.dram_tensor("H", [NTOK, D], F32).ap()
    UB = nc.dram_tensor("UB", [2, 128, 128, 1024], BF16).ap()
    VB = nc.dram_tensor("VB", [2, 16384, D], BF16).ap()
    WQB = nc.dram_tensor("WQB", [2, 128, 8, 2048], BF16).ap()

    with ExitStack() as es:
        P = Prog(nc, es)
        sb = lambda name, shape, dt: es.enter_context(nc.sbuf_tensor(name, list(shape), dt))

        ident_bf = sb("ident_bf_s", [128, 128], BF16)
        ident_f = sb("ident_f_s", [128, 128], F32)
        ntri = sb("ntri_s", [128, 128], BF16)
        nones = sb("nones_s", [128, 128], BF16)
        iota = sb("iota_s", [128, 128], F32)
        iota_bf = sb("iota_bf_s", [128, 128], BF16)
        gain_bc = sb("gain_bc", [128, 6, D], F32)
        for dst, src in ((ident_bf, ident_bf_d), (ident_f, ident_f_d), (ntri, ntri_d), (nones, nones_d), (iota, iota_d), (iota_bf, iotab_d)):
            P.dma(dst[:], src, writes=["consts"], nowaw=True)
        for i in range(6):
            P.dma(gain_bc[:, i, :], gains[i:i + 1, :].partition_broadcast(128), writes=["consts"], nowaw=True)

        es2 = ExitStack()
        NCB = 2
        cb = [es2.enter_context(nc.sbuf_tensor("castbuf%d" % i, [128, 8192], BF16)) for i in range(NCB)]
        pre = []
        n = 0
        for l in range(2):
            usrc = uS[l]
            udst = UB[l].rearrange("i (q a) f -> (i q) (a f)", a=8)
            vsrc = vS[l].rearrange("(q a) d -> q (a d)", a=8)
            vdst = VB[l].rearrange("(q a) d -> q (a d)", a=8)
            for src, dst in ((usrc, udst), (vsrc, vdst)):
                for blk in range(16):
                    def th(src=src, dst=dst, blk=blk, n=n):
                        b, r = cb[n % NCB], "castbuf%d" % (n % NCB)
                        P.dma(b[:], src[blk * 128:(blk + 1) * 128, :], writes=[r], eng="pool")
                        P.dma(dst[blk * 128:(blk + 1) * 128, :], b[:], reads=[r], writes=["UVB"], nowaw=True)
                    pre.append(th)
                    n += 1
        for l in range(2):
            for hf in range(2):
                def th(l=l, hf=hf, n=n):
                    b, r = cb[n % NCB], "castbuf%d" % (n % NCB)
                    bv = b[:].rearrange("p (k n) -> p k n", k=8)
                    P.dma(bv, wq_p[l].rearrange("(k p) n -> p k n", p=128)[:, :, hf * 1024:(hf + 1) * 1024], writes=[r], eng="pool")
                    P.dma(WQB[l, :, :, hf * 1024:(hf + 1) * 1024], bv, reads=[r], writes=["UVB"], nowaw=True)
                pre.insert(0, th)
                n += 1
        pre_state = (pre, es2)

        C = dict(locals())
        emit_attention(nc, P, C)
        if dbg == "attn":
            emit_copy_rows(nc, P, H, dbg_out, NTILE)
            return nc
        emit_peer(nc, P, C, layer=0, ntiles=NTILE, dst=H)
        if dbg == "peer0":
            emit_copy_rows(nc, P, H, dbg_out, NTILE)
            return nc
        emit_pool(nc, P, C)
        if dbg == "pool":
            emit_copy_rows(nc, P, H, dbg_out, NTILE)
            return nc
        emit_peer(nc, P, C, layer=1, ntiles=NOWN // 128, dst=out)
        P.barrier()
        print("[kernel] instructions=%d dma_sems=%d" % (P.nins, len(P.dsem)))
    return nc


def emit_copy_rows(nc, P, src, dst, ntile):
    P.barrier()
    P.dma(dst, src, writes=["dbgout"])
    P.barrier()


def emit_attention(nc, P, C):
    xq, xall, wqkv, wo = C["xq"], C["xall"], C["wqkv"], C["wo"]
    KT, VS, QT, OT, H = C["KT"], C["VS"], C["QT"], C["OT"], C["H"]
    gain_bc, ident_bf, ntri, nones = C["gain_bc"], C["ident_bf"], C["ntri"], C["nones"]
    maskd_d, hmask_d = C["maskd_d"], C["hmask_d"]

    pre, es_pre = C["pre_state"]
    with ExitStack() as es:
        sb = lambda name, shape, dt: es.enter_context(nc.sbuf_tensor(name, list(shape), dt))
        ps = lambda name, shape, dt: es.enter_context(nc.psum_tensor(name, list(shape), dt))
        w_sb = sb("wqkv_s", [128, 8, 3072], BF16)
        P.dma(w_sb[:], wqkv.rearrange("(k p) n -> p k n", p=128), writes=["wqkv"], eng="pool")
        gq = sb("gq", [128, 64], F32)
        P.op("dve", lambda e: e.scalar_tensor_tensor(out=gq[:], in0=gain_bc[:, 5, 0:64], scalar=0.125,
                                                     in1=gain_bc[:, 5, 64:128], op0=ALU.mult, op1=ALU.mult),
             reads=["consts"], writes=["gq"])
        two = lambda name, shape, dt: [sb("%s%d" % (name, i), shape, dt) for i in range(2)]
        x_t = two("xt", [128, D], F32)
        hn = two("hn", [128, D], BF16)
        junk = two("junk", [128, D], BF16)
        ss = two("ss", [128, 1], F32)
        rstd = two("rstd", [128, 1], F32)
        hnT = two("hnT", [128, 8, 128], BF16)
        sq = two("sq", [128, D], F32)
        ssh = two("ssh", [128, 16], F32)
        rsh = two("rsh", [128, 16], F32)
        kn = two("kn", [128, D], BF16)
        knT = two("knT", [128, 8, 128], BF16)
        v_sb = two("v_sb", [128, D], BF16)
        psT_ = ps("psT", [128, 8, 128], BF16)
        psT2_ = ps("psT2", [128, 8, 128], BF16)
        psT = [psT_, psT_]
        psT2 = [psT2_, psT2_]
        psKs = [ps("psK%d" % i, [128, D], F32) for i in range(2)]
        psV = ps("psV", [128, D], F32)

        def proj(dst_ps, res, hT, hres, col0):
            for half in range(2):
                for kc in range(8):
                    P.op("pe", f_mm(dst_ps[:, half * 512:(half + 1) * 512], hT[:, kc, :],
                                    w_sb[:, kc, col0 + half * 512: col0 + (half + 1) * 512], kc == 0, kc == 7),
                         reads=[hres, "wqkv"], writes=[res], sig=(kc == 7))

        def headnorm_1(is_q, b):
            sfx = str(b)
            psK, pkr = psKs[b], "psK%d" % b
            hv = lambda ap: ap.rearrange("p (h d) -> p h d", d=64)
            P.op("act", f_act(sq[b][:], psK[:], AF.Square), reads=[pkr], writes=["sq" + sfx])
            P.op("dve", f_red(ssh[b][:], hv(sq[b][:])), reads=["sq" + sfx], writes=["ssh" + sfx])
            P.op("act", f_act(ssh[b][:], ssh[b][:], AF.Sqrt, bias=EPS, scale=1.0 / 64), reads=["ssh" + sfx], writes=["ssh" + sfx])
            P.op("dve", f_rcp(rsh[b][:], ssh[b][:]), reads=["ssh" + sfx], writes=["rsh" + sfx])
            rb = rsh[b][:].unsqueeze(2).to_broadcast([128, 16, 64])
            if is_q:
                P.op("dve", f_tt(hv(sq[b][:]), hv(psK[:]), rb, ALU.mult), reads=[pkr, "rsh" + sfx, "sq" + sfx], writes=["sq" + sfx])
                P.op("dve", f_tt(hv(kn[b][:]), hv(sq[b][:]), gq[:].unsqueeze(1).to_broadcast([128, 16, 64]), ALU.mult),
                     reads=["sq" + sfx, "gq"], writes=["kn" + sfx])
            else:
                P.op("dve", f_tt(hv(kn[b][:]), hv(psK[:]), rb, ALU.mult), reads=[pkr, "rsh" + sfx], writes=["kn" + sfx])

        def headnorm_2(n_):
            rows, is_q, t = jobs[n_]
            b = n_ % 2
            sfx = str(b)
            for pr in range(8):
                P.op("pe", f_tr(psT2[b][:, pr, :], kn[b][:, pr * 128:(pr + 1) * 128], ident_bf[:]),
                     reads=["kn" + sfx, "consts"], writes=["psT2"], sig=(pr == 7))
            P.op("dve", f_cp(knT[b][:], psT2[b][:]), reads=["psT2"], writes=["knT" + sfx])
            dstT = QT if is_q else KT
            for hh in range(2):
                P.dma(dstT[hh::2, :, t * 128:(t + 1) * 128].rearrange("h p t -> p h t"), knT[b][hh * 64:(hh + 1) * 64, :, :],
                      reads=["knT%d" % b], writes=["QT" if is_q else "KT"], nowaw=True)

        def load_x(src_rows, t):
            b = t % 2
            P.dma(x_t[b][:], src_rows, writes=["xt%d" % b])

        def norm_a(t):
            b = t % 2
            sfx = str(b)
            xb, xr = x_t[b], "xt" + sfx
            P.op("act", f_act(junk[b][:], xb[:], AF.Square, accum_out=ss[b][:]), reads=[xr], writes=["junk" + sfx, "ss" + sfx])
            P.op("act", f_act(ss[b][:], ss[b][:], AF.Sqrt, bias=EPS, scale=1.0 / D), reads=["ss" + sfx], writes=["ss" + sfx])
            P.op("dve", f_rcp(rstd[b][:], ss[b][:]), reads=["ss" + sfx], writes=["rstd" + sfx])
            P.op("dve", f_stt(hn[b][:], xb[:], rstd[b][:], gain_bc[:, 0, :], ALU.mult, ALU.mult),
                 reads=[xr, "rstd" + sfx, "consts"], writes=["hn" + sfx])

        def norm_b(t):
            b = t % 2
            sfx = str(b)
            for kc in range(8):
                P.op("pe", f_tr(psT[b][:, kc, :], hn[b][:, kc * 128:(kc + 1) * 128], ident_bf[:]),
                     reads=["hn" + sfx, "consts"], writes=["psT"], sig=(kc == 7))
            P.op("act", f_acp(hnT[b][:], psT[b][:]), reads=["psT"], writes=["hnT" + sfx])
            return hnT[b], "hnT" + sfx

        npop = [0]

        def pop_pre():
            if pre and npop[0] < 36:
                pre.pop(0)()
                npop[0] += 1

        jobs = [(xall[t * 128:(t + 1) * 128, :], False, t) for t in range(NKB)] + [(xq[t * 128:(t + 1) * 128, :], True, t) for t in range(NTILE)]
        NJ = len(jobs)
        load_x(jobs[0][0], 0)
        load_x(jobs[1][0], 1)
        norm_a(0)
        nxt = norm_b(0)
        for n_, (rows, is_q, t) in enumerate(jobs):
            b = n_ % 2
            hT, hres = nxt
            if n_ + 2 < NJ:
                load_x(jobs[n_ + 2][0], n_ + 2)
            if n_ + 1 < NJ:
                norm_a(n_ + 1)
            if not is_q:
                proj(psKs[b], "psK%d" % b, hT, hres, 1024)
                proj(psV, "psV", hT, hres, 2048)
            else:
                proj(psKs[b], "psK%d" % b, hT, hres, 0)
            if n_ >= 1:
                headnorm_2(n_ - 1)
            if n_ + 1 < NJ:
                nxt = norm_b(n_ + 1)
            if not is_q:
                P.op("act", f_acp(v_sb[b][:], psV[:]), reads=["psV"], writes=["v_sb%d" % b])
                P.dma(VS[:, :, t, :].rearrange("h p d -> p h d"), v_sb[b][:].rearrange("p (h d) -> p h d", d=64),
                      reads=["v_sb%d" % b], writes=["VS"], nowaw=True)
            headnorm_1(is_q, b)
            pop_pre()
        headnorm_2(NJ - 1)
        while pre and npop[0] < 36:
            pop_pre()
        P.barrier()

    with ExitStack() as es:
        sb = lambda name, shape, dt: es.enter_context(nc.sbuf_tensor(name, list(shape), dt))
        ps = lambda name, shape, dt: es.enter_context(nc.psum_tensor(name, list(shape), dt))
        maskd = sb("maskd_s", [128, 9, 512], BF16)
        hmask = sb("hmask_s", [128, NHB, 128], BF16)
        P.dma(maskd[:], maskd_d, writes=["masks"], nowaw=True)
        P.dma(hmask[:], hmask_d, writes=["masks"], nowaw=True)
        kt = [sb("kt%d" % i, [64, LP], BF16) for i in range(2)]
        vt = [sb("vt%d" % i, [128, NKB, 64], BF16) for i in range(2)]
        qt = [sb("qt%d" % i, [64, NTOK], BF16) for i in range(2)]
        ot = [sb("ot%d" % i, [64, NTOK], BF16) for i in range(2)]
        Eb = [sb("E%d" % i, [128, 512], BF16) for i in range(3)]
        SPb = [sb("SP%d" % i, [128, 512], BF16) for i in range(3)]
        Ab = [sb("A%d" % i, [128, 512], BF16) for i in range(3)]
        RSb = [sb("RS%d" % i, [128, 512], BF16) for i in range(2)]
        ps1 = [ps("ps1_%d" % i, [128, 512], F32) for i in range(4)]
        pso = [ps("pso%d" % i, [64, 512], F32) for i in range(2)]

        groups = [(r, r * RUNT, RUNT, 8 * r + 9) for r in range(NRUN)] + [(-1, NOWN, 128, NHB)]
        steps = []
        gidx = 0
        for h in range(16):
            for gi_, (r, off, N, nkb) in enumerate(groups):
                for s_, j in enumerate(range(nkb - 1, -1, -1)):
                    steps.append(dict(h=h, b=h % 2, r=r, off=off, N=N, nkb=nkb, s=s_, j=j, g=gidx,
                                      first=(gi_ == 0 and s_ == 0), last_of_head=(gi_ == len(groups) - 1 and s_ == nkb - 1)))
                gidx += 1

        def mask_for(d):
            r, j = d["r"], d["j"]
            if r < 0:
                return hmask[:, j, :]
            if j == 0:
                return maskd[:, 0, :]
            if j >= 8 * r + 1:
                return maskd[:, 1 + j - (8 * r + 1), :]
            return None

        def S1(G):
            d = steps[G]
            h, b, N, off, j = d["h"], d["b"], d["N"], d["off"], d["j"]
            if d["first"]:
                P.dma(kt[b][:], KT[h], reads=["KT"], writes=["kt%d" % b])
                P.dma(vt[b][:], VS[h], reads=["VS"], writes=["vt%d" % b])
                P.dma(qt[b][:], QT[h], reads=["QT"], writes=["qt%d" % b])
            if d["s"] == 0 and d["g"] % 4 == 0 and pre:
                pre.pop(0)()
            p1, pres = ps1[G % 4], "ps1_%d" % (G % 4)
            m = mask_for(d)
            P.op("pe", f_mm(p1[:, :N], kt[b][:, j * 128:(j + 1) * 128], qt[b][:, off:off + N], True, m is None),
                 reads=["kt%d" % b, "qt%d" % b], writes=[pres], sig=(m is None))
            if m is not None:
                P.op("pe", f_mm(p1[:, :N], ident_bf[:], m, False, True), reads=["masks", "consts"], writes=[pres])
            P.op("act", f_act(Eb[G % 3][:, :N], p1[:, :N], AF.Exp), reads=[pres], writes=["E%d" % (G % 3)])

        def S1b(G):
            N = steps[G]["N"]
            P.op("act", f_act(SPb[G % 3][:, :N], Eb[G % 3][:, :N], AF.Ln, bias=1.0), reads=["E%d" % (G % 3)], writes=["SP%d" % (G % 3)])

        def S2a(G):
            d = steps[G]
            N, s_, nkb = d["N"], d["s"], d["nkb"]
            p1, pres = ps1[G % 4], "ps1_%d" % (G % 4)
            P.op("pe", lambda e: e.matmul(p1[:, :N], lhsT=ntri[:], rhs=SPb[G % 3][:, :N], start=False, stop=(s_ == 0), skip_group_check=True),
                 reads=["SP%d" % (G % 3), "consts"], writes=[pres], sig=(s_ == 0))
            if s_ > 0:
                P.op("pe", lambda e: e.matmul(p1[:, :N], lhsT=nones[:], rhs=RSb[(G - 1) % 2][:, :N], start=False, stop=True, skip_group_check=True),
                     reads=["RS%d" % ((G - 1) % 2), "consts"], writes=[pres])
            if s_ < nkb - 1:
                if s_ == 0:
                    P.op("dve", f_cp(RSb[G % 2][:, :N], SPb[G % 3][:, :N]), reads=["SP%d" % (G % 3)], writes=["RS%d" % (G % 2)])
                else:
                    P.op("dve", f_tt(RSb[G % 2][:, :N], RSb[(G - 1) % 2][:, :N], SPb[G % 3][:, :N], ALU.add),
                         reads=["RS%d" % ((G - 1) % 2), "SP%d" % (G % 3)], writes=["RS%d" % (G % 2)])
            P.op("act", f_act(Ab[G % 3][:, :N], p1[:, :N], AF.Exp), reads=[pres], writes=["A%d" % (G % 3)])

        def S2b(G):
            d = steps[G]
            h, b, N, off, j, s_, nkb = d["h"], d["b"], d["N"], d["off"], d["j"], d["s"], d["nkb"]
            o_ps, ores = pso[d["g"] % 2], "pso%d" % (d["g"] % 2)
            P.op("pe", f_mm(o_ps[:, :N], vt[b][:, j, :], Ab[G % 3][:, :N], s_ == 0, s_ == nkb - 1),
                 reads=["A%d" % (G % 3), "vt%d" % b], writes=[ores], sig=(s_ == nkb - 1))
            if s_ == nkb - 1:
                P.op("dve", f_cp(ot[b][:, off:off + N], o_ps[:, :N]), reads=[ores], writes=["ot%d" % b])
            if d["last_of_head"]:
                P.dma(OT[h], ot[b][:], reads=["ot%d" % b], writes=["OT"], nowaw=True)

        NS = len(steps)
        for G in range(NS + 3):
            if G < NS:
                S1(G)
            if 0 <= G - 1 < NS:
                S1b(G - 1)
            if 0 <= G - 2 < NS:
                S2a(G - 2)
            if 0 <= G - 3 < NS:
                S2b(G - 3)
        while pre:
            pre.pop(0)()
        P.barrier()
    es_pre.close()

    with ExitStack() as es:
        sb = lambda name, shape, dt: es.enter_context(nc.sbuf_tensor(name, list(shape), dt))
        ps = lambda name, shape, dt: es.enter_context(nc.psum_tensor(name, list(shape), dt))
        wo_sb = sb("wo_s", [128, 8, D], BF16)
        P.dma(wo_sb[:], wo.rearrange("(p r) n -> r p n", r=128), writes=["wo"], eng="pool")
        OTp = OT.rearrange("(p two) d t -> p (two d) t", two=2)
        ott = [sb("ott%d" % i, [128, 8, 128], BF16) for i in range(2)]
        xt = [sb("xc%d" % i, [128, D], F32) for i in range(2)]
        res = [sb("res%d" % i, [128, D], F32) for i in range(2)]
        psC = [ps("psC%d" % i, [128, D], F32) for i in range(2)]

        def loadC(t):
            b = t % 2
            P.dma(ott[b][:], OTp[:, :, t * 128:(t + 1) * 128].rearrange("p r t -> r p t"), reads=["OT"], writes=["ott%d" % b])
            P.dma(xt[b][:], xq[t * 128:(t + 1) * 128, :], writes=["xc%d" % b])

        loadC(0)
        for t in range(NTILE):
            b = t % 2
            if t + 1 < NTILE:
                loadC(t + 1)
            for half in range(2):
                for pr in range(8):
                    P.op("pe", f_mm(psC[b][:, half * 512:(half + 1) * 512], ott[b][:, pr, :], wo_sb[:, pr, half * 512:(half + 1) * 512],
                                    pr == 0, pr == 7),
                         reads=["ott%d" % b, "wo"], writes=["psC%d" % b], sig=(pr == 7))
            P.op("dve", f_tt(res[b][:], psC[b][:], xt[b][:], ALU.add), reads=["psC%d" % b, "xc%d" % b], writes=["res%d" % b])
            P.dma(H[t * 128:(t + 1) * 128, :], res[b][:], reads=["res%d" % b], writes=["H"], nowaw=True)
        P.barrier()


def f_max(o, i): return lambda e: e.max(out=o, in_=i)
def f_mr(o, r, v): return lambda e: e.match_replace(out=o, in_to_replace=r, in_values=v, imm_value=-1e30)
def f_mi(o, m, v): return lambda e: e.max_index(out=o, in_max=m, in_values=v)
def f_tt(o, a, b, op): return lambda e: e.tensor_tensor(out=o, in0=a, in1=b, op=op)
def f_cp(o, i): return lambda e: e.tensor_copy(out=o, in_=i)
def f_acp(o, i): return lambda e: e.copy(out=o, in_=i)
def f_red(o, i): return lambda e: e.tensor_reduce(out=o, in_=i, axis=AX.X, op=ALU.add)
def f_mm(o, l, r, st, sp): return lambda e: e.matmul(o, lhsT=l, rhs=r, start=st, stop=sp)
def f_tr(o, i, ident): return lambda e: e.transpose(o, i, ident)
def f_act(o, i, func, **kw): return lambda e: e.activation(out=o, in_=i, func=func, **kw)
def f_stt(o, a, sc, b, op0, op1): return lambda e: e.scalar_tensor_tensor(out=o, in0=a, scalar=sc, in1=b, op0=op0, op1=op1)
def f_rcp(o, i): return lambda e: e.reciprocal(out=o, in_=i)


def emit_peer(nc, P, C, layer, ntiles, dst):
    H, WQB, skT, UB, VB = C["H"], C["WQB"], C["skT"], C["UB"], C["VB"]
    gain_bc, ident_bf, ident_f, iota, iota_bf = C["gain_bc"], C["ident_bf"], C["ident_f"], C["iota"], C["iota_bf"]
    NT = 2
    TM = NT * 128
    NB = 5
    with ExitStack() as es:
        sb = lambda name, shape, dt: es.enter_context(nc.sbuf_tensor("%s_L%d" % (name, layer), list(shape), dt))
        ps = lambda name, shape, dt: es.enter_context(nc.psum_tensor("%s_L%d" % (name, layer), list(shape), dt))
        big = sb("big", [128, 16384], BF16)
        bigf = big[:].bitcast(F32)
        S, S2, EQ, TMP = (bigf[:, i * 2048:(i + 1) * 2048] for i in range(4))
        JR = [(big[:, (2 * x) * 4096:(2 * x + 1) * 4096].rearrange("p (t j) -> p t j", j=128),
               big[:, (2 * x + 1) * 4096:(2 * x + 2) * 4096].rearrange("p (t j) -> p t j", j=128)) for x in range(2)]
        JRn = ["JmA", "RmA", "JmB", "RmB"]
        GT = sb("GT", [128, TM, 128], BF16)
        hnT = [sb("hnTp%d" % i, [128, 8, TM], BF16) for i in range(2)]
        qT = sb("qT", [128, 16, TM], BF16)
        wqs = [sb("wqs%d" % i, [128, 8, 128], BF16) for i in range(2)]
        skT_sb = sb("skT_s", [128, 2, 128], BF16)
        P.dma(skT_sb[:], skT[layer].rearrange("s d n -> d s n"), writes=["skT"], eng="pool")
        xt = [sb("xp%d" % i, [128, D], F32) for i in range(2)]
        hn = sb("hnp", [128, D], BF16)
        ss = sb("ssp", [128, 1], F32)
        rstd = sb("rstdp", [128, 1], F32)
        mx = sb("mx", [128, 16, 16], F32)
        ixu = sb("ixu", [128, 16, 16], U32)
        ixf = sb("ixf", [128, 16, 16], F32)
        posu = sb("posu", [128, 8, 16], U32)
        posf = sb("posf", [128, 8, 16], F32)
        aq = sb("aq", [128, 8, 16], F32)
        bq = sb("bq", [128, 8, 16], F32)
        ex3 = sb("ex3", [128, 8, 16], F32)
        Z = sb("Z", [128, 8], F32)
        rZ = sb("rZ", [128, 8], F32)
        selT = sb("selT", [128, 3, TM], F32)
        ub = [sb("ub%d" % i, [128, 1024], BF16) for i in range(NB)]
        vb = [sb("vb%d" % i, [128, 1024], BF16) for i in range(NB)]
        gl = [sb("gl%d" % i, [128, TM], BF16) for i in range(2)]
        wt = [sb("wt%d" % i, [128, TM], BF16) for i in range(3)]
        c16 = sb("c16", [128, 32], F32)
        P.op("dve", lambda e: e.tensor_single_scalar(out=c16[:], in_=iota[:, 0:32], scalar=16.0, op=ALU.mult), reads=["consts"], writes=["c16"])
        acc = [ps("acc%d" % i, [128, 512], F32) for i in range(4)]
        psA = [ps("psA%d" % i, [128, 512], F32) for i in range(2)]
        rt = [ps("rtb%d" % i, [128, 512], F32) for i in range(2)]
        psT = rt[0][:].bitcast(BF16)[:, 0:1024].rearrange("p (k t) -> p k t", k=8)
        psSel = rt[0][:, 0:384].rearrange("p (m t) -> p m t", m=3)
        gain = gain_bc[:, 2 + layer, :]

        tiles = list(range(ntiles))
        groups = [tiles[i:i + NT] for i in range(0, ntiles, NT)]
        B4 = [128, 8, 16, 16]
        B3 = [128, 8, 16]

        SX = [sb("SX%d" % i, [128, 2048], F32)[:] for i in range(2)]
        S3 = EQ
        bigb = big[:]
        EQb = bigb[:, 12288:14336].rearrange("p (h k a) -> p h k a", h=8, k=16)
        TMb = bigb[:, 14336:16384].rearrange("p (h k a) -> p h k a", h=8, k=16)
        scs = [sb("sc%d" % i, [128, 8, 16], F32) for i in range(2)]
        sels = [sb("sel%d" % i, [128, 3, 128], F32) for i in range(2)]

        bT = acc[0][:].bitcast(BF16)[:, 0:1024].rearrange("p (k t) -> p k t", k=8)

        def burst(gi):
            gt = groups[gi]
            T = len(gt) * 128
            hT, hres = hnT[gi % 2], "hnT%d" % (gi % 2)
            L = []
            R = lambda eng, fn, **kw: L.append(lambda: P.op(eng, fn, **kw))
            for ti, t in enumerate(gt):
                xb, xr = xt[ti % 2], "xp%d" % (ti % 2)
                L.append(lambda xb=xb, xr=xr, t=t: P.dma(xb[:], H[t * 128:(t + 1) * 128, :], reads=["H"], writes=[xr]))
                R("act", f_act(hn[:], xb[:], AF.Square, accum_out=ss[:]), reads=[xr], writes=["hn", "ss"])
                R("act", f_act(ss[:], ss[:], AF.Sqrt, bias=EPS, scale=1.0 / D), reads=["ss"], writes=["ss"])
                R("dve", f_rcp(rstd[:], ss[:]), reads=["ss"], writes=["rstd"])
                R("dve", f_stt(hn[:], xb[:], rstd[:], gain, ALU.mult, ALU.mult), reads=[xr, "rstd", "consts"], writes=["hn"])
                for kc in range(8):
                    R("pe", f_tr(bT[:, kc, :], hn[:, kc * 128:(kc + 1) * 128], ident_bf[:]), reads=["hn", "consts"], writes=["acc0"], sig=(kc == 7))
                R("act", f_acp(hT[:, :, ti * 128:(ti + 1) * 128], bT), reads=["acc0"], writes=[hres])
            for g in range(16):
                wb, wr = wqs[g % 2], "wqs%d" % (g % 2)
                L.append(lambda wb=wb, wr=wr, g=g: P.dma(wb[:], WQB[layer, :, :, g * 128:(g + 1) * 128], reads=["UVB"], writes=[wr]))
                pq, pr = acc[1 + g % 2], "acc%d" % (1 + g % 2)
                for kc in range(8):
                    R("pe", f_mm(pq[:, :T], wb[:, kc, :], hT[:, kc, :T], kc == 0, kc == 7), reads=[wr, hres], writes=[pr], sig=(kc == 7))
                R("act", f_acp(qT[:, g, :T], pq[:, :T]), reads=[pr], writes=["qT"])
            for ti, t in enumerate(gt):
                for q4 in range(4):
                    pq, pr = acc[1 + q4 % 2], "acc%d" % (1 + q4 % 2)
                    for gg in range(4):
                        g = q4 * 4 + gg
                        R("pe", f_mm(pq[:, gg * 128:(gg + 1) * 128], qT[:, g, ti * 128:(ti + 1) * 128], skT_sb[:, g % 2, :], True, True),
                          reads=["qT", "skT"], writes=[pr], sig=(gg == 3))
                    R("act", f_acp(SX[ti][:, q4 * 512:(q4 + 1) * 512], pq[:]), reads=[pr], writes=["SX%d" % ti])
            return L

        def routing_thunks(gi):
            gt = groups[gi]
            L = []
            R = lambda eng, fn, **kw: L.append(lambda: P.op(eng, fn, **kw))
            for ti, t in enumerate(gt):
                RT = dict(reads=["rt", "SX%d" % ti], writes=["rt", "SX%d" % ti] + (JRn if ti == 0 else []))
                RC = dict(reads=["rt", "consts", "c16", "SX%d" % ti], writes=["rt", "SX%d" % ti])
                Sx = SX[ti]
                sc, sel = scs[ti], sels[ti]
                sxn = "SX%d" % ti
                G16 = range(16)
                sl16 = [slice(g * 128, (g + 1) * 128) for g in G16]
                for g in G16:
                    R("dve", f_max(mx[:, g, 0:8], Sx[:, sl16[g]]), reads=[sxn, "rt"], writes=["mxa%d" % g])
                for g in G16:
                    R("dve", f_mr(S3[:, sl16[g]], mx[:, g, 0:8], Sx[:, sl16[g]]), reads=[sxn, "mxa%d" % g, "rt"], writes=["S3_%d" % g] + (JRn if ti == 0 else []))
                for g in G16:
                    R("dve", f_max(mx[:, g, 8:16], S3[:, sl16[g]]), reads=["S3_%d" % g], writes=["mxb%d" % g])
                for g in G16:
                    R("dve", f_mi(ixu[:, g, 0:8], mx[:, g, 0:8], Sx[:, sl16[g]]), reads=[sxn, "mxa%d" % g], writes=["ixa%d" % g])
                for g in G16:
                    R("dve", f_mi(ixu[:, g, 8:16], mx[:, g, 8:16], S3[:, sl16[g]]), reads=["S3_%d" % g, "mxb%d" % g], writes=["ixb%d" % g])
                allmx = ["mxa%d" % g for g in G16] + ["mxb%d" % g for g in G16]
                allix = ["ixa%d" % g for g in G16] + ["ixb%d" % g for g in G16]
                alls3 = ["S3_%d" % g for g in G16]
                R("dve", f_cp(ixf[:], ixu[:]), reads=allix + ["rt"], writes=["rt"])
                mxv = mx[:].rearrange("p (h two) k -> p h two k", two=2)
                ixv = ixf[:].rearrange("p (h two) k -> p h two k", two=2)
                cand4 = Sx.rearrange("p (h a b) -> p h a b", h=8, a=16)
                R("dve", f_tt(cand4, mxv[:, :, 0, :].unsqueeze(3).to_broadcast(B4), mxv[:, :, 1, :].unsqueeze(2).to_broadcast(B4), ALU.add),
                  reads=allmx + allix + [sxn, "rt"], writes=[sxn, "rt"])
                H8 = range(8)
                sl8 = [slice(h * 256, (h + 1) * 256) for h in H8]
                for h in H8:
                    R("dve", f_max(sc[:, h, 0:8], Sx[:, sl8[h]]), reads=[sxn, "rt"], writes=["sca%d" % h])
                for h in H8:
                    R("dve", f_mr(S3[:, sl8[h]], sc[:, h, 0:8], Sx[:, sl8[h]]), reads=[sxn, "sca%d" % h] + alls3[2 * h:2 * h + 2],
                      writes=alls3[2 * h:2 * h + 2])
                for h in H8:
                    R("dve", f_max(sc[:, h, 8:16], S3[:, sl8[h]]), reads=alls3[2 * h:2 * h + 2], writes=["scb%d" % h])
                for h in H8:
                    R("dve", f_mi(posu[:, h, 0:8], sc[:, h, 0:8], Sx[:, sl8[h]]), reads=[sxn, "sca%d" % h, "rt"], writes=["posa%d" % h])
                for h in H8:
                    R("dve", f_mi(posu[:, h, 8:16], sc[:, h, 8:16], S3[:, sl8[h]]), reads=alls3[2 * h:2 * h + 2] + ["scb%d" % h, "rt"], writes=["posb%d" % h])
                allpos = ["posa%d" % h for h in H8] + ["posb%d" % h for h in H8]
                allsc = ["sca%d" % h for h in H8] + ["scb%d" % h for h in H8]
                RT = dict(reads=["rt", sxn] + allpos + allsc + alls3 + allmx, writes=["rt", sxn] + allpos + alls3 + (JRn if ti == 0 else []))
                RC = dict(reads=RT["reads"] + ["consts", "c16"], writes=RT["writes"])
                R("dve", f_cp(posf[:], posu[:]), **RT)
                io4 = iota[:, 0:16].unsqueeze(1).unsqueeze(1).to_broadcast(B4)
                selv = lambda m, sel=sel: sel[:, m, :].rearrange("p (h k) -> p h k", h=8)
                R("dve", lambda e: e.tensor_single_scalar(out=aq[:], in_=posf[:], scalar=0.0625, op=ALU.mult), **RT)
                R("dve", f_cp(posu[:], aq[:]), **RT)
                R("dve", f_cp(aq[:], posu[:]), **RT)
                R("dve", lambda e: e.tensor_single_scalar(out=bq[:], in_=aq[:], scalar=16.0, op=ALU.mult), **RT)
                R("dve", f_tt(bq[:], bq[:], posf[:], ALU.is_gt), **RT)
                R("dve", f_tt(aq[:], aq[:], bq[:], ALU.subtract), **RT)
                R("dve", f_stt(bq[:], aq[:], -16.0, posf[:], ALU.mult, ALU.add), **RT)
                R("dve", f_tt(EQb, aq[:].unsqueeze(3).to_broadcast(B4), io4, ALU.is_equal), **RC)
                R("dve", f_tt(TMb, EQb, ixv[:, :, 0, :].unsqueeze(2).to_broadcast(B4), ALU.mult), **RT)
                R("dve", f_red(selv(0), TMb), **RT)
                R("dve", f_tt(EQb, bq[:].unsqueeze(3).to_broadcast(B4), io4, ALU.is_equal), **RC)
                R("dve", f_tt(TMb, EQb, ixv[:, :, 1, :].unsqueeze(2).to_broadcast(B4), ALU.mult), **RT)
                R("dve", f_red(selv(1), TMb), **RT)
            return L

        def rtail(gi):
            gt = groups[gi]
            L = []
            R = lambda eng, fn, **kw: L.append(lambda: P.op(eng, fn, **kw))
            for ti, t in enumerate(gt):
                sc, sel = scs[ti], sels[ti]
                RT = dict(reads=["rt"] + ["sca%d" % h for h in range(8)] + ["scb%d" % h for h in range(8)], writes=["rt"])
                selg = sel[:, 2, :].rearrange("p (h k) -> p h k", h=8)
                R("dve", f_tt(ex3[:], sc[:], sc[:, :, 0:1].to_broadcast(B3), ALU.subtract), **RT)
                R("act", f_act(ex3[:], ex3[:], AF.Exp), **RT)
                R("dve", f_red(Z[:], ex3[:]), **RT)
                R("dve", f_rcp(rZ[:], Z[:]), **RT)
                R("dve", f_tt(selg, ex3[:], rZ[:].unsqueeze(2).to_broadcast(B3), ALU.mult), **RT)
                for m in range(3):
                    R("pe", f_tr(psSel[:, m, :], sel[:, m, :], ident_f[:]), reads=["rt", "consts"], writes=["rt0"], sig=(m == 2))
                R("act", f_acp(selT[:, :, ti * 128:(ti + 1) * 128], psSel), reads=["rt0"], writes=["selT"])
            return L

        prebuilt = set()

        def gb_dve(gi, u):
            t0 = u * 32
            Jm, Rm = JR[u % 2]
            jn, rn = JRn[2 * (u % 2)], JRn[2 * (u % 2) + 1]
            for tl in range(32):
                tk = t0 + tl
                P.op("dve", lambda e, tl=tl, tk=tk: e.tensor_scalar(out=Jm[:, tl, :], in0=iota_bf[:], scalar1=selT[:, 1, tk:tk + 1], scalar2=None,
                                                                   op0=ALU.is_equal),
                     reads=["selT", "consts", "rt"] if tl == 0 else ["selT"], writes=[jn, "rt"] if (u == 0 and tl == 0) else [jn], nowaw=True)
                P.op("dve", lambda e, tl=tl, tk=tk: e.tensor_scalar(out=Rm[:, tl, :], in0=iota_bf[:], scalar1=selT[:, 0, tk:tk + 1],
                                                                   scalar2=selT[:, 2, tk:tk + 1], op0=ALU.is_equal, op1=ALU.mult),
                     reads=["selT"], writes=[rn], nowaw=True)

        def gbuild(gi, extra=()):
            extra = list(extra)
            gt = groups[gi]
            T = len(gt) * 128
            nsub = T // 32
            per_sub = (len(extra) + nsub - 1) // nsub if extra else 0
            io3 = iota[:].unsqueeze(1).to_broadcast([128, 32, 128])
            nev = 0
            for u in range(T // 32):
                t0 = u * 32
                Jm, Rm = JR[u % 2]
                jn, rn = JRn[2 * (u % 2)], JRn[2 * (u % 2) + 1]
                if not (gi in prebuilt and u < 2):
                    gb_dve(gi, u)
                for t4 in range(8):
                    pb, pr = rt[nev % 2], "rt%d" % (nev % 2)
                    for t in range(4):
                        tok = t4 * 4 + t
                        P.op("pe", f_mm(pb[:, t * 128:(t + 1) * 128], Jm[:, tok, :], Rm[:, tok, :], True, True),
                             reads=[jn, rn], writes=[pr], sig=(t == 3))
                    dstv = GT[:, t0 + t4 * 4:t0 + t4 * 4 + 4, :].rearrange("p t i -> p (t i)")
                    P.op("act", f_acp(dstv, pb[:]), reads=[pr], writes=["GT"], nowaw=True)
                    nev += 1
                for _ in range(per_sub):
                    if extra:
                        extra.pop(0)()
            while extra:
                extra.pop(0)()

        def main_loop(gi, thunks, late=()):
            late = list(late)
            gt = groups[gi]
            nt = len(gt)
            T = nt * 128
            hT, hres = hnT[gi % 2], "hnT%d" % (gi % 2)
            per = (len(thunks) + 71) // 72 if thunks else 0
            pos = [0]

            def load(i):
                b = i % NB
                P.dma(ub[b][:], UB[layer, i], reads=["UVB"], writes=["ub%d" % b])
                P.dma(vb[b][:], VB[layer, i * 128:(i + 1) * 128, :], reads=["UVB"], writes=["vb%d" % b])

            def stA(i):
                b = i % NB
                for kc in range(8):
                    P.op("pe", f_mm(psA[i % 2][:, :T], ub[b][:, kc * 128:(kc + 1) * 128], hT[:, kc, :T], kc == 0, kc == 7),
                         reads=["ub%d" % b, hres], writes=["psA%d" % (i % 2)], sig=(kc == 7))

            def stB(i):
                P.op("act", f_act(gl[i % 2][:, :T], psA[i % 2][:, :T], AF.Gelu), reads=["psA%d" % (i % 2)], writes=["gl%d" % (i % 2)])
                P.op("pool", f_tt(wt[i % 3][:, :T], gl[i % 2][:, :T], GT[:, 0:T, i], ALU.mult),
                     reads=["gl%d" % (i % 2), "GT"], writes=["wt%d" % (i % 3)])

            def stV(i):
                b = i % NB
                for tt in range(nt):
                    for hf in range(2):
                        a = acc[tt * 2 + hf]
                        P.op("pe", f_mm(a[:], wt[i % 3][:, tt * 128:(tt + 1) * 128], vb[b][:, hf * 512:(hf + 1) * 512], i == 0, i == 127),
                             reads=["wt%d" % (i % 3), "vb%d" % b], writes=["acc%d" % (tt * 2 + hf)], sig=(tt == nt - 1 and hf == 1))

            for i in range(3):
                load(i)
            for i in range(130):
                if i < 128:
                    stA(i)
                if 0 <= i - 1 < 128:
                    stB(i - 1)
                if i - 2 >= 0:
                    stV(i - 2)
                if i + 3 < 128:
                    load(i + 3)
                if i == 118:
                    for ti, t in enumerate(gt):
                        P.dma(xt[ti % 2][:], H[t * 128:(t + 1) * 128, :], reads=["H"], writes=["xp%d" % (ti % 2)])
                for _ in range(per):
                    if pos[0] < len(thunks):
                        thunks[pos[0]]()
                        pos[0] += 1
                if i >= 110:
                    for _ in range(2):
                        if late and pos[0] >= len(thunks):
                            late.pop(0)()
            while pos[0] < len(thunks):
                thunks[pos[0]]()
                pos[0] += 1
            while late:
                late.pop(0)()

        def epilogue(gi):
            gt = groups[gi]
            for ti, t in enumerate(gt):
                xb, xr = xt[ti % 2], "xp%d" % (ti % 2)
                for hf in range(2):
                    P.op("act", f_acp(SX[ti][:, hf * 512:(hf + 1) * 512], acc[ti * 2 + hf][:]),
                         reads=["acc%d" % (ti * 2 + hf)], writes=["SX%d" % ti], nowaw=(hf == 1))
                P.op("pool", f_tt(xb[:], xb[:], SX[ti][:, 0:1024], ALU.add), reads=["SX%d" % ti, xr], writes=[xr])
                P.dma(dst[t * 128:(t + 1) * 128, :], xb[:], reads=[xr], writes=["H" if dst is H else "outd"], nowaw=True)

        for th in burst(0):
            th()
        for th in routing_thunks(0):
            th()
        for th in rtail(0):
            th()
        for gi in range(len(groups)):
            more = gi + 1 < len(groups)
            gbuild(gi, burst(gi + 1) if more else ())
            late = []
            if more:
                late = rtail(gi + 1)
                late.append(lambda g1=gi + 1: (gb_dve(g1, 0), gb_dve(g1, 1), prebuilt.add(g1)))
            main_loop(gi, routing_thunks(gi + 1) if more else [], late)
            epilogue(gi)
        P.barrier()


def emit_pool(nc, P, C):
    H, poolw = C["H"], C["poolw"]
    gain_bc, ident_f = C["gain_bc"], C["ident_f"]
    W = 16 + RUNT
    with ExitStack() as es:
        sb = lambda name, shape, dt: es.enter_context(nc.sbuf_tensor(name, list(shape), dt))
        ps = lambda name, shape, dt: es.enter_context(nc.psum_tensor(name, list(shape), dt))
        wp = sb("wp", [128, 4, 2, 256], BF16)
        P.dma(wp[:], poolw.rearrange("g (k p) d -> p g k d", p=128), writes=["wp"], eng="pool")
        xt = [sb("xq%d" % i, [128, D], F32) for i in range(8)]
        xh = sb("xh", [128, D], F32)
        hn = sb("hnq", [128, D], F32)
        ss = sb("ssq", [128, 1], F32)
        rstd = sb("rstdq", [128, 1], F32)
        haloT = sb("haloT", [128, 8, 128], F32)
        XT = sb("XT", [128, 8, W], F32)
        SAB = [sb("SAB%d" % i, [128, 2, W], F32) for i in range(2)]
        PT = sb("PT", [128, 8, RUNT], BF16)
        psTf = ps("psTf", [128, 8, 128], F32)
        psY = [ps("psY%d" % i, [128, D], F32) for i in range(2)]
        gain = gain_bc[:, 1, :]
        scale = gain_bc[:, 4, :]

        def norm_T(xb, xr, dstv, dres):
            P.op("act", lambda e: e.activation(out=hn[:], in_=xb[:], func=AF.Square, accum_out=ss[:]), reads=[xr], writes=["hn", "ss"])
            P.op("act", lambda e: e.activation(out=ss[:], in_=ss[:], func=AF.Sqrt, bias=EPS, scale=1.0 / D), reads=["ss"], writes=["ss"])
            P.op("dve", lambda e: e.reciprocal(out=rstd[:], in_=ss[:]), reads=["ss"], writes=["rstd"])
            P.op("dve", lambda e: e.scalar_tensor_tensor(out=hn[:], in0=xb[:], scalar=rstd[:], in1=gain, op0=ALU.mult, op1=ALU.mult),
                 reads=[xr, "rstd", "consts"], writes=["hn"])
            for kc in range(8):
                P.op("pe", lambda e, kc=kc: e.transpose(psTf[:, kc, :], hn[:, kc * 128:(kc + 1) * 128], ident_f[:]),
                     reads=["hn", "consts"], writes=["psTf"], sig=(kc == 7))
            P.op("act", lambda e: e.copy(out=dstv, in_=psTf[:]), reads=["psTf"], writes=[dres])

        P.dma(xh[:], H[NOWN:NOWN + 128, :], reads=["H"], writes=["xh"])
        norm_T(xh, "xh", haloT[:], "haloT")
        def load_run(r):
            for ti in range(4):
                t = 4 * r + ti
                bi = (r % 2) * 4 + ti
                P.dma(xt[bi][:], H[t * 128:(t + 1) * 128, :], reads=["H"], writes=["xq%d" % bi])

        load_run(0)
        for r in range(NRUN):
            if r + 1 < NRUN:
                load_run(r + 1)
            P.op("dve", lambda e: e.tensor_copy(out=XT[:, :, 0:16], in_=haloT[:, :, 16 * r:16 * r + 16]), reads=["haloT"], writes=["XT"])
            for ti in range(4):
                t = 4 * r + ti
                bi = (r % 2) * 4 + ti
                norm_T(xt[bi], "xq%d" % bi, XT[:, :, 16 + ti * 128:16 + (ti + 1) * 128], "XT")
            for g in range(4):
                w = 2 << g
                cur, cres = XT[:, 2 * g:2 * g + 2, :], "XT"
                s_, k = 1, 0
                while s_ < w:
                    nxt, nres = SAB[k % 2], "SAB%d" % (k % 2)
                    P.op("dve", lambda e: e.tensor_tensor(out=nxt[:, :, s_:W], in0=cur[:, :, s_:W], in1=cur[:, :, 0:W - s_], op=ALU.add),
                         reads=[cres], writes=[nres])
                    cur, cres = nxt[:], nres
                    s_ *= 2
                    k += 1
                P.op("dve", lambda e: e.scalar_tensor_tensor(out=PT[:, 2 * g:2 * g + 2, :], in0=cur[:, :, 16:W], scalar=1.0 / w,
                                                             in1=XT[:, 2 * g:2 * g + 2, 16:W], op0=ALU.mult, op1=ALU.subtract),
                     reads=[cres, "XT"], writes=["PT"])
            for ti in range(4):
                t = 4 * r + ti
                py, pr = psY[ti % 2], "psY%d" % (ti % 2)
                for g in range(4):
                    for k2 in range(2):
                        P.op("pe", lambda e, g=g, k2=k2: e.matmul(py[:, g * 256:(g + 1) * 256], lhsT=PT[:, 2 * g + k2, ti * 128:(ti + 1) * 128],
                                                                  rhs=wp[:, g, k2, :], start=(k2 == 0), stop=(k2 == 1)),
                             reads=["PT", "wp"], writes=[pr], sig=(g == 3 and k2 == 1))
                P.op("dve", lambda e: e.tensor_tensor(out=hn[:], in0=py[:], in1=scale, op=ALU.mult), reads=[pr, "consts"], writes=["hn"])
                bi = (r % 2) * 4 + ti
                P.op("dve", lambda e: e.tensor_tensor(out=xt[bi][:], in0=hn[:], in1=xt[bi][:], op=ALU.add),
                     reads=["hn", "xq%d" % bi], writes=["xq%d" % bi])
                P.dma(H[t * 128:(t + 1) * 128, :], xt[bi][:], reads=["xq%d" % bi], writes=["H"], nowaw=True)
        P.barrier()


def _consts():
    bf = ml_dtypes.bfloat16
    kk = np.arange(128)
    c = {}
    c["ident_bf"] = np.eye(128, dtype=np.float32).astype(bf)
    c["ident_f"] = np.eye(128, dtype=np.float32)
    c["ntri"] = (-(kk[:, None] >= kk[None, :]).astype(np.float32)).astype(bf)
    c["nones"] = (-np.ones((128, 128), np.float32)).astype(bf)
    c["iota"] = np.tile(np.arange(128, dtype=np.float32)[None, :], (128, 1))
    c["iota_bf"] = c["iota"].astype(bf)
    return c


def _masks(p):
    bf = ml_dtypes.bfloat16
    kk = np.arange(128)[:, None]
    q = np.arange(512)[None, :]
    maskd = np.zeros((128, 9, 512), np.float32)
    maskd[:112, 0, :] = NEG
    for jb in range(8):
        allowed = (128 * jb + kk) < (512 * p + q)
        maskd[:, 1 + jb, :] = np.where(allowed, 0.0, NEG)
    hmask = np.zeros((128, NHB, 128), np.float32)
    for g in range(8):
        qb = 8 * g + 4 * p
        for i in range(16):
            col = g * 16 + i
            for j in range(NHB):
                if j < qb:
                    hmask[:, j, col] = 0.0
                elif j == qb:
                    hmask[:, j, col] = np.where(np.arange(128) < 112 + i, 0.0, NEG)
                else:
                    hmask[:, j, col] = NEG
            hmask[:112, 0, col] = NEG
    return maskd.astype(bf), hmask.astype(bf)


def prepare_in_maps(inputs, cores=range(8)):
    x = np.asarray(inputs["x"], np.float32)
    meta = np.asarray(inputs["meta"], np.float32)
    consts = _consts()
    gains = np.zeros((6, D), np.float32)
    gains[0:2] = np.asarray(inputs["norm_mix"], np.float32)
    gains[2:4] = np.asarray(inputs["norm_ffn"], np.float32)
    gains[4] = np.asarray(inputs["pool_scale"], np.float32)[0]
    gains[5, 0:64] = np.asarray(inputs["sb_q_gain"], np.float32)[0]
    gains[5, 64:128] = np.asarray(inputs["sb_k_gain"], np.float32)[0]
    u = np.asarray(inputs["peer_u"], np.float32)
    uS = np.ascontiguousarray(u.reshape(2, 128, 128, 8, 128).transpose(0, 1, 4, 3, 2)).reshape(2, 2048, 8192)
    shared = dict(
        wqkv=np.ascontiguousarray(np.asarray(inputs["sb_w_qkv"], np.float32)[0]),
        wo=np.ascontiguousarray(np.asarray(inputs["sb_w_o"], np.float32)[0]),
        gains=gains,
        poolw=np.ascontiguousarray(np.asarray(inputs["pool_w"], np.float32)[0]),
        wq_p=np.ascontiguousarray(np.asarray(inputs["peer_w_q"], np.float32)),
        skT=np.ascontiguousarray(np.asarray(inputs["peer_subkeys"], np.float32).transpose(0, 1, 3, 2)),
        uS=uS,
        vS=np.ascontiguousarray(np.asarray(inputs["peer_v"], np.float32)),
        **consts,
    )
    masks = [_masks(0), _masks(1)]
    in_maps = []
    for c in cores:
        b, p = c // 2, c % 2
        xall = np.zeros((LP, D), np.float32)
        xall[112:128] = meta
        xall[128:] = x[b]
        xq = np.zeros((NTOK, D), np.float32)
        for r in range(NRUN):
            k = 2 * r + p
            xq[r * RUNT:(r + 1) * RUNT] = x[b, k * RUNT:(k + 1) * RUNT]
            xq[NOWN + 16 * r: NOWN + 16 * (r + 1)] = meta if k == 0 else x[b, k * RUNT - 16:k * RUNT]
        m = dict(shared)
        m.update(xq=xq, xall=xall, maskd=masks[p][0], hmask=masks[p][1])
        in_maps.append(m)
    return in_maps


def kernel(**inputs):
    nc = build_program()
    in_maps = prepare_in_maps(inputs)
    res = run_bass_kernel_spmd(nc, in_maps, core_ids=list(range(8)))
    out = np.zeros((4, 8192, D), np.float32)
    for c in range(8):
        b, p = c // 2, c % 2
        o = np.asarray(res.results[c]["out"], np.float32)
        for r in range(NRUN):
            k = 2 * r + p
            out[b, k * RUNT:(k + 1) * RUNT] = o[r * RUNT:(r + 1) * RUNT]
    return out
```

```python
import numpy as np
import ml_dtypes
from contextlib import ExitStack
import concourse.bass as bass
import concourse.mybir as mybir
from concourse.bass_utils import run_bass_kernel_spmd

F32 = mybir.dt.float32
BF16 = mybir.dt.bfloat16
U32 = mybir.dt.uint32
AF = mybir.ActivationFunctionType
ALU = mybir.AluOpType
AX = mybir.AxisListType

D = 1024
NRUN = 8
RUNT = 512
NOWN = NRUN * RUNT
NTOK = NOWN + 128
NTILE = NTOK // 128
LP = 8320
NKB = LP // 128
NHB = 61
NEG = -30000.0
EPS = 1e-6


class Prog:
    ENG = ("pe", "act", "dve", "pool", "sp")

    def __init__(self, nc, es):
        self.nc = nc
        self.es = es
        self.eng = {"pe": nc.tensor, "act": nc.scalar, "dve": nc.vector, "pool": nc.gpsimd, "sp": nc.sync}
        self.esem = {e: es.enter_context(nc.semaphore("prog_" + e)) for e in ("pe", "act", "dve", "pool")}
        self.ecnt = {e: 0 for e in self.esem}
        self.dsem = {}
        self.lastw = {}
        self.readers = {}
        self.waited = {e: {} for e in self.ENG}
        self.nins = 0

    def _wait(self, eng, k, val):
        kind, key = k
        if kind == "d":
            val = max(val, self.dsem[key][1])
        d = self.waited[eng]
        if d.get(k, 0) >= val:
            return
        d[k] = val
        sem = self.esem[key] if kind == "e" else self.dsem[key][0]
        self.eng[eng].wait_ge(sem, val)
        self.nins += 1

    def op(self, eng, fn, reads=(), writes=(), sig=True, dma=False, nowaw=False):
        need = {}

        def add(dct):
            for k, v in dct.items():
                if need.get(k, 0) < v:
                    need[k] = v
        for r in reads:
            add(self.lastw.get(r, {}))
        for w in writes:
            if not nowaw:
                add(self.lastw.get(w, {}))
            add(self.readers.get(w, {}))
        for k, v in need.items():
            if k[0] == "e" and k[1] == eng and eng == "pe" and not dma:
                continue
            self._wait(eng, k, v)
        ins = fn(self.eng[eng])
        self.nins += 1
        if dma:
            key = writes[0] + "|" + (reads[0] if reads else "")
            if key not in self.dsem:
                self.dsem[key] = [self.es.enter_context(self.nc.semaphore("d%d" % len(self.dsem))), 0]
            self.dsem[key][1] += 16
            ins.then_inc(self.dsem[key][0], 16)
            tok = (("d", key), self.dsem[key][1])
        elif sig:
            self.ecnt[eng] += 1
            ins.then_inc(self.esem[eng], 1)
            tok = (("e", eng), self.ecnt[eng])
        else:
            tok = (("e", eng), self.ecnt[eng] + 1)
        for r in reads:
            rd = self.readers.setdefault(r, {})
            rd[tok[0]] = max(rd.get(tok[0], 0), tok[1])
        for w in writes:
            if nowaw:
                lw = self.lastw.setdefault(w, {})
                lw[tok[0]] = max(lw.get(tok[0], 0), tok[1])
            else:
                self.lastw[w] = {tok[0]: tok[1]}
            self.readers[w] = {}
        return tok

    def dma(self, out, in_, reads=(), writes=(), eng="sp", nowaw=False):
        return self.op(eng, lambda e: e.dma_start(out=out, in_=in_), reads=reads, writes=writes, dma=True, nowaw=nowaw)

    def barrier(self):
        toks = [(("e", e), c) for e, c in self.ecnt.items() if c > 0]
        toks += [(("d", k), v[1]) for k, v in self.dsem.items()]
        for eng in self.ENG:
            for k, v in toks:
                self._wait(eng, k, v)
        self.lastw = {}
        self.readers = {}


def bview(ap, shape):
    return ap.to_broadcast(shape)


def emit_norm_T(P, tag, x_sb, x_res, gain_bc, ident_bf, hn_sb, ss_sb, rstd_sb, junk_sb, psT, dst, dst_res):
    P.op("act", lambda e: e.activation(out=junk_sb, in_=x_sb, func=AF.Square, accum_out=ss_sb),
         reads=[x_res], writes=["junk", "ss"])
    P.op("act", lambda e: e.activation(out=ss_sb, in_=ss_sb, func=AF.Sqrt, bias=EPS, scale=1.0 / D),
         reads=["ss"], writes=["ss"])
    P.op("dve", lambda e: e.reciprocal(out=rstd_sb, in_=ss_sb), reads=["ss"], writes=["rstd"])
    P.op("dve", lambda e: e.scalar_tensor_tensor(out=hn_sb, in0=x_sb, scalar=rstd_sb, in1=gain_bc, op0=ALU.mult, op1=ALU.mult),
         reads=[x_res, "rstd", "consts"], writes=["hn"])
    for kc in range(8):
        P.op("pe", lambda e, kc=kc: e.transpose(psT[:, kc, :], hn_sb[:, kc * 128:(kc + 1) * 128], ident_bf),
             reads=["hn", "consts"], writes=["psT"], sig=(kc == 7))
    P.op("act", lambda e: e.copy(out=dst, in_=psT), reads=["psT"], writes=[dst_res])


def build_program(dbg=None):
    nc = bass.Bass("TRN2", target_bir_lowering=False)
    dram_in = {}

    def din(name, shape, dt=F32):
        dram_in[name] = nc.dram_tensor(name, list(shape), dt, kind="ExternalInput").ap()
        return dram_in[name]

    xq = din("xq", [NTOK, D])
    xall = din("xall", [LP, D])
    wqkv = din("wqkv", [D, 3 * D])
    wo = din("wo", [D, D])
    gains = din("gains", [6, D])
    poolw = din("poolw", [4, 256, 256])
    wq_p = din("wq_p", [2, D, 2048])
    skT = din("skT", [2, 2, 128, 128])
    uS = din("uS", [2, 2048, 8192])
    vS = din("vS", [2, 16384, D])
    ident_bf_d = din("ident_bf", [128, 128], BF16)
    ident_f_d = din("ident_f", [128, 128])
    ntri_d = din("ntri", [128, 128], BF16)
    nones_d = din("nones", [128, 128], BF16)
    iota_d = din("iota", [128, 128])
    iotab_d = din("iota_bf", [128, 128], BF16)
    maskd_d = din("maskd", [128, 9, 512], BF16)
    hmask_d = din("hmask", [128, NHB, 128], BF16)
    out = nc.dram_tensor("out", [NOWN, D], F32, kind="ExternalOutput").ap()
    dbg_out = None
    if dbg:
        dbg_out = nc.dram_tensor("dbg", [NTOK, D], F32, kind="ExternalOutput").ap()

    KT = nc.dram_tensor("KT", [16, 64, LP], BF16).ap()
    VS = nc.dram_tensor("VSc", [16, 128, NKB, 64], BF16).ap()
    QT = nc.dram_tensor("QT", [16, 64, NTOK], BF16).ap()
    OT = nc.dram_tensor("OT", [16, 64, NTOK], BF16).ap()
    H = nc.dram_tensor("H", [NTOK, D], F32).ap()
    UB = nc.dram_tensor("UB", [2, 128, 128, 1024], BF16).ap()
    VB = nc.dram_tensor("VB", [2, 16384, D], BF16).ap()
    WQB = nc.dram_tensor("WQB", [2, 128, 8, 2048], BF16).ap()

    with ExitStack() as es:
        P = Prog(nc, es)
        sb = lambda name, shape, dt: es.enter_context(nc.sbuf_tensor(name, list(shape), dt))

        ident_bf = sb("ident_bf_s", [128, 128], BF16)
        ident_f = sb("ident_f_s", [128, 128], F32)
        ntri = sb("ntri_s", [128, 128], BF16)
        nones = sb("nones_s", [128, 128], BF16)
        iota = sb("iota_s", [128, 128], F32)
        iota_bf = sb("iota_bf_s", [128, 128], BF16)
        gain_bc = sb("gain_bc", [128, 6, D], F32)
        for dst, src in ((ident_bf, ident_bf_d), (ident_f, ident_f_d), (ntri, ntri_d), (nones, nones_d), (iota, iota_d), (iota_bf, iotab_d)):
            P.dma(dst[:], src, writes=["consts"], nowaw=True)
        for i in range(6):
            P.dma(gain_bc[:, i, :], gains[i:i + 1, :].partition_broadcast(128), writes=["consts"], nowaw=True)

        es2 = ExitStack()
        NCB = 2
        cb = [es2.enter_context(nc.sbuf_tensor("castbuf%d" % i, [128, 8192], BF16)) for i in range(NCB)]
        pre = []
        n = 0
        for l in range(2):
            usrc = uS[l]
            udst = UB[l].rearrange("i (q a) f -> (i q) (a f)", a=8)
            vsrc = vS[l].rearrange("(q a) d -> q (a d)", a=8)
            vdst = VB[l].rearrange("(q a) d -> q (a d)", a=8)
            for src, dst in ((usrc, udst), (vsrc, vdst)):
                for blk in range(16):
                    def th(src=src, dst=dst, blk=blk, n=n):
                        b, r = cb[n % NCB], "castbuf%d" % (n % NCB)
                        P.dma(b[:], src[blk * 128:(blk + 1) * 128, :], writes=[r], eng="pool")
                        P.dma(dst[blk * 128:(blk + 1) * 128, :], b[:], reads=[r], writes=["UVB"], nowaw=True)
                    pre.append(th)
                    n += 1
        for l in range(2):
            for hf in range(2):
                def th(l=l, hf=hf, n=n):
                    b, r = cb[n % NCB], "castbuf%d" % (n % NCB)
                    bv = b[:].rearrange("p (k n) -> p k n", k=8)
                    P.dma(bv, wq_p[l].rearrange("(k p) n -> p k n", p=128)[:, :, hf * 1024:(hf + 1) * 1024], writes=[r], eng="pool")
                    P.dma(WQB[l, :, :, hf * 1024:(hf + 1) * 1024], bv, reads=[r], writes=["UVB"], nowaw=True)
                pre.insert(0, th)
                n += 1
        pre_state = (pre, es2)

        C = dict(locals())
        emit_attention(nc, P, C)
        if dbg == "attn":
            emit_copy_rows(nc, P, H, dbg_out, NTILE)
            return nc
        emit_peer(nc, P, C, layer=0, ntiles=NTILE, dst=H)
        if dbg == "peer0":
            emit_copy_rows(nc, P, H, dbg_out, NTILE)
            return nc
        emit_pool(nc, P, C)
        if dbg == "pool":
            emit_copy_rows(nc, P, H, dbg_out, NTILE)
            return nc
        emit_peer(nc, P, C, layer=1, ntiles=NOWN // 128, dst=out)
        P.barrier()
        print("[kernel] instructions=%d dma_sems=%d" % (P.nins, len(P.dsem)))
    return nc


def emit_copy_rows(nc, P, src, dst, ntile):
    P.barrier()
    P.dma(dst, src, writes=["dbgout"])
    P.barrier()


def emit_attention(nc, P, C):
    xq, xall, wqkv, wo = C["xq"], C["xall"], C["wqkv"], C["wo"]
    KT, VS, QT, OT, H = C["KT"], C["VS"], C["QT"], C["OT"], C["H"]
    gain_bc, ident_bf, ntri, nones = C["gain_bc"], C["ident_bf"], C["ntri"], C["nones"]
    maskd_d, hmask_d = C["maskd_d"], C["hmask_d"]

    pre, es_pre = C["pre_state"]
    with ExitStack() as es:
        sb = lambda name, shape, dt: es.enter_context(nc.sbuf_tensor(name, list(shape), dt))
        ps = lambda name, shape, dt: es.enter_context(nc.psum_tensor(name, list(shape), dt))
        w_sb = sb("wqkv_s", [128, 8, 3072], BF16)
        P.dma(w_sb[:], wqkv.rearrange("(k p) n -> p k n", p=128), writes=["wqkv"], eng="pool")
        gq = sb("gq", [128, 64], F32)
        P.op("dve", lambda e: e.scalar_tensor_tensor(out=gq[:], in0=gain_bc[:, 5, 0:64], scalar=0.125,
                                                     in1=gain_bc[:, 5, 64:128], op0=ALU.mult, op1=ALU.mult),
             reads=["consts"], writes=["gq"])
        two = lambda name, shape, dt: [sb("%s%d" % (name, i), shape, dt) for i in range(2)]
        x_t = two("xt", [128, D], F32)
        hn = two("hn", [128, D], BF16)
        junk = two("junk", [128, D], BF16)
        ss = two("ss", [128, 1], F32)
        rstd = two("rstd", [128, 1], F32)
        hnT = two("hnT", [128, 8, 128], BF16)
        sq = two("sq", [128, D], F32)
        ssh = two("ssh", [128, 16], F32)
        rsh = two("rsh", [128, 16], F32)
        kn = two("kn", [128, D], BF16)
        knT = two("knT", [128, 8, 128], BF16)
        v_sb = two("v_sb", [128, D], BF16)
        psT_ = ps("psT", [128, 8, 128], BF16)
        psT2_ = ps("psT2", [128, 8, 128], BF16)
        psT = [psT_, psT_]
        psT2 = [psT2_, psT2_]
        psKs = [ps("psK%d" % i, [128, D], F32) for i in range(2)]
        psV = ps("psV", [128, D], F32)

        def proj(dst_ps, res, hT, hres, col0):
            for half in range(2):
                for kc in range(8):
                    P.op("pe", f_mm(dst_ps[:, half * 512:(half + 1) * 512], hT[:, kc, :],
                                    w_sb[:, kc, col0 + half * 512: col0 + (half + 1) * 512], kc == 0, kc == 7),
                         reads=[hres, "wqkv"], writes=[res], sig=(kc == 7))

        def headnorm_1(is_q, b):
            sfx = str(b)
            psK, pkr = psKs[b], "psK%d" % b
            hv = lambda ap: ap.rearrange("p (h d) -> p h d", d=64)
            P.op("act", f_act(sq[b][:], psK[:], AF.Square), reads=[pkr], writes=["sq" + sfx])
            P.op("dve", f_red(ssh[b][:], hv(sq[b][:])), reads=["sq" + sfx], writes=["ssh" + sfx])
            P.op("act", f_act(ssh[b][:], ssh[b][:], AF.Sqrt, bias=EPS, scale=1.0 / 64), reads=["ssh" + sfx], writes=["ssh" + sfx])
            P.op("dve", f_rcp(rsh[b][:], ssh[b][:]), reads=["ssh" + sfx], writes=["rsh" + sfx])
            rb = rsh[b][:].unsqueeze(2).to_broadcast([128, 16, 64])
            if is_q:
                P.op("dve", f_tt(hv(sq[b][:]), hv(psK[:]), rb, ALU.mult), reads=[pkr, "rsh" + sfx, "sq" + sfx], writes=["sq" + sfx])
                P.op("dve", f_tt(hv(kn[b][:]), hv(sq[b][:]), gq[:].unsqueeze(1).to_broadcast([128, 16, 64]), ALU.mult),
                     reads=["sq" + sfx, "gq"], writes=["kn" + sfx])
            else:
                P.op("dve", f_tt(hv(kn[b][:]), hv(psK[:]), rb, ALU.mult), reads=[pkr, "rsh" + sfx], writes=["kn" + sfx])

        def headnorm_2(n_):
            rows, is_q, t = jobs[n_]
            b = n_ % 2
            sfx = str(b)
            for pr in range(8):
                P.op("pe", f_tr(psT2[b][:, pr, :], kn[b][:, pr * 128:(pr + 1) * 128], ident_bf[:]),
                     reads=["kn" + sfx, "consts"], writes=["psT2"], sig=(pr == 7))
            P.op("dve", f_cp(knT[b][:], psT2[b][:]), reads=["psT2"], writes=["knT" + sfx])
            dstT = QT if is_q else KT
            for hh in range(2):
                P.dma(dstT[hh::2, :, t * 128:(t + 1) * 128].rearrange("h p t -> p h t"), knT[b][hh * 64:(hh + 1) * 64, :, :],
                      reads=["knT%d" % b], writes=["QT" if is_q else "KT"], nowaw=True)

        def load_x(src_rows, t):
            b = t % 2
            P.dma(x_t[b][:], src_rows, writes=["xt%d" % b])

        def norm_a(t):
            b = t % 2
            sfx = str(b)
            xb, xr = x_t[b], "xt" + sfx
            P.op("act", f_act(junk[b][:], xb[:], AF.Square, accum_out=ss[b][:]), reads=[xr], writes=["junk" + sfx, "ss" + sfx])
            P.op("act", f_act(ss[b][:], ss[b][:], AF.Sqrt, bias=EPS, scale=1.0 / D), reads=["ss" + sfx], writes=["ss" + sfx])
            P.op("dve", f_rcp(rstd[b][:], ss[b][:]), reads=["ss" + sfx], writes=["rstd" + sfx])
            P.op("dve", f_stt(hn[b][:], xb[:], rstd[b][:], gain_bc[:, 0, :], ALU.mult, ALU.mult),
                 reads=[xr, "rstd" + sfx, "consts"], writes=["hn" + sfx])

        def norm_b(t):
            b = t % 2
            sfx = str(b)
            for kc in range(8):
                P.op("pe", f_tr(psT[b][:, kc, :], hn[b][:, kc * 128:(kc + 1) * 128], ident_bf[:]),
                     reads=["hn" + sfx, "consts"], writes=["psT"], sig=(kc == 7))
            P.op("act", f_acp(hnT[b][:], psT[b][:]), reads=["psT"], writes=["hnT" + sfx])
            return hnT[b], "hnT" + sfx

        npop = [0]

        def pop_pre():
            if pre and npop[0] < 36:
                pre.pop(0)()
                npop[0] += 1

        jobs = [(xall[t * 128:(t + 1) * 128, :], False, t) for t in range(NKB)] + [(xq[t * 128:(t + 1) * 128, :], True, t) for t in range(NTILE)]
        NJ = len(jobs)
        load_x(jobs[0][0], 0)
        load_x(jobs[1][0], 1)
        norm_a(0)
        nxt = norm_b(0)
        for n_, (rows, is_q, t) in enumerate(jobs):
            b = n_ % 2
            hT, hres = nxt
            if n_ + 2 < NJ:
                load_x(jobs[n_ + 2][0], n_ + 2)
            if n_ + 1 < NJ:
                norm_a(n_ + 1)
            if not is_q:
                proj(psKs[b], "psK%d" % b, hT, hres, 1024)
                proj(psV, "psV", hT, hres, 2048)
            else:
                proj(psKs[b], "psK%d" % b, hT, hres, 0)
            if n_ >= 1:
                headnorm_2(n_ - 1)
            if n_ + 1 < NJ:
                nxt = norm_b(n_ + 1)
            if not is_q:
                P.op("act", f_acp(v_sb[b][:], psV[:]), reads=["psV"], writes=["v_sb%d" % b])
                P.dma(VS[:, :, t, :].rearrange("h p d -> p h d"), v_sb[b][:].rearrange("p (h d) -> p h d", d=64),
                      reads=["v_sb%d" % b], writes=["VS"], nowaw=True)
            headnorm_1(is_q, b)
            pop_pre()
        headnorm_2(NJ - 1)
        while pre and npop[0] < 36:
            pop_pre()
        P.barrier()

    with ExitStack() as es:
        sb = lambda name, shape, dt: es.enter_context(nc.sbuf_tensor(name, list(shape), dt))
        ps = lambda name, shape, dt: es.enter_context(nc.psum_tensor(name, list(shape), dt))
        maskd = sb("maskd_s", [128, 9, 512], BF16)
        hmask = sb("hmask_s", [128, NHB, 128], BF16)
        P.dma(maskd[:], maskd_d, writes=["masks"], nowaw=True)
        P.dma(hmask[:], hmask_d, writes=["masks"], nowaw=True)
        kt = [sb("kt%d" % i, [64, LP], BF16) for i in range(2)]
        vt = [sb("vt%d" % i, [128, NKB, 64], BF16) for i in range(2)]
        qt = [sb("qt%d" % i, [64, NTOK], BF16) for i in range(2)]
        ot = [sb("ot%d" % i, [64, NTOK], BF16) for i in range(2)]
        Eb = [sb("E%d" % i, [128, 512], BF16) for i in range(3)]
        SPb = [sb("SP%d" % i, [128, 512], BF16) for i in range(3)]
        Ab = [sb("A%d" % i, [128, 512], BF16) for i in range(3)]
        RSb = [sb("RS%d" % i, [128, 512], BF16) for i in range(2)]
        ps1 = [ps("ps1_%d" % i, [128, 512], F32) for i in range(4)]
        pso = [ps("pso%d" % i, [64, 512], F32) for i in range(2)]

        groups = [(r, r * RUNT, RUNT, 8 * r + 9) for r in range(NRUN)] + [(-1, NOWN, 128, NHB)]
        steps = []
        gidx = 0
        for h in range(16):
            for gi_, (r, off, N, nkb) in enumerate(groups):
                for s_, j in enumerate(range(nkb - 1, -1, -1)):
                    steps.append(dict(h=h, b=h % 2, r=r, off=off, N=N, nkb=nkb, s=s_, j=j, g=gidx,
                                      first=(gi_ == 0 and s_ == 0), last_of_head=(gi_ == len(groups) - 1 and s_ == nkb - 1)))
                gidx += 1

        def mask_for(d):
            r, j = d["r"], d["j"]
            if r < 0:
                return hmask[:, j, :]
            if j == 0:
                return maskd[:, 0, :]
            if j >= 8 * r + 1:
                return maskd[:, 1 + j - (8 * r + 1), :]
            return None

        def S1(G):
            d = steps[G]
            h, b, N, off, j = d["h"], d["b"], d["N"], d["off"], d["j"]
            if d["first"]:
                P.dma(kt[b][:], KT[h], reads=["KT"], writes=["kt%d" % b])
                P.dma(vt[b][:], VS[h], reads=["VS"], writes=["vt%d" % b])
                P.dma(qt[b][:], QT[h], reads=["QT"], writes=["qt%d" % b])
            if d["s"] == 0 and d["g"] % 4 == 0 and pre:
                pre.pop(0)()
            p1, pres = ps1[G % 4], "ps1_%d" % (G % 4)
            m = mask_for(d)
            P.op("pe", f_mm(p1[:, :N], kt[b][:, j * 128:(j + 1) * 128], qt[b][:, off:off + N], True, m is None),
                 reads=["kt%d" % b, "qt%d" % b], writes=[pres], sig=(m is None))
            if m is not None:
                P.op("pe", f_mm(p1[:, :N], ident_bf[:], m, False, True), reads=["masks", "consts"], writes=[pres])
            P.op("act", f_act(Eb[G % 3][:, :N], p1[:, :N], AF.Exp), reads=[pres], writes=["E%d" % (G % 3)])

        def S1b(G):
            N = steps[G]["N"]
            P.op("act", f_act(SPb[G % 3][:, :N], Eb[G % 3][:, :N], AF.Ln, bias=1.0), reads=["E%d" % (G % 3)], writes=["SP%d" % (G % 3)])

        def S2a(G):
            d = steps[G]
            N, s_, nkb = d["N"], d["s"], d["nkb"]
            p1, pres = ps1[G % 4], "ps1_%d" % (G % 4)
            P.op("pe", lambda e: e.matmul(p1[:, :N], lhsT=ntri[:], rhs=SPb[G % 3][:, :N], start=False, stop=(s_ == 0), skip_group_check=True),
                 reads=["SP%d" % (G % 3), "consts"], writes=[pres], sig=(s_ == 0))
            if s_ > 0:
                P.op("pe", lambda e: e.matmul(p1[:, :N], lhsT=nones[:], rhs=RSb[(G - 1) % 2][:, :N], start=False, stop=True, skip_group_check=True),
                     reads=["RS%d" % ((G - 1) % 2), "consts"], writes=[pres])
            if s_ < nkb - 1:
                if s_ == 0:
                    P.op("dve", f_cp(RSb[G % 2][:, :N], SPb[G % 3][:, :N]), reads=["SP%d" % (G % 3)], writes=["RS%d" % (G % 2)])
                else:
                    P.op("dve", f_tt(RSb[G % 2][:, :N], RSb[(G - 1) % 2][:, :N], SPb[G % 3][:, :N], ALU.add),
                         reads=["RS%d" % ((G - 1) % 2), "SP%d" % (G % 3)], writes=["RS%d" % (G % 2)])
            P.op("act", f_act(Ab[G % 3][:, :N], p1[:, :N], AF.Exp), reads=[pres], writes=["A%d" % (G % 3)])

        def S2b(G):
            d = steps[G]
            h, b, N, off, j, s_, nkb = d["h"], d["b"], d["N"], d["off"], d["j"], d["s"], d["nkb"]
            o_ps, ores = pso[d["g"] % 2], "pso%d" % (d["g"] % 2)
            P.op("pe", f_mm(o_ps[:, :N], vt[b][:, j, :], Ab[G % 3][:, :N], s_ == 0, s_ == nkb - 1),
                 reads=["A%d" % (G % 3), "vt%d" % b], writes=[ores], sig=(s_ == nkb - 1))
            if s_ == nkb - 1:
                P.op("dve", f_cp(ot[b][:, off:off + N], o_ps[:, :N]), reads=[ores], writes=["ot%d" % b])
            if d["last_of_head"]:
                P.dma(OT[h], ot[b][:], reads=["ot%d" % b], writes=["OT"], nowaw=True)

        NS = len(steps)
        for G in range(NS + 3):
            if G < NS:
                S1(G)
            if 0 <= G - 1 < NS:
                S1b(G - 1)
            if 0 <= G - 2 < NS:
                S2a(G - 2)
            if 0 <= G - 3 < NS:
                S2b(G - 3)
        while pre:
            pre.pop(0)()
        P.barrier()
    es_pre.close()

    with ExitStack() as es:
        sb = lambda name, shape, dt: es.enter_context(nc.sbuf_tensor(name, list(shape), dt))
        ps = lambda name, shape, dt: es.enter_context(nc.psum_tensor(name, list(shape), dt))
        wo_sb = sb("wo_s", [128, 8, D], BF16)
        P.dma(wo_sb[:], wo.rearrange("(p r) n -> r p n", r=128), writes=["wo"], eng="pool")
        OTp = OT.rearrange("(p two) d t -> p (two d) t", two=2)
        ott = [sb("ott%d" % i, [128, 8, 128], BF16) for i in range(2)]
        xt = [sb("xc%d" % i, [128, D], F32) for i in range(2)]
        res = [sb("res%d" % i, [128, D], F32) for i in range(2)]
        psC = [ps("psC%d" % i, [128, D], F32) for i in range(2)]

        def loadC(t):
            b = t % 2
            P.dma(ott[b][:], OTp[:, :, t * 128:(t + 1) * 128].rearrange("p r t -> r p t"), reads=["OT"], writes=["ott%d" % b])
            P.dma(xt[b][:], xq[t * 128:(t + 1) * 128, :], writes=["xc%d" % b])

        loadC(0)
        for t in range(NTILE):
            b = t % 2
            if t + 1 < NTILE:
                loadC(t + 1)
            for half in range(2):
                for pr in range(8):
                    P.op("pe", f_mm(psC[b][:, half * 512:(half + 1) * 512], ott[b][:, pr, :], wo_sb[:, pr, half * 512:(half + 1) * 512],
                                    pr == 0, pr == 7),
                         reads=["ott%d" % b, "wo"], writes=["psC%d" % b], sig=(pr == 7))
            P.op("dve", f_tt(res[b][:], psC[b][:], xt[b][:], ALU.add), reads=["psC%d" % b, "xc%d" % b], writes=["res%d" % b])
            P.dma(H[t * 128:(t + 1) * 128, :], res[b][:], reads=["res%d" % b], writes=["H"], nowaw=True)
        P.barrier()


def f_max(o, i): return lambda e: e.max(out=o, in_=i)
def f_mr(o, r, v): return lambda e: e.match_replace(out=o, in_to_replace=r, in_values=v, imm_value=-1e30)
def f_mi(o, m, v): return lambda e: e.max_index(out=o, in_max=m, in_values=v)
def f_tt(o, a, b, op): return lambda e: e.tensor_tensor(out=o, in0=a, in1=b, op=op)
def f_cp(o, i): return lambda e: e.tensor_copy(out=o, in_=i)
def f_acp(o, i): return lambda e: e.copy(out=o, in_=i)
def f_red(o, i): return lambda e: e.tensor_reduce(out=o, in_=i, axis=AX.X, op=ALU.add)
def f_mm(o, l, r, st, sp): return lambda e: e.matmul(o, lhsT=l, rhs=r, start=st, stop=sp)
def f_tr(o, i, ident): return lambda e: e.transpose(o, i, ident)
def f_act(o, i, func, **kw): return lambda e: e.activation(out=o, in_=i, func=func, **kw)
def f_stt(o, a, sc, b, op0, op1): return lambda e: e.scalar_tensor_tensor(out=o, in0=a, scalar=sc, in1=b, op0=op0, op1=op1)
def f_rcp(o, i): return lambda e: e.reciprocal(out=o, in_=i)


def emit_peer(nc, P, C, layer, ntiles, dst):
    H, WQB, skT, UB, VB = C["H"], C["WQB"], C["skT"], C["UB"], C["VB"]
    gain_bc, ident_bf, ident_f, iota, iota_bf = C["gain_bc"], C["ident_bf"], C["ident_f"], C["iota"], C["iota_bf"]
    NT = 2
    TM = NT * 128
    NB = 6
    with ExitStack() as es:
        sb = lambda name, shape, dt: es.enter_context(nc.sbuf_tensor("%s_L%d" % (name, layer), list(shape), dt))
        ps = lambda name, shape, dt: es.enter_context(nc.psum_tensor("%s_L%d" % (name, layer), list(shape), dt))
        big = sb("big", [128, 16384], BF16)
        bigf = big[:].bitcast(F32)
        S, S2, EQ, TMP = (bigf[:, i * 2048:(i + 1) * 2048] for i in range(4))
        JR = [(big[:, (2 * x) * 4096:(2 * x + 1) * 4096].rearrange("p (t j) -> p t j", j=128),
               big[:, (2 * x + 1) * 4096:(2 * x + 2) * 4096].rearrange("p (t j) -> p t j", j=128)) for x in range(2)]
        JRn = ["JmA", "RmA", "JmB", "RmB"]
        GT = sb("GT", [128, TM, 128], BF16)
        hnT = [sb("hnTp%d" % i, [128, 8, TM], BF16) for i in range(2)]
        qT = sb("qT", [128, 16, TM], BF16)
        wqs = [sb("wqs%d" % i, [128, 8, 128], BF16) for i in range(2)]
        skT_sb = sb("skT_s", [128, 2, 128], BF16)
        P.dma(skT_sb[:], skT[layer].rearrange("s d n -> d s n"), writes=["skT"], eng="pool")
        xt = [sb("xp%d" % i, [128, D], F32) for i in range(2)]
        hn = sb("hnp", [128, D], BF16)
        ss = sb("ssp", [128, 1], F32)
        rstd = sb("rstdp", [128, 1], F32)
        mx = sb("mx", [128, 16, 16], F32)
        ixu = sb("ixu", [128, 16, 16], U32)
        ixf = sb("ixf", [128, 16, 16], F32)
        posu = sb("posu", [128, 8, 16], U32)
        posf = sb("posf", [128, 8, 16], F32)
        aq = sb("aq", [128, 8, 16], F32)
        bq = sb("bq", [128, 8, 16], F32)
        ex3 = sb("ex3", [128, 8, 16], F32)
        Z = sb("Z", [128, 8], F32)
        rZ = sb("rZ", [128, 8], F32)
        selT = sb("selT", [128, 3, TM], F32)
        ub = [sb("ub%d" % i, [128, 1024], BF16) for i in range(NB)]
        vb = [sb("vb%d" % i, [128, 1024], BF16) for i in range(NB)]
        gl = [sb("gl%d" % i, [128, TM], BF16) for i in range(2)]
        wt = [sb("wt%d" % i, [128, TM], BF16) for i in range(3)]
        c16 = sb("c16", [128, 32], F32)
        P.op("dve", lambda e: e.tensor_single_scalar(out=c16[:], in_=iota[:, 0:32], scalar=16.0, op=ALU.mult), reads=["consts"], writes=["c16"])
        acc = [ps("acc%d" % i, [128, 512], F32) for i in range(4)]
        psA = [ps("psA%d" % i, [128, 512], F32) for i in range(2)]
        rt = [ps("rtb%d" % i, [128, 512], F32) for i in range(2)]
        psT = rt[0][:].bitcast(BF16)[:, 0:1024].rearrange("p (k t) -> p k t", k=8)
        psSel = rt[0][:, 0:384].rearrange("p (m t) -> p m t", m=3)
        gain = gain_bc[:, 2 + layer, :]

        tiles = list(range(ntiles))
        groups = [tiles[i:i + NT] for i in range(0, ntiles, NT)]
        B4 = [128, 8, 16, 16]
        B3 = [128, 8, 16]

        SX = [sb("SX%d" % i, [128, 2048], F32)[:] for i in range(2)]
        S3 = EQ
        bigb = big[:]
        EQb = bigb[:, 12288:14336].rearrange("p (h k a) -> p h k a", h=8, k=16)
        TMb = bigb[:, 14336:16384].rearrange("p (h k a) -> p h k a", h=8, k=16)
        scs = [sb("sc%d" % i, [128, 8, 16], F32) for i in range(2)]
        sels = [sb("sel%d" % i, [128, 3, 128], F32) for i in range(2)]

        bT = acc[0][:].bitcast(BF16)[:, 0:1024].rearrange("p (k t) -> p k t", k=8)

        def burst(gi):
            gt = groups[gi]
            T = len(gt) * 128
            hT, hres = hnT[gi % 2], "hnT%d" % (gi % 2)
            L = []
            R = lambda eng, fn, **kw: L.append(lambda: P.op(eng, fn, **kw))
            for ti, t in enumerate(gt):
                xb, xr = xt[ti % 2], "xp%d" % (ti % 2)
                L.append(lambda xb=xb, xr=xr, t=t: P.dma(xb[:], H[t * 128:(t + 1) * 128, :], reads=["H"], writes=[xr]))
                R("act", f_act(hn[:], xb[:], AF.Square, accum_out=ss[:]), reads=[xr], writes=["hn", "ss"])
                R("act", f_act(ss[:], ss[:], AF.Sqrt, bias=EPS, scale=1.0 / D), reads=["ss"], writes=["ss"])
                R("dve", f_rcp(rstd[:], ss[:]), reads=["ss"], writes=["rstd"])
                R("dve", f_stt(hn[:], xb[:], rstd[:], gain, ALU.mult, ALU.mult), reads=[xr, "rstd", "consts"], writes=["hn"])
                for kc in range(8):
                    R("pe", f_tr(bT[:, kc, :], hn[:, kc * 128:(kc + 1) * 128], ident_bf[:]), reads=["hn", "consts"], writes=["acc0"], sig=(kc == 7))
                R("act", f_acp(hT[:, :, ti * 128:(ti + 1) * 128], bT), reads=["acc0"], writes=[hres])
            for g in range(16):
                wb, wr = wqs[g % 2], "wqs%d" % (g % 2)
                L.append(lambda wb=wb, wr=wr, g=g: P.dma(wb[:], WQB[layer, :, :, g * 128:(g + 1) * 128], reads=["UVB"], writes=[wr]))
                pq, pr = acc[1 + g % 2], "acc%d" % (1 + g % 2)
                for kc in range(8):
                    R("pe", f_mm(pq[:, :T], wb[:, kc, :], hT[:, kc, :T], kc == 0, kc == 7), reads=[wr, hres], writes=[pr], sig=(kc == 7))
                R("act", f_acp(qT[:, g, :T], pq[:, :T]), reads=[pr], writes=["qT"])
            for ti, t in enumerate(gt):
                for q4 in range(4):
                    pq, pr = acc[1 + q4 % 2], "acc%d" % (1 + q4 % 2)
                    for gg in range(4):
                        g = q4 * 4 + gg
                        R("pe", f_mm(pq[:, gg * 128:(gg + 1) * 128], qT[:, g, ti * 128:(ti + 1) * 128], skT_sb[:, g % 2, :], True, True),
                          reads=["qT", "skT"], writes=[pr], sig=(gg == 3))
                    R("act", f_acp(SX[ti][:, q4 * 512:(q4 + 1) * 512], pq[:]), reads=[pr], writes=["SX%d" % ti])
            return L

        def routing_thunks(gi):
            gt = groups[gi]
            L = []
            R = lambda eng, fn, **kw: L.append(lambda: P.op(eng, fn, **kw))
            for ti, t in enumerate(gt):
                RT = dict(reads=["rt", "SX%d" % ti], writes=["rt", "SX%d" % ti] + (JRn if ti == 0 else []))
                RC = dict(reads=["rt", "consts", "c16", "SX%d" % ti], writes=["rt", "SX%d" % ti])
                Sx = SX[ti]
                sc, sel = scs[ti], sels[ti]
                sxn = "SX%d" % ti
                G16 = range(16)
                sl16 = [slice(g * 128, (g + 1) * 128) for g in G16]
                for g in G16:
                    R("dve", f_max(mx[:, g, 0:8], Sx[:, sl16[g]]), reads=[sxn, "rt"], writes=["mxa%d" % g])
                for g in G16:
                    R("dve", f_mr(S3[:, sl16[g]], mx[:, g, 0:8], Sx[:, sl16[g]]), reads=[sxn, "mxa%d" % g, "rt"], writes=["S3_%d" % g] + (JRn if ti == 0 else []))
                for g in G16:
                    R("dve", f_max(mx[:, g, 8:16], S3[:, sl16[g]]), reads=["S3_%d" % g], writes=["mxb%d" % g])
                for g in G16:
                    R("dve", f_mi(ixu[:, g, 0:8], mx[:, g, 0:8], Sx[:, sl16[g]]), reads=[sxn, "mxa%d" % g], writes=["ixa%d" % g])
                for g in G16:
                    R("dve", f_mi(ixu[:, g, 8:16], mx[:, g, 8:16], S3[:, sl16[g]]), reads=["S3_%d" % g, "mxb%d" % g], writes=["ixb%d" % g])
                allmx = ["mxa%d" % g for g in G16] + ["mxb%d" % g for g in G16]
                allix = ["ixa%d" % g for g in G16] + ["ixb%d" % g for g in G16]
                alls3 = ["S3_%d" % g for g in G16]
                R("dve", f_cp(ixf[:], ixu[:]), reads=allix + ["rt"], writes=["rt"])
                mxv = mx[:].rearrange("p (h two) k -> p h two k", two=2)
                ixv = ixf[:].rearrange("p (h two) k -> p h two k", two=2)
                cand4 = Sx.rearrange("p (h a b) -> p h a b", h=8, a=16)
                R("dve", f_tt(cand4, mxv[:, :, 0, :].unsqueeze(3).to_broadcast(B4), mxv[:, :, 1, :].unsqueeze(2).to_broadcast(B4), ALU.add),
                  reads=allmx + allix + [sxn, "rt"], writes=[sxn, "rt"])
                H8 = range(8)
                sl8 = [slice(h * 256, (h + 1) * 256) for h in H8]
                for h in H8:
                    R("dve", f_max(sc[:, h, 0:8], Sx[:, sl8[h]]), reads=[sxn, "rt"], writes=["sca%d" % h])
                for h in H8:
                    R("dve", f_mr(S3[:, sl8[h]], sc[:, h, 0:8], Sx[:, sl8[h]]), reads=[sxn, "sca%d" % h] + alls3[2 * h:2 * h + 2],
                      writes=alls3[2 * h:2 * h + 2])
                for h in H8:
                    R("dve", f_max(sc[:, h, 8:16], S3[:, sl8[h]]), reads=alls3[2 * h:2 * h + 2], writes=["scb%d" % h])
                for h in H8:
                    R("dve", f_mi(posu[:, h, 0:8], sc[:, h, 0:8], Sx[:, sl8[h]]), reads=[sxn, "sca%d" % h, "rt"], writes=["posa%d" % h])
                for h in H8:
                    R("dve", f_mi(posu[:, h, 8:16], sc[:, h, 8:16], S3[:, sl8[h]]), reads=alls3[2 * h:2 * h + 2] + ["scb%d" % h, "rt"], writes=["posb%d" % h])
                allpos = ["posa%d" % h for h in H8] + ["posb%d" % h for h in H8]
                allsc = ["sca%d" % h for h in H8] + ["scb%d" % h for h in H8]
                RT = dict(reads=["rt", sxn] + allpos + allsc + alls3 + allmx, writes=["rt", sxn] + allpos + alls3 + (JRn if ti == 0 else []))
                RC = dict(reads=RT["reads"] + ["consts", "c16"], writes=RT["writes"])
                R("dve", f_cp(posf[:], posu[:]), **RT)
                io4 = iota[:, 0:16].unsqueeze(1).unsqueeze(1).to_broadcast(B4)
                selv = lambda m, sel=sel: sel[:, m, :].rearrange("p (h k) -> p h k", h=8)
                R("dve", lambda e: e.tensor_single_scalar(out=aq[:], in_=posf[:], scalar=0.0625, op=ALU.mult), **RT)
                R("dve", f_cp(posu[:], aq[:]), **RT)
                R("dve", f_cp(aq[:], posu[:]), **RT)
                R("dve", lambda e: e.tensor_single_scalar(out=bq[:], in_=aq[:], scalar=16.0, op=ALU.mult), **RT)
                R("dve", f_tt(bq[:], bq[:], posf[:], ALU.is_gt), **RT)
                R("dve", f_tt(aq[:], aq[:], bq[:], ALU.subtract), **RT)
                R("dve", f_stt(bq[:], aq[:], -16.0, posf[:], ALU.mult, ALU.add), **RT)
                R("dve", f_tt(EQb, aq[:].unsqueeze(3).to_broadcast(B4), io4, ALU.is_equal), **RC)
                R("dve", f_tt(TMb, EQb, ixv[:, :, 0, :].unsqueeze(2).to_broadcast(B4), ALU.mult), **RT)
                R("dve", f_red(selv(0), TMb), **RT)
                R("dve", f_tt(EQb, bq[:].unsqueeze(3).to_broadcast(B4), io4, ALU.is_equal), **RC)
                R("dve", f_tt(TMb, EQb, ixv[:, :, 1, :].unsqueeze(2).to_broadcast(B4), ALU.mult), **RT)
                R("dve", f_red(selv(1), TMb), **RT)
            return L

        def rtail(gi):
            gt = groups[gi]
            L = []
            R = lambda eng, fn, **kw: L.append(lambda: P.op(eng, fn, **kw))
            for ti, t in enumerate(gt):
                sc, sel = scs[ti], sels[ti]
                RT = dict(reads=["rt"] + ["sca%d" % h for h in range(8)] + ["scb%d" % h for h in range(8)], writes=["rt"])
                selg = sel[:, 2, :].rearrange("p (h k) -> p h k", h=8)
                R("dve", f_tt(ex3[:], sc[:], sc[:, :, 0:1].to_broadcast(B3), ALU.subtract), **RT)
                R("act", f_act(ex3[:], ex3[:], AF.Exp), **RT)
                R("dve", f_red(Z[:], ex3[:]), **RT)
                R("dve", f_rcp(rZ[:], Z[:]), **RT)
                R("dve", f_tt(selg, ex3[:], rZ[:].unsqueeze(2).to_broadcast(B3), ALU.mult), **RT)
                for m in range(3):
                    R("pe", f_tr(psSel[:, m, :], sel[:, m, :], ident_f[:]), reads=["rt", "consts"], writes=["rt0"], sig=(m == 2))
                R("act", f_acp(selT[:, :, ti * 128:(ti + 1) * 128], psSel), reads=["rt0"], writes=["selT"])
            return L

        prebuilt = set()

        def gb_dve(gi, u):
            t0 = u * 32
            Jm, Rm = JR[u % 2]
            jn, rn = JRn[2 * (u % 2)], JRn[2 * (u % 2) + 1]
            for tl in range(32):
                tk = t0 + tl
                P.op("dve", lambda e, tl=tl, tk=tk: e.tensor_scalar(out=Jm[:, tl, :], in0=iota_bf[:], scalar1=selT[:, 1, tk:tk + 1], scalar2=None,
                                                                   op0=ALU.is_equal),
                     reads=["selT", "consts", "rt"] if tl == 0 else ["selT"], writes=[jn, "rt"] if (u == 0 and tl == 0) else [jn], nowaw=True)
                P.op("dve", lambda e, tl=tl, tk=tk: e.tensor_scalar(out=Rm[:, tl, :], in0=iota_bf[:], scalar1=selT[:, 0, tk:tk + 1],
                                                                   scalar2=selT[:, 2, tk:tk + 1], op0=ALU.is_equal, op1=ALU.mult),
                     reads=["selT"], writes=[rn], nowaw=True)

        def gbuild(gi, extra=()):
            extra = list(extra)
            gt = groups[gi]
            T = len(gt) * 128
            nsub = T // 32
            per_sub = (len(extra) + nsub - 1) // nsub if extra else 0
            io3 = iota[:].unsqueeze(1).to_broadcast([128, 32, 128])
            nev = 0
            for u in range(T // 32):
                t0 = u * 32
                Jm, Rm = JR[u % 2]
                jn, rn = JRn[2 * (u % 2)], JRn[2 * (u % 2) + 1]
                if not (gi in prebuilt and u < 2):
                    gb_dve(gi, u)
                for t4 in range(8):
                    pb, pr = rt[nev % 2], "rt%d" % (nev % 2)
                    for t in range(4):
                        tok = t4 * 4 + t
                        P.op("pe", f_mm(pb[:, t * 128:(t + 1) * 128], Jm[:, tok, :], Rm[:, tok, :], True, True),
                             reads=[jn, rn], writes=[pr], sig=(t == 3))
                    dstv = GT[:, t0 + t4 * 4:t0 + t4 * 4 + 4, :].rearrange("p t i -> p (t i)")
                    P.op("act", f_acp(dstv, pb[:]), reads=[pr], writes=["GT"], nowaw=True)
                    nev += 1
                for _ in range(per_sub):
                    if extra:
                        extra.pop(0)()
            while extra:
                extra.pop(0)()

        def main_loop(gi, thunks, late=()):
            late = list(late)
            gt = groups[gi]
            nt = len(gt)
            T = nt * 128
            hT, hres = hnT[gi % 2], "hnT%d" % (gi % 2)
            per = (len(thunks) + 71) // 72 if thunks else 0
            pos = [0]

            def load(i):
                b = i % NB
                P.dma(ub[b][:], UB[layer, i], reads=["UVB"], writes=["ub%d" % b])
                P.dma(vb[b][:], VB[layer, i * 128:(i + 1) * 128, :], reads=["UVB"], writes=["vb%d" % b])

            def stA(i):
                b = i % NB
                for kc in range(8):
                    P.op("pe", f_mm(psA[i % 2][:, :T], ub[b][:, kc * 128:(kc + 1) * 128], hT[:, kc, :T], kc == 0, kc == 7),
                         reads=["ub%d" % b, hres], writes=["psA%d" % (i % 2)], sig=(kc == 7))

            def stB(i):
                P.op("act", f_act(gl[i % 2][:, :T], psA[i % 2][:, :T], AF.Gelu), reads=["psA%d" % (i % 2)], writes=["gl%d" % (i % 2)])
                P.op("pool", f_tt(wt[i % 3][:, :T], gl[i % 2][:, :T], GT[:, 0:T, i], ALU.mult),
                     reads=["gl%d" % (i % 2), "GT"], writes=["wt%d" % (i % 3)])

            def stV(i):
                b = i % NB
                for tt in range(nt):
                    for hf in range(2):
                        a = acc[tt * 2 + hf]
                        P.op("pe", f_mm(a[:], wt[i % 3][:, tt * 128:(tt + 1) * 128], vb[b][:, hf * 512:(hf + 1) * 512], i == 0, i == 127),
                             reads=["wt%d" % (i % 3), "vb%d" % b], writes=["acc%d" % (tt * 2 + hf)], sig=(tt == nt - 1 and hf == 1))

            for i in range(4):
                load(i)
            for i in range(130):
                if i < 128:
                    stA(i)
                if 0 <= i - 1 < 128:
                    stB(i - 1)
                if i - 2 >= 0:
                    stV(i - 2)
                if i + 4 < 128:
                    load(i + 4)
                if i == 118:
                    for ti, t in enumerate(gt):
                        P.dma(xt[ti % 2][:], H[t * 128:(t + 1) * 128, :], reads=["H"], writes=["xp%d" % (ti % 2)])
                for _ in range(per):
                    if pos[0] < len(thunks):
                        thunks[pos[0]]()
                        pos[0] += 1
                if i >= 110:
                    for _ in range(2):
                        if late and pos[0] >= len(thunks):
                            late.pop(0)()
            while pos[0] < len(thunks):
                thunks[pos[0]]()
                pos[0] += 1
            while late:
                late.pop(0)()

        def epilogue(gi):
            gt = groups[gi]
            for ti, t in enumerate(gt):
                xb, xr = xt[ti % 2], "xp%d" % (ti % 2)
                for hf in range(2):
                    P.op("act", f_acp(SX[ti][:, hf * 512:(hf + 1) * 512], acc[ti * 2 + hf][:]),
                         reads=["acc%d" % (ti * 2 + hf)], writes=["SX%d" % ti], nowaw=(hf == 1))
                P.op("pool", f_tt(xb[:], xb[:], SX[ti][:, 0:1024], ALU.add), reads=["SX%d" % ti, xr], writes=[xr])
                P.dma(dst[t * 128:(t + 1) * 128, :], xb[:], reads=[xr], writes=["H" if dst is H else "outd"], nowaw=True)

        for th in burst(0):
            th()
        for th in routing_thunks(0):
            th()
        for th in rtail(0):
            th()
        for gi in range(len(groups)):
            more = gi + 1 < len(groups)
            gbuild(gi, burst(gi + 1) if more else ())
            late = []
            if more:
                late = rtail(gi + 1)
                late.append(lambda g1=gi + 1: (gb_dve(g1, 0), gb_dve(g1, 1), prebuilt.add(g1)))
            main_loop(gi, routing_thunks(gi + 1) if more else [], late)
            epilogue(gi)
        P.barrier()


def emit_pool(nc, P, C):
    H, poolw = C["H"], C["poolw"]
    gain_bc, ident_f = C["gain_bc"], C["ident_f"]
    W = 16 + RUNT
    with ExitStack() as es:
        sb = lambda name, shape, dt: es.enter_context(nc.sbuf_tensor(name, list(shape), dt))
        ps = lambda name, shape, dt: es.enter_context(nc.psum_tensor(name, list(shape), dt))
        wp = sb("wp", [128, 4, 2, 256], BF16)
        P.dma(wp[:], poolw.rearrange("g (k p) d -> p g k d", p=128), writes=["wp"], eng="pool")
        xt = [sb("xq%d" % i, [128, D], F32) for i in range(8)]
        xh = sb("xh", [128, D], F32)
        hn = sb("hnq", [128, D], F32)
        ss = sb("ssq", [128, 1], F32)
        rstd = sb("rstdq", [128, 1], F32)
        haloT = sb("haloT", [128, 8, 128], F32)
        XT = sb("XT", [128, 8, W], F32)
        SAB = [sb("SAB%d" % i, [128, 2, W], F32) for i in range(2)]
        PT = sb("PT", [128, 8, RUNT], BF16)
        psTf = ps("psTf", [128, 8, 128], F32)
        psY = [ps("psY%d" % i, [128, D], F32) for i in range(2)]
        gain = gain_bc[:, 1, :]
        scale = gain_bc[:, 4, :]

        def norm_T(xb, xr, dstv, dres):
            P.op("act", lambda e: e.activation(out=hn[:], in_=xb[:], func=AF.Square, accum_out=ss[:]), reads=[xr], writes=["hn", "ss"])
            P.op("act", lambda e: e.activation(out=ss[:], in_=ss[:], func=AF.Sqrt, bias=EPS, scale=1.0 / D), reads=["ss"], writes=["ss"])
            P.op("dve", lambda e: e.reciprocal(out=rstd[:], in_=ss[:]), reads=["ss"], writes=["rstd"])
            P.op("dve", lambda e: e.scalar_tensor_tensor(out=hn[:], in0=xb[:], scalar=rstd[:], in1=gain, op0=ALU.mult, op1=ALU.mult),
                 reads=[xr, "rstd", "consts"], writes=["hn"])
            for kc in range(8):
                P.op("pe", lambda e, kc=kc: e.transpose(psTf[:, kc, :], hn[:, kc * 128:(kc + 1) * 128], ident_f[:]),
                     reads=["hn", "consts"], writes=["psTf"], sig=(kc == 7))
            P.op("act", lambda e: e.copy(out=dstv, in_=psTf[:]), reads=["psTf"], writes=[dres])

        P.dma(xh[:], H[NOWN:NOWN + 128, :], reads=["H"], writes=["xh"])
        norm_T(xh, "xh", haloT[:], "haloT")
        def load_run(r):
            for ti in range(4):
                t = 4 * r + ti
                bi = (r % 2) * 4 + ti
                P.dma(xt[bi][:], H[t * 128:(t + 1) * 128, :], reads=["H"], writes=["xq%d" % bi])

        load_run(0)
        for r in range(NRUN):
            if r + 1 < NRUN:
                load_run(r + 1)
            P.op("dve", lambda e: e.tensor_copy(out=XT[:, :, 0:16], in_=haloT[:, :, 16 * r:16 * r + 16]), reads=["haloT"], writes=["XT"])
            for ti in range(4):
                t = 4 * r + ti
                bi = (r % 2) * 4 + ti
                norm_T(xt[bi], "xq%d" % bi, XT[:, :, 16 + ti * 128:16 + (ti + 1) * 128], "XT")
            for g in range(4):
                w = 2 << g
                cur, cres = XT[:, 2 * g:2 * g + 2, :], "XT"
                s_, k = 1, 0
                while s_ < w:
                    nxt, nres = SAB[k % 2], "SAB%d" % (k % 2)
                    P.op("dve", lambda e: e.tensor_tensor(out=nxt[:, :, s_:W], in0=cur[:, :, s_:W], in1=cur[:, :, 0:W - s_], op=ALU.add),
                         reads=[cres], writes=[nres])
                    cur, cres = nxt[:], nres
                    s_ *= 2
                    k += 1
                P.op("dve", lambda e: e.scalar_tensor_tensor(out=PT[:, 2 * g:2 * g + 2, :], in0=cur[:, :, 16:W], scalar=1.0 / w,
                                                             in1=XT[:, 2 * g:2 * g + 2, 16:W], op0=ALU.mult, op1=ALU.subtract),
                     reads=[cres, "XT"], writes=["PT"])
            for ti in range(4):
                t = 4 * r + ti
                py, pr = psY[ti % 2], "psY%d" % (ti % 2)
                for g in range(4):
                    for k2 in range(2):
                        P.op("pe", lambda e, g=g, k2=k2: e.matmul(py[:, g * 256:(g + 1) * 256], lhsT=PT[:, 2 * g + k2, ti * 128:(ti + 1) * 128],
                                                                  rhs=wp[:, g, k2, :], start=(k2 == 0), stop=(k2 == 1)),
                             reads=["PT", "wp"], writes=[pr], sig=(g == 3 and k2 == 1))
                P.op("dve", lambda e: e.tensor_tensor(out=hn[:], in0=py[:], in1=scale, op=ALU.mult), reads=[pr, "consts"], writes=["hn"])
                bi = (r % 2) * 4 + ti
                P.op("dve", lambda e: e.tensor_tensor(out=xt[bi][:], in0=hn[:], in1=xt[bi][:], op=ALU.add),
                     reads=["hn", "xq%d" % bi], writes=["xq%d" % bi])
                P.dma(H[t * 128:(t + 1) * 128, :], xt[bi][:], reads=["xq%d" % bi], writes=["H"], nowaw=True)
        P.barrier()


def _consts():
    bf = ml_dtypes.bfloat16
    kk = np.arange(128)
    c = {}
    c["ident_bf"] = np.eye(128, dtype=np.float32).astype(bf)
    c["ident_f"] = np.eye(128, dtype=np.float32)
    c["ntri"] = (-(kk[:, None] >= kk[None, :]).astype(np.float32)).astype(bf)
    c["nones"] = (-np.ones((128, 128), np.float32)).astype(bf)
    c["iota"] = np.tile(np.arange(128, dtype=np.float32)[None, :], (128, 1))
    c["iota_bf"] = c["iota"].astype(bf)
    return c


def _masks(p):
    bf = ml_dtypes.bfloat16
    kk = np.arange(128)[:, None]
    q = np.arange(512)[None, :]
    maskd = np.zeros((128, 9, 512), np.float32)
    maskd[:112, 0, :] = NEG
    for jb in range(8):
        allowed = (128 * jb + kk) < (512 * p + q)
        maskd[:, 1 + jb, :] = np.where(allowed, 0.0, NEG)
    hmask = np.zeros((128, NHB, 128), np.float32)
    for g in range(8):
        qb = 8 * g + 4 * p
        for i in range(16):
            col = g * 16 + i
            for j in range(NHB):
                if j < qb:
                    hmask[:, j, col] = 0.0
                elif j == qb:
                    hmask[:, j, col] = np.where(np.arange(128) < 112 + i, 0.0, NEG)
                else:
                    hmask[:, j, col] = NEG
            hmask[:112, 0, col] = NEG
    return maskd.astype(bf), hmask.astype(bf)


def prepare_in_maps(inputs, cores=range(8)):
    x = np.asarray(inputs["x"], np.float32)
    meta = np.asarray(inputs["meta"], np.float32)
    consts = _consts()
    gains = np.zeros((6, D), np.float32)
    gains[0:2] = np.asarray(inputs["norm_mix"], np.float32)
    gains[2:4] = np.asarray(inputs["norm_ffn"], np.float32)
    gains[4] = np.asarray(inputs["pool_scale"], np.float32)[0]
    gains[5, 0:64] = np.asarray(inputs["sb_q_gain"], np.float32)[0]
    gains[5, 64:128] = np.asarray(inputs["sb_k_gain"], np.float32)[0]
    u = np.asarray(inputs["peer_u"], np.float32)
    uS = np.ascontiguousarray(u.reshape(2, 128, 128, 8, 128).transpose(0, 1, 4, 3, 2)).reshape(2, 2048, 8192)
    shared = dict(
        wqkv=np.ascontiguousarray(np.asarray(inputs["sb_w_qkv"], np.float32)[0]),
        wo=np.ascontiguousarray(np.asarray(inputs["sb_w_o"], np.float32)[0]),
        gains=gains,
        poolw=np.ascontiguousarray(np.asarray(inputs["pool_w"], np.float32)[0]),
        wq_p=np.ascontiguousarray(np.asarray(inputs["peer_w_q"], np.float32)),
        skT=np.ascontiguousarray(np.asarray(inputs["peer_subkeys"], np.float32).transpose(0, 1, 3, 2)),
        uS=uS,
        vS=np.ascontiguousarray(np.asarray(inputs["peer_v"], np.float32)),
        **consts,
    )
    masks = [_masks(0), _masks(1)]
    in_maps = []
    for c in cores:
        b, p = c // 2, c % 2
        xall = np.zeros((LP, D), np.float32)
        xall[112:128] = meta
        xall[128:] = x[b]
        xq = np.zeros((NTOK, D), np.float32)
        for r in range(NRUN):
            k = 2 * r + p
            xq[r * RUNT:(r + 1) * RUNT] = x[b, k * RUNT:(k + 1) * RUNT]
            xq[NOWN + 16 * r: NOWN + 16 * (r + 1)] = meta if k == 0 else x[b, k * RUNT - 16:k * RUNT]
        m = dict(shared)
        m.update(xq=xq, xall=xall, maskd=masks[p][0], hmask=masks[p][1])
        in_maps.append(m)
    return in_maps


def kernel(**inputs):
    nc = build_program()
    in_maps = prepare_in_maps(inputs)
    res = run_bass_kernel_spmd(nc, in_maps, core_ids=list(range(8)))
    out = np.zeros((4, 8192, D), np.float32)
    for c in range(8):
        b, p = c // 2, c % 2
        o = np.asarray(res.results[c]["out"], np.float32)
        for r in range(NRUN):
            k = 2 * r + p
            out[b, k * RUNT:(k + 1) * RUNT] = o[r * RUNT:(r + 1) * RUNT]
    return out
```

```python
import numpy as np
import ml_dtypes
from contextlib import ExitStack
import concourse.bass as bass
import concourse.mybir as mybir
from concourse.bass_utils import run_bass_kernel_spmd

F32 = mybir.dt.float32
BF16 = mybir.dt.bfloat16
U32 = mybir.dt.uint32
AF = mybir.ActivationFunctionType
ALU = mybir.AluOpType
AX = mybir.AxisListType

D = 1024
NRUN = 8
RUNT = 512
NOWN = NRUN * RUNT
NTOK = NOWN + 128
NTILE = NTOK // 128
LP = 8320
NKB = LP // 128
NHB = 61
NEG = -30000.0
EPS = 1e-6


class Prog:
    ENG = ("pe", "act", "dve", "pool", "sp")

    def __init__(self, nc, es):
        self.nc = nc
        self.es = es
        self.eng = {"pe": nc.tensor, "act": nc.scalar, "dve": nc.vector, "pool": nc.gpsimd, "sp": nc.sync}
        self.esem = {e: es.enter_context(nc.semaphore("prog_" + e)) for e in ("pe", "act", "dve", "pool")}
        self.ecnt = {e: 0 for e in self.esem}
        self.dsem = {}
        self.lastw = {}
        self.readers = {}
        self.waited = {e: {} for e in self.ENG}
        self.nins = 0

    def _wait(self, eng, k, val):
        kind, key = k
        if kind == "d":
            val = max(val, self.dsem[key][1])
        d = self.waited[eng]
        if d.get(k, 0) >= val:
            return
        d[k] = val
        sem = self.esem[key] if kind == "e" else self.dsem[key][0]
        self.eng[eng].wait_ge(sem, val)
        self.nins += 1

    def op(self, eng, fn, reads=(), writes=(), sig=True, dma=False, nowaw=False):
        need = {}

        def add(dct):
            for k, v in dct.items():
                if need.get(k, 0) < v:
                    need[k] = v
        for r in reads:
            add(self.lastw.get(r, {}))
        for w in writes:
            if not nowaw:
                add(self.lastw.get(w, {}))
            add(self.readers.get(w, {}))
        for k, v in need.items():
            if k[0] == "e" and k[1] == eng and eng == "pe" and not dma:
                continue
            self._wait(eng, k, v)
        ins = fn(self.eng[eng])
        self.nins += 1
        if dma:
            key = writes[0] + "|" + (reads[0] if reads else "")
            if key not in self.dsem:
                self.dsem[key] = [self.es.enter_context(self.nc.semaphore("d%d" % len(self.dsem))), 0]
            self.dsem[key][1] += 16
            ins.then_inc(self.dsem[key][0], 16)
            tok = (("d", key), self.dsem[key][1])
        elif sig:
            self.ecnt[eng] += 1
            ins.then_inc(self.esem[eng], 1)
            tok = (("e", eng), self.ecnt[eng])
        else:
            tok = (("e", eng), self.ecnt[eng] + 1)
        for r in reads:
            rd = self.readers.setdefault(r, {})
            rd[tok[0]] = max(rd.get(tok[0], 0), tok[1])
        for w in writes:
            if nowaw:
                lw = self.lastw.setdefault(w, {})
                lw[tok[0]] = max(lw.get(tok[0], 0), tok[1])
            else:
                self.lastw[w] = {tok[0]: tok[1]}
            self.readers[w] = {}
        return tok

    def dma(self, out, in_, reads=(), writes=(), eng="sp", nowaw=False):
        return self.op(eng, lambda e: e.dma_start(out=out, in_=in_), reads=reads, writes=writes, dma=True, nowaw=nowaw)

    def barrier(self):
        toks = [(("e", e), c) for e, c in self.ecnt.items() if c > 0]
        toks += [(("d", k), v[1]) for k, v in self.dsem.items()]
        for eng in self.ENG:
            for k, v in toks:
                self._wait(eng, k, v)
        self.lastw = {}
        self.readers = {}


def bview(ap, shape):
    return ap.to_broadcast(shape)


def emit_norm_T(P, tag, x_sb, x_res, gain_bc, ident_bf, hn_sb, ss_sb, rstd_sb, junk_sb, psT, dst, dst_res):
    P.op("act", lambda e: e.activation(out=junk_sb, in_=x_sb, func=AF.Square, accum_out=ss_sb),
         reads=[x_res], writes=["junk", "ss"])
    P.op("act", lambda e: e.activation(out=ss_sb, in_=ss_sb, func=AF.Sqrt, bias=EPS, scale=1.0 / D),
         reads=["ss"], writes=["ss"])
    P.op("dve", lambda e: e.reciprocal(out=rstd_sb, in_=ss_sb), reads=["ss"], writes=["rstd"])
    P.op("dve", lambda e: e.scalar_tensor_tensor(out=hn_sb, in0=x_sb, scalar=rstd_sb, in1=gain_bc, op0=ALU.mult, op1=ALU.mult),
         reads=[x_res, "rstd", "consts"], writes=["hn"])
    for kc in range(8):
        P.op("pe", lambda e, kc=kc: e.transpose(psT[:, kc, :], hn_sb[:, kc * 128:(kc + 1) * 128], ident_bf),
             reads=["hn", "consts"], writes=["psT"], sig=(kc == 7))
    P.op("act", lambda e: e.copy(out=dst, in_=psT), reads=["psT"], writes=[dst_res])


def build_program(dbg=None):
    nc = bass.Bass("TRN2", target_bir_lowering=False)
    dram_in = {}

    def din(name, shape, dt=F32):
        dram_in[name] = nc.dram_tensor(name, list(shape), dt, kind="ExternalInput").ap()
        return dram_in[name]

    xq = din("xq", [NTOK, D])
    xall = din("xall", [LP, D])
    wqkv = din("wqkv", [D, 3 * D])
    wo = din("wo", [D, D])
    gains = din("gains", [6, D])
    poolw = din("poolw", [4, 256, 256])
    wq_p = din("wq_p", [2, D, 2048])
    skT = din("skT", [2, 2, 128, 128])
    uS = din("uS", [2, 2048, 8192])
    vS = din("vS", [2, 16384, D])
    ident_bf_d = din("ident_bf", [128, 128], BF16)
    ident_f_d = din("ident_f", [128, 128])
    ntri_d = din("ntri", [128, 128], BF16)
    nones_d = din("nones", [128, 128], BF16)
    iota_d = din("iota", [128, 128])
    iotab_d = din("iota_bf", [128, 128], BF16)
    maskd_d = din("maskd", [128, 9, 512], BF16)
    hmask_d = din("hmask", [128, NHB, 128], BF16)
    out = nc.dram_tensor("out", [NOWN, D], F32, kind="ExternalOutput").ap()
    dbg_out = None
    if dbg:
        dbg_out = nc.dram_tensor("dbg", [NTOK, D], F32, kind="ExternalOutput").ap()

    KT = nc.dram_tensor("KT", [16, 64, LP], BF16).ap()
    VS = nc.dram_tensor("VSc", [16, 128, NKB, 64], BF16).ap()
    QT = nc.dram_tensor("QT", [16, 64, NTOK], BF16).ap()
    OT = nc.dram_tensor("OT", [16, 64, NTOK], BF16).ap()
    H = nc.dram_tensor("H", [NTOK, D], F32).ap()
    UB = nc.dram_tensor("UB", [2, 128, 128, 1024], BF16).ap()
    VB = nc.dram_tensor("VB", [2, 16384, D], BF16).ap()
    WQB = nc.dram_tensor("WQB", [2, 128, 8, 2048], BF16).ap()

    with ExitStack() as es:
        P = Prog(nc, es)
        sb = lambda name, shape, dt: es.enter_context(nc.sbuf_tensor(name, list(shape), dt))

        ident_bf = sb("ident_bf_s", [128, 128], BF16)
        ident_f = sb("ident_f_s", [128, 128], F32)
        ntri = sb("ntri_s", [128, 128], BF16)
        nones = sb("nones_s", [128, 128], BF16)
        iota = sb("iota_s", [128, 128], F32)
        iota_bf = sb("iota_bf_s", [128, 128], BF16)
        gain_bc = sb("gain_bc", [128, 6, D], F32)
        for dst, src in ((ident_bf, ident_bf_d), (ident_f, ident_f_d), (ntri, ntri_d), (nones, nones_d), (iota, iota_d), (iota_bf, iotab_d)):
            P.dma(dst[:], src, writes=["consts"], nowaw=True)
        for i in range(6):
            P.dma(gain_bc[:, i, :], gains[i:i + 1, :].partition_broadcast(128), writes=["consts"], nowaw=True)

        es2 = ExitStack()
        NCB = 2
        cb = [es2.enter_context(nc.sbuf_tensor("castbuf%d" % i, [128, 8192], BF16)) for i in range(NCB)]
        pre = []
        n = 0
        for l in range(2):
            usrc = uS[l]
            udst = UB[l].rearrange("i (q a) f -> (i q) (a f)", a=8)
            vsrc = vS[l].rearrange("(q a) d -> q (a d)", a=8)
            vdst = VB[l].rearrange("(q a) d -> q (a d)", a=8)
            for src, dst in ((usrc, udst), (vsrc, vdst)):
                for blk in range(16):
                    def th(src=src, dst=dst, blk=blk, n=n):
                        b, r = cb[n % NCB], "castbuf%d" % (n % NCB)
                        P.dma(b[:], src[blk * 128:(blk + 1) * 128, :], writes=[r], eng="pool")
                        P.dma(dst[blk * 128:(blk + 1) * 128, :], b[:], reads=[r], writes=["UVB"], nowaw=True)
                    pre.append(th)
                    n += 1
        for l in range(2):
            for hf in range(2):
                def th(l=l, hf=hf, n=n):
                    b, r = cb[n % NCB], "castbuf%d" % (n % NCB)
                    bv = b[:].rearrange("p (k n) -> p k n", k=8)
                    P.dma(bv, wq_p[l].rearrange("(k p) n -> p k n", p=128)[:, :, hf * 1024:(hf + 1) * 1024], writes=[r], eng="pool")
                    P.dma(WQB[l, :, :, hf * 1024:(hf + 1) * 1024], bv, reads=[r], writes=["UVB"], nowaw=True)
                pre.insert(0, th)
                n += 1
        pre_state = (pre, es2)

        C = dict(locals())
        emit_attention(nc, P, C)
        if dbg == "attn":
            emit_copy_rows(nc, P, H, dbg_out, NTILE)
            return nc
        emit_peer(nc, P, C, layer=0, ntiles=NTILE, dst=H)
        if dbg == "peer0":
            emit_copy_rows(nc, P, H, dbg_out, NTILE)
            return nc
        emit_pool(nc, P, C)
        if dbg == "pool":
            emit_copy_rows(nc, P, H, dbg_out, NTILE)
            return nc
        emit_peer(nc, P, C, layer=1, ntiles=NOWN // 128, dst=out)
        P.barrier()
        print("[kernel] instructions=%d dma_sems=%d" % (P.nins, len(P.dsem)))
    return nc


def emit_copy_rows(nc, P, src, dst, ntile):
    P.barrier()
    P.dma(dst, src, writes=["dbgout"])
    P.barrier()


def emit_attention(nc, P, C):
    xq, xall, wqkv, wo = C["xq"], C["xall"], C["wqkv"], C["wo"]
    KT, VS, QT, OT, H = C["KT"], C["VS"], C["QT"], C["OT"], C["H"]
    gain_bc, ident_bf, ntri, nones = C["gain_bc"], C["ident_bf"], C["ntri"], C["nones"]
    maskd_d, hmask_d = C["maskd_d"], C["hmask_d"]

    pre, es_pre = C["pre_state"]
    with ExitStack() as es:
        sb = lambda name, shape, dt: es.enter_context(nc.sbuf_tensor(name, list(shape), dt))
        ps = lambda name, shape, dt: es.enter_context(nc.psum_tensor(name, list(shape), dt))
        w_sb = sb("wqkv_s", [128, 8, 3072], BF16)
        P.dma(w_sb[:], wqkv.rearrange("(k p) n -> p k n", p=128), writes=["wqkv"], eng="pool")
        gq = sb("gq", [128, 64], F32)
        P.op("dve", lambda e: e.scalar_tensor_tensor(out=gq[:], in0=gain_bc[:, 5, 0:64], scalar=0.125,
                                                     in1=gain_bc[:, 5, 64:128], op0=ALU.mult, op1=ALU.mult),
             reads=["consts"], writes=["gq"])
        two = lambda name, shape, dt: [sb("%s%d" % (name, i), shape, dt) for i in range(2)]
        x_t = two("xt", [128, D], F32)
        hn = two("hn", [128, D], BF16)
        junk = two("junk", [128, D], BF16)
        ss = two("ss", [128, 1], F32)
        rstd = two("rstd", [128, 1], F32)
        hnT = two("hnT", [128, 8, 128], BF16)
        sq = two("sq", [128, D], F32)
        ssh = two("ssh", [128, 16], F32)
        rsh = two("rsh", [128, 16], F32)
        kn = two("kn", [128, D], BF16)
        knT = two("knT", [128, 8, 128], BF16)
        v_sb = two("v_sb", [128, D], BF16)
        psT_ = ps("psT", [128, 8, 128], BF16)
        psT2_ = ps("psT2", [128, 8, 128], BF16)
        psT = [psT_, psT_]
        psT2 = [psT2_, psT2_]
        psKs = [ps("psK%d" % i, [128, D], F32) for i in range(2)]
        psV = ps("psV", [128, D], F32)

        def proj(dst_ps, res, hT, hres, col0):
            for half in range(2):
                for kc in range(8):
                    P.op("pe", f_mm(dst_ps[:, half * 512:(half + 1) * 512], hT[:, kc, :],
                                    w_sb[:, kc, col0 + half * 512: col0 + (half + 1) * 512], kc == 0, kc == 7),
                         reads=[hres, "wqkv"], writes=[res], sig=(kc == 7))

        def headnorm_1(is_q, b):
            sfx = str(b)
            psK, pkr = psKs[b], "psK%d" % b
            hv = lambda ap: ap.rearrange("p (h d) -> p h d", d=64)
            P.op("act", f_act(sq[b][:], psK[:], AF.Square), reads=[pkr], writes=["sq" + sfx])
            P.op("dve", f_red(ssh[b][:], hv(sq[b][:])), reads=["sq" + sfx], writes=["ssh" + sfx])
            P.op("act", f_act(ssh[b][:], ssh[b][:], AF.Sqrt, bias=EPS, scale=1.0 / 64), reads=["ssh" + sfx], writes=["ssh" + sfx])
            P.op("dve", f_rcp(rsh[b][:], ssh[b][:]), reads=["ssh" + sfx], writes=["rsh" + sfx])
            rb = rsh[b][:].unsqueeze(2).to_broadcast([128, 16, 64])
            if is_q:
                P.op("dve", f_tt(hv(sq[b][:]), hv(psK[:]), rb, ALU.mult), reads=[pkr, "rsh" + sfx, "sq" + sfx], writes=["sq" + sfx])
                P.op("dve", f_tt(hv(kn[b][:]), hv(sq[b][:]), gq[:].unsqueeze(1).to_broadcast([128, 16, 64]), ALU.mult),
                     reads=["sq" + sfx, "gq"], writes=["kn" + sfx])
            else:
                P.op("dve", f_tt(hv(kn[b][:]), hv(psK[:]), rb, ALU.mult), reads=[pkr, "rsh" + sfx], writes=["kn" + sfx])

        def headnorm_2(n_):
            rows, is_q, t = jobs[n_]
            b = n_ % 2
            sfx = str(b)
            for pr in range(8):
                P.op("pe", f_tr(psT2[b][:, pr, :], kn[b][:, pr * 128:(pr + 1) * 128], ident_bf[:]),
                     reads=["kn" + sfx, "consts"], writes=["psT2"], sig=(pr == 7))
            P.op("dve", f_cp(knT[b][:], psT2[b][:]), reads=["psT2"], writes=["knT" + sfx])
            dstT = QT if is_q else KT
            for hh in range(2):
                P.dma(dstT[hh::2, :, t * 128:(t + 1) * 128].rearrange("h p t -> p h t"), knT[b][hh * 64:(hh + 1) * 64, :, :],
                      reads=["knT%d" % b], writes=["QT" if is_q else "KT"], nowaw=True)

        def load_x(src_rows, t):
            b = t % 2
            P.dma(x_t[b][:], src_rows, writes=["xt%d" % b])

        def norm_a(t):
            b = t % 2
            sfx = str(b)
            xb, xr = x_t[b], "xt" + sfx
            P.op("act", f_act(junk[b][:], xb[:], AF.Square, accum_out=ss[b][:]), reads=[xr], writes=["junk" + sfx, "ss" + sfx])
            P.op("act", f_act(ss[b][:], ss[b][:], AF.Sqrt, bias=EPS, scale=1.0 / D), reads=["ss" + sfx], writes=["ss" + sfx])
            P.op("dve", f_rcp(rstd[b][:], ss[b][:]), reads=["ss" + sfx], writes=["rstd" + sfx])
            P.op("dve", f_stt(hn[b][:], xb[:], rstd[b][:], gain_bc[:, 0, :], ALU.mult, ALU.mult),
                 reads=[xr, "rstd" + sfx, "consts"], writes=["hn" + sfx])

        def norm_b(t):
            b = t % 2
            sfx = str(b)
            for kc in range(8):
                P.op("pe", f_tr(psT[b][:, kc, :], hn[b][:, kc * 128:(kc + 1) * 128], ident_bf[:]),
                     reads=["hn" + sfx, "consts"], writes=["psT"], sig=(kc == 7))
            P.op("act", f_acp(hnT[b][:], psT[b][:]), reads=["psT"], writes=["hnT" + sfx])
            return hnT[b], "hnT" + sfx

        npop = [0]

        def pop_pre():
            if pre and npop[0] < 36:
                pre.pop(0)()
                npop[0] += 1

        jobs = [(xall[t * 128:(t + 1) * 128, :], False, t) for t in range(NKB)] + [(xq[t * 128:(t + 1) * 128, :], True, t) for t in range(NTILE)]
        NJ = len(jobs)
        load_x(jobs[0][0], 0)
        load_x(jobs[1][0], 1)
        norm_a(0)
        nxt = norm_b(0)
        for n_, (rows, is_q, t) in enumerate(jobs):
            b = n_ % 2
            hT, hres = nxt
            if n_ + 2 < NJ:
                load_x(jobs[n_ + 2][0], n_ + 2)
            if n_ + 1 < NJ:
                norm_a(n_ + 1)
            if not is_q:
                proj(psKs[b], "psK%d" % b, hT, hres, 1024)
                proj(psV, "psV", hT, hres, 2048)
            else:
                proj(psKs[b], "psK%d" % b, hT, hres, 0)
            if n_ >= 1:
                headnorm_2(n_ - 1)
            if n_ + 1 < NJ:
                nxt = norm_b(n_ + 1)
            if not is_q:
                P.op("act", f_acp(v_sb[b][:], psV[:]), reads=["psV"], writes=["v_sb%d" % b])
                P.dma(VS[:, :, t, :].rearrange("h p d -> p h d"), v_sb[b][:].rearrange("p (h d) -> p h d", d=64),
                      reads=["v_sb%d" % b], writes=["VS"], nowaw=True)
            headnorm_1(is_q, b)
            pop_pre()
        headnorm_2(NJ - 1)
        while pre and npop[0] < 36:
            pop_pre()
        P.barrier()

    with ExitStack() as es:
        sb = lambda name, shape, dt: es.enter_context(nc.sbuf_tensor(name, list(shape), dt))
        ps = lambda name, shape, dt: es.enter_context(nc.psum_tensor(name, list(shape), dt))
        maskd = sb("maskd_s", [128, 9, 512], BF16)
        hmask = sb("hmask_s", [128, NHB, 128], BF16)
        P.dma(maskd[:], maskd_d, writes=["masks"], nowaw=True)
        P.dma(hmask[:], hmask_d, writes=["masks"], nowaw=True)
        kt = [sb("kt%d" % i, [64, LP], BF16) for i in range(2)]
        vt = [sb("vt%d" % i, [128, NKB, 64], BF16) for i in range(2)]
        qt = [sb("qt%d" % i, [64, NTOK], BF16) for i in range(2)]
        ot = [sb("ot%d" % i, [64, NTOK], BF16) for i in range(2)]
        Eb = [sb("E%d" % i, [128, 512], BF16) for i in range(3)]
        SPb = [sb("SP%d" % i, [128, 512], BF16) for i in range(3)]
        Ab = [sb("A%d" % i, [128, 512], BF16) for i in range(3)]
        RSb = [sb("RS%d" % i, [128, 512], BF16) for i in range(2)]
        ps1 = [ps("ps1_%d" % i, [128, 512], F32) for i in range(4)]
        pso = [ps("pso%d" % i, [64, 512], F32) for i in range(2)]

        groups = [(r, r * RUNT, RUNT, 8 * r + 9) for r in range(NRUN)] + [(-1, NOWN, 128, NHB)]
        steps = []
        gidx = 0
        for h in range(16):
            for gi_, (r, off, N, nkb) in enumerate(groups):
                for s_, j in enumerate(range(nkb - 1, -1, -1)):
                    steps.append(dict(h=h, b=h % 2, r=r, off=off, N=N, nkb=nkb, s=s_, j=j, g=gidx,
                                      first=(gi_ == 0 and s_ == 0), last_of_head=(gi_ == len(groups) - 1 and s_ == nkb - 1)))
                gidx += 1

        def mask_for(d):
            r, j = d["r"], d["j"]
            if r < 0:
                return hmask[:, j, :]
            if j == 0:
                return maskd[:, 0, :]
            if j >= 8 * r + 1:
                return maskd[:, 1 + j - (8 * r + 1), :]
            return None

        def S1(G):
            d = steps[G]
            h, b, N, off, j = d["h"], d["b"], d["N"], d["off"], d["j"]
            if d["first"]:
                P.dma(kt[b][:], KT[h], reads=["KT"], writes=["kt%d" % b])
                P.dma(vt[b][:], VS[h], reads=["VS"], writes=["vt%d" % b])
                P.dma(qt[b][:], QT[h], reads=["QT"], writes=["qt%d" % b])
            if d["s"] == 0 and d["g"] % 4 == 0 and pre:
                pre.pop(0)()
            p1, pres = ps1[G % 4], "ps1_%d" % (G % 4)
            m = mask_for(d)
            P.op("pe", f_mm(p1[:, :N], kt[b][:, j * 128:(j + 1) * 128], qt[b][:, off:off + N], True, m is None),
                 reads=["kt%d" % b, "qt%d" % b], writes=[pres], sig=(m is None))
            if m is not None:
                P.op("pe", f_mm(p1[:, :N], ident_bf[:], m, False, True), reads=["masks", "consts"], writes=[pres])
            P.op("act", f_act(Eb[G % 3][:, :N], p1[:, :N], AF.Exp), reads=[pres], writes=["E%d" % (G % 3)])

        def S1b(G):
            N = steps[G]["N"]
            P.op("act", f_act(SPb[G % 3][:, :N], Eb[G % 3][:, :N], AF.Ln, bias=1.0), reads=["E%d" % (G % 3)], writes=["SP%d" % (G % 3)])

        def S2a(G):
            d = steps[G]
            N, s_, nkb = d["N"], d["s"], d["nkb"]
            p1, pres = ps1[G % 4], "ps1_%d" % (G % 4)
            P.op("pe", lambda e: e.matmul(p1[:, :N], lhsT=ntri[:], rhs=SPb[G % 3][:, :N], start=False, stop=(s_ == 0), skip_group_check=True),
                 reads=["SP%d" % (G % 3), "consts"], writes=[pres], sig=(s_ == 0))
            if s_ > 0:
                P.op("pe", lambda e: e.matmul(p1[:, :N], lhsT=nones[:], rhs=RSb[(G - 1) % 2][:, :N], start=False, stop=True, skip_group_check=True),
                     reads=["RS%d" % ((G - 1) % 2), "consts"], writes=[pres])
            if s_ < nkb - 1:
                if s_ == 0:
                    P.op("dve", f_cp(RSb[G % 2][:, :N], SPb[G % 3][:, :N]), reads=["SP%d" % (G % 3)], writes=["RS%d" % (G % 2)])
                else:
                    P.op("dve", f_tt(RSb[G % 2][:, :N], RSb[(G - 1) % 2][:, :N], SPb[G % 3][:, :N], ALU.add),
                         reads=["RS%d" % ((G - 1) % 2), "SP%d" % (G % 3)], writes=["RS%d" % (G % 2)])
            P.op("act", f_act(Ab[G % 3][:, :N], p1[:, :N], AF.Exp), reads=[pres], writes=["A%d" % (G % 3)])

        def S2b(G):
            d = steps[G]
            h, b, N, off, j, s_, nkb = d["h"], d["b"], d["N"], d["off"], d["j"], d["s"], d["nkb"]
            o_ps, ores = pso[d["g"] % 2], "pso%d" % (d["g"] % 2)
            P.op("pe", f_mm(o_ps[:, :N], vt[b][:, j, :], Ab[G % 3][:, :N], s_ == 0, s_ == nkb - 1),
                 reads=["A%d" % (G % 3), "vt%d" % b], writes=[ores], sig=(s_ == nkb - 1))
            if s_ == nkb - 1:
                P.op("dve", f_cp(ot[b][:, off:off + N], o_ps[:, :N]), reads=[ores], writes=["ot%d" % b])
            if d["last_of_head"]:
                P.dma(OT[h], ot[b][:], reads=["ot%d" % b], writes=["OT"], nowaw=True)

        NS = len(steps)
        for G in range(NS + 3):
            if G < NS:
                S1(G)
            if 0 <= G - 1 < NS:
                S1b(G - 1)
            if 0 <= G - 2 < NS:
                S2a(G - 2)
            if 0 <= G - 3 < NS:
                S2b(G - 3)
        while pre:
            pre.pop(0)()
        P.barrier()
    es_pre.close()

    with ExitStack() as es:
        sb = lambda name, shape, dt: es.enter_context(nc.sbuf_tensor(name, list(shape), dt))
        ps = lambda name, shape, dt: es.enter_context(nc.psum_tensor(name, list(shape), dt))
        wo_sb = sb("wo_s", [128, 8, D], BF16)
        P.dma(wo_sb[:], wo.rearrange("(p r) n -> r p n", r=128), writes=["wo"], eng="pool")
        OTp = OT.rearrange("(p two) d t -> p (two d) t", two=2)
        ott = [sb("ott%d" % i, [128, 8, 128], BF16) for i in range(2)]
        xt = [sb("xc%d" % i, [128, D], F32) for i in range(2)]
        res = [sb("res%d" % i, [128, D], F32) for i in range(2)]
        psC = [ps("psC%d" % i, [128, D], F32) for i in range(2)]

        def loadC(t):
            b = t % 2
            P.dma(ott[b][:], OTp[:, :, t * 128:(t + 1) * 128].rearrange("p r t -> r p t"), reads=["OT"], writes=["ott%d" % b])
            P.dma(xt[b][:], xq[t * 128:(t + 1) * 128, :], writes=["xc%d" % b])

        loadC(0)
        for t in range(NTILE):
            b = t % 2
            if t + 1 < NTILE:
                loadC(t + 1)
            for half in range(2):
                for pr in range(8):
                    P.op("pe", f_mm(psC[b][:, half * 512:(half + 1) * 512], ott[b][:, pr, :], wo_sb[:, pr, half * 512:(half + 1) * 512],
                                    pr == 0, pr == 7),
                         reads=["ott%d" % b, "wo"], writes=["psC%d" % b], sig=(pr == 7))
            P.op("dve", f_tt(res[b][:], psC[b][:], xt[b][:], ALU.add), reads=["psC%d" % b, "xc%d" % b], writes=["res%d" % b])
            P.dma(H[t * 128:(t + 1) * 128, :], res[b][:], reads=["res%d" % b], writes=["H"], nowaw=True)
        P.barrier()


def f_max(o, i): return lambda e: e.max(out=o, in_=i)
def f_mr(o, r, v): return lambda e: e.match_replace(out=o, in_to_replace=r, in_values=v, imm_value=-1e30)
def f_mi(o, m, v): return lambda e: e.max_index(out=o, in_max=m, in_values=v)
def f_tt(o, a, b, op): return lambda e: e.tensor_tensor(out=o, in0=a, in1=b, op=op)
def f_cp(o, i): return lambda e: e.tensor_copy(out=o, in_=i)
def f_acp(o, i): return lambda e: e.copy(out=o, in_=i)
def f_red(o, i): return lambda e: e.tensor_reduce(out=o, in_=i, axis=AX.X, op=ALU.add)
def f_mm(o, l, r, st, sp): return lambda e: e.matmul(o, lhsT=l, rhs=r, start=st, stop=sp)
def f_tr(o, i, ident): return lambda e: e.transpose(o, i, ident)
def f_act(o, i, func, **kw): return lambda e: e.activation(out=o, in_=i, func=func, **kw)
def f_stt(o, a, sc, b, op0, op1): return lambda e: e.scalar_tensor_tensor(out=o, in0=a, scalar=sc, in1=b, op0=op0, op1=op1)
def f_rcp(o, i): return lambda e: e.reciprocal(out=o, in_=i)


def emit_peer(nc, P, C, layer, ntiles, dst):
    H, WQB, skT, UB, VB = C["H"], C["WQB"], C["skT"], C["UB"], C["VB"]
    gain_bc, ident_bf, ident_f, iota, iota_bf = C["gain_bc"], C["ident_bf"], C["ident_f"], C["iota"], C["iota_bf"]
    NT = 2
    TM = NT * 128
    NBU, NBV = 5, 7
    with ExitStack() as es:
        sb = lambda name, shape, dt: es.enter_context(nc.sbuf_tensor("%s_L%d" % (name, layer), list(shape), dt))
        ps = lambda name, shape, dt: es.enter_context(nc.psum_tensor("%s_L%d" % (name, layer), list(shape), dt))
        big = sb("big", [128, 16384], BF16)
        bigf = big[:].bitcast(F32)
        S, S2, EQ, TMP = (bigf[:, i * 2048:(i + 1) * 2048] for i in range(4))
        JR = [(big[:, (2 * x) * 4096:(2 * x + 1) * 4096].rearrange("p (t j) -> p t j", j=128),
               big[:, (2 * x + 1) * 4096:(2 * x + 2) * 4096].rearrange("p (t j) -> p t j", j=128)) for x in range(2)]
        JRn = ["JmA", "RmA", "JmB", "RmB"]
        GT = sb("GT", [128, TM, 128], BF16)
        hnT = [sb("hnTp%d" % i, [128, 8, TM], BF16) for i in range(2)]
        qT = sb("qT", [128, 16, TM], BF16)
        wqs = [sb("wqs%d" % i, [128, 8, 128], BF16) for i in range(2)]
        skT_sb = sb("skT_s", [128, 2, 128], BF16)
        P.dma(skT_sb[:], skT[layer].rearrange("s d n -> d s n"), writes=["skT"], eng="pool")
        xt = [sb("xp%d" % i, [128, D], F32) for i in range(2)]
        hn = sb("hnp", [128, D], BF16)
        ss = sb("ssp", [128, 1], F32)
        rstd = sb("rstdp", [128, 1], F32)
        mx = sb("mx", [128, 16, 16], F32)
        ixu = sb("ixu", [128, 16, 16], U32)
        ixf = sb("ixf", [128, 16, 16], F32)
        posu = sb("posu", [128, 8, 16], U32)
        posf = sb("posf", [128, 8, 16], F32)
        aq = sb("aq", [128, 8, 16], F32)
        bq = sb("bq", [128, 8, 16], F32)
        ex3 = sb("ex3", [128, 8, 16], F32)
        Z = sb("Z", [128, 8], F32)
        rZ = sb("rZ", [128, 8], F32)
        selT = sb("selT", [128, 3, TM], F32)
        ub = [sb("ub%d" % i, [128, 1024], BF16) for i in range(NBU)]
        vb = [sb("vb%d" % i, [128, 1024], BF16) for i in range(NBV)]
        gl = [sb("gl%d" % i, [128, TM], BF16) for i in range(2)]
        wt = [sb("wt%d" % i, [128, TM], BF16) for i in range(3)]
        c16 = sb("c16", [128, 32], F32)
        P.op("dve", lambda e: e.tensor_single_scalar(out=c16[:], in_=iota[:, 0:32], scalar=16.0, op=ALU.mult), reads=["consts"], writes=["c16"])
        acc = [ps("acc%d" % i, [128, 512], F32) for i in range(4)]
        psA = [ps("psA%d" % i, [128, 512], F32) for i in range(2)]
        rt = [ps("rtb%d" % i, [128, 512], F32) for i in range(2)]
        psT = rt[0][:].bitcast(BF16)[:, 0:1024].rearrange("p (k t) -> p k t", k=8)
        psSel = rt[0][:, 0:384].rearrange("p (m t) -> p m t", m=3)
        gain = gain_bc[:, 2 + layer, :]

        tiles = list(range(ntiles))
        groups = [tiles[i:i + NT] for i in range(0, ntiles, NT)]
        B4 = [128, 8, 16, 16]
        B3 = [128, 8, 16]

        SX = [sb("SX%d" % i, [128, 2048], F32)[:] for i in range(2)]
        S3 = EQ
        bigb = big[:]
        EQb = bigb[:, 12288:14336].rearrange("p (h k a) -> p h k a", h=8, k=16)
        TMb = bigb[:, 14336:16384].rearrange("p (h k a) -> p h k a", h=8, k=16)
        scs = [sb("sc%d" % i, [128, 8, 16], F32) for i in range(2)]
        sels = [sb("sel%d" % i, [128, 3, 128], F32) for i in range(2)]

        bT = acc[0][:].bitcast(BF16)[:, 0:1024].rearrange("p (k t) -> p k t", k=8)

        def burst(gi):
            gt = groups[gi]
            T = len(gt) * 128
            hT, hres = hnT[gi % 2], "hnT%d" % (gi % 2)
            L = []
            R = lambda eng, fn, **kw: L.append(lambda: P.op(eng, fn, **kw))
            for ti, t in enumerate(gt):
                xb, xr = xt[ti % 2], "xp%d" % (ti % 2)
                L.append(lambda xb=xb, xr=xr, t=t: P.dma(xb[:], H[t * 128:(t + 1) * 128, :], reads=["H"], writes=[xr]))
                R("act", f_act(hn[:], xb[:], AF.Square, accum_out=ss[:]), reads=[xr], writes=["hn", "ss"])
                R("act", f_act(ss[:], ss[:], AF.Sqrt, bias=EPS, scale=1.0 / D), reads=["ss"], writes=["ss"])
                R("dve", f_rcp(rstd[:], ss[:]), reads=["ss"], writes=["rstd"])
                R("dve", f_stt(hn[:], xb[:], rstd[:], gain, ALU.mult, ALU.mult), reads=[xr, "rstd", "consts"], writes=["hn"])
                for kc in range(8):
                    R("pe", f_tr(bT[:, kc, :], hn[:, kc * 128:(kc + 1) * 128], ident_bf[:]), reads=["hn", "consts"], writes=["acc0"], sig=(kc == 7))
                R("act", f_acp(hT[:, :, ti * 128:(ti + 1) * 128], bT), reads=["acc0"], writes=[hres])
            for g in range(16):
                wb, wr = wqs[g % 2], "wqs%d" % (g % 2)
                L.append(lambda wb=wb, wr=wr, g=g: P.dma(wb[:], WQB[layer, :, :, g * 128:(g + 1) * 128], reads=["UVB"], writes=[wr]))
                pq, pr = acc[1 + g % 2], "acc%d" % (1 + g % 2)
                for kc in range(8):
                    R("pe", f_mm(pq[:, :T], wb[:, kc, :], hT[:, kc, :T], kc == 0, kc == 7), reads=[wr, hres], writes=[pr], sig=(kc == 7))
                R("act", f_acp(qT[:, g, :T], pq[:, :T]), reads=[pr], writes=["qT"])
            for ti, t in enumerate(gt):
                for q4 in range(4):
                    pq, pr = acc[1 + q4 % 2], "acc%d" % (1 + q4 % 2)
                    for gg in range(4):
                        g = q4 * 4 + gg
                        R("pe", f_mm(pq[:, gg * 128:(gg + 1) * 128], qT[:, g, ti * 128:(ti + 1) * 128], skT_sb[:, g % 2, :], True, True),
                          reads=["qT", "skT"], writes=[pr], sig=(gg == 3))
                    R("act", f_acp(SX[ti][:, q4 * 512:(q4 + 1) * 512], pq[:]), reads=[pr], writes=["SX%d" % ti])
            return L

        def routing_thunks(gi):
            gt = groups[gi]
            L = []
            R = lambda eng, fn, **kw: L.append(lambda: P.op(eng, fn, **kw))
            for ti, t in enumerate(gt):
                RT = dict(reads=["rt", "SX%d" % ti], writes=["rt", "SX%d" % ti] + (JRn if ti == 0 else []))
                RC = dict(reads=["rt", "consts", "c16", "SX%d" % ti], writes=["rt", "SX%d" % ti])
                Sx = SX[ti]
                sc, sel = scs[ti], sels[ti]
                sxn = "SX%d" % ti
                G16 = range(16)
                sl16 = [slice(g * 128, (g + 1) * 128) for g in G16]
                for g in G16:
                    R("dve", f_max(mx[:, g, 0:8], Sx[:, sl16[g]]), reads=[sxn, "rt"], writes=["mxa%d" % g])
                for g in G16:
                    R("dve", f_mr(S3[:, sl16[g]], mx[:, g, 0:8], Sx[:, sl16[g]]), reads=[sxn, "mxa%d" % g, "rt"], writes=["S3_%d" % g] + (JRn if ti == 0 else []))
                for g in G16:
                    R("dve", f_max(mx[:, g, 8:16], S3[:, sl16[g]]), reads=["S3_%d" % g], writes=["mxb%d" % g])
                for g in G16:
                    R("dve", f_mi(ixu[:, g, 0:8], mx[:, g, 0:8], Sx[:, sl16[g]]), reads=[sxn, "mxa%d" % g], writes=["ixa%d" % g])
                for g in G16:
                    R("dve", f_mi(ixu[:, g, 8:16], mx[:, g, 8:16], S3[:, sl16[g]]), reads=["S3_%d" % g, "mxb%d" % g], writes=["ixb%d" % g])
                allmx = ["mxa%d" % g for g in G16] + ["mxb%d" % g for g in G16]
                allix = ["ixa%d" % g for g in G16] + ["ixb%d" % g for g in G16]
                alls3 = ["S3_%d" % g for g in G16]
                R("dve", f_cp(ixf[:], ixu[:]), reads=allix + ["rt"], writes=["rt"])
                mxv = mx[:].rearrange("p (h two) k -> p h two k", two=2)
                ixv = ixf[:].rearrange("p (h two) k -> p h two k", two=2)
                cand4 = Sx.rearrange("p (h a b) -> p h a b", h=8, a=16)
                R("dve", f_tt(cand4, mxv[:, :, 0, :].unsqueeze(3).to_broadcast(B4), mxv[:, :, 1, :].unsqueeze(2).to_broadcast(B4), ALU.add),
                  reads=allmx + allix + [sxn, "rt"], writes=[sxn, "rt"])
                H8 = range(8)
                sl8 = [slice(h * 256, (h + 1) * 256) for h in H8]
                for h in H8:
                    R("dve", f_max(sc[:, h, 0:8], Sx[:, sl8[h]]), reads=[sxn, "rt"], writes=["sca%d" % h])
                for h in H8:
                    R("dve", f_mr(S3[:, sl8[h]], sc[:, h, 0:8], Sx[:, sl8[h]]), reads=[sxn, "sca%d" % h] + alls3[2 * h:2 * h + 2],
                      writes=alls3[2 * h:2 * h + 2])
                for h in H8:
                    R("dve", f_max(sc[:, h, 8:16], S3[:, sl8[h]]), reads=alls3[2 * h:2 * h + 2], writes=["scb%d" % h])
                for h in H8:
                    R("dve", f_mi(posu[:, h, 0:8], sc[:, h, 0:8], Sx[:, sl8[h]]), reads=[sxn, "sca%d" % h, "rt"], writes=["posa%d" % h])
                for h in H8:
                    R("dve", f_mi(posu[:, h, 8:16], sc[:, h, 8:16], S3[:, sl8[h]]), reads=alls3[2 * h:2 * h + 2] + ["scb%d" % h, "rt"], writes=["posb%d" % h])
                allpos = ["posa%d" % h for h in H8] + ["posb%d" % h for h in H8]
                allsc = ["sca%d" % h for h in H8] + ["scb%d" % h for h in H8]
                RT = dict(reads=["rt", sxn] + allpos + allsc + alls3 + allmx, writes=["rt", sxn] + allpos + alls3 + (JRn if ti == 0 else []))
                RC = dict(reads=RT["reads"] + ["consts", "c16"], writes=RT["writes"])
                R("dve", f_cp(posf[:], posu[:]), **RT)
                io4 = iota[:, 0:16].unsqueeze(1).unsqueeze(1).to_broadcast(B4)
                selv = lambda m, sel=sel: sel[:, m, :].rearrange("p (h k) -> p h k", h=8)
                R("dve", lambda e: e.tensor_single_scalar(out=aq[:], in_=posf[:], scalar=0.0625, op=ALU.mult), **RT)
                R("dve", f_cp(posu[:], aq[:]), **RT)
                R("dve", f_cp(aq[:], posu[:]), **RT)
                R("dve", lambda e: e.tensor_single_scalar(out=bq[:], in_=aq[:], scalar=16.0, op=ALU.mult), **RT)
                R("dve", f_tt(bq[:], bq[:], posf[:], ALU.is_gt), **RT)
                R("dve", f_tt(aq[:], aq[:], bq[:], ALU.subtract), **RT)
                R("dve", f_stt(bq[:], aq[:], -16.0, posf[:], ALU.mult, ALU.add), **RT)
                R("dve", f_tt(EQb, aq[:].unsqueeze(3).to_broadcast(B4), io4, ALU.is_equal), **RC)
                R("dve", f_tt(TMb, EQb, ixv[:, :, 0, :].unsqueeze(2).to_broadcast(B4), ALU.mult), **RT)
                R("dve", f_red(selv(0), TMb), **RT)
                R("dve", f_tt(EQb, bq[:].unsqueeze(3).to_broadcast(B4), io4, ALU.is_equal), **RC)
                R("dve", f_tt(TMb, EQb, ixv[:, :, 1, :].unsqueeze(2).to_broadcast(B4), ALU.mult), **RT)
                R("dve", f_red(selv(1), TMb), **RT)
            return L

        def rtail(gi):
            gt = groups[gi]
            L = []
            R = lambda eng, fn, **kw: L.append(lambda: P.op(eng, fn, **kw))
            for ti, t in enumerate(gt):
                sc, sel = scs[ti], sels[ti]
                RT = dict(reads=["rt"] + ["sca%d" % h for h in range(8)] + ["scb%d" % h for h in range(8)], writes=["rt"])
                selg = sel[:, 2, :].rearrange("p (h k) -> p h k", h=8)
                R("dve", f_tt(ex3[:], sc[:], sc[:, :, 0:1].to_broadcast(B3), ALU.subtract), **RT)
                R("act", f_act(ex3[:], ex3[:], AF.Exp), **RT)
                R("dve", f_red(Z[:], ex3[:]), **RT)
                R("dve", f_rcp(rZ[:], Z[:]), **RT)
                R("dve", f_tt(selg, ex3[:], rZ[:].unsqueeze(2).to_broadcast(B3), ALU.mult), **RT)
                for m in range(3):
                    R("pe", f_tr(psSel[:, m, :], sel[:, m, :], ident_f[:]), reads=["rt", "consts"], writes=["rt0"], sig=(m == 2))
                R("act", f_acp(selT[:, :, ti * 128:(ti + 1) * 128], psSel), reads=["rt0"], writes=["selT"])
            return L

        prebuilt = set()

        def gb_dve(gi, u):
            t0 = u * 32
            Jm, Rm = JR[u % 2]
            jn, rn = JRn[2 * (u % 2)], JRn[2 * (u % 2) + 1]
            for tl in range(32):
                tk = t0 + tl
                P.op("dve", lambda e, tl=tl, tk=tk: e.tensor_scalar(out=Jm[:, tl, :], in0=iota_bf[:], scalar1=selT[:, 1, tk:tk + 1], scalar2=None,
                                                                   op0=ALU.is_equal),
                     reads=["selT", "consts", "rt"] if tl == 0 else ["selT"], writes=[jn, "rt"] if (u == 0 and tl == 0) else [jn], nowaw=True)
                P.op("dve", lambda e, tl=tl, tk=tk: e.tensor_scalar(out=Rm[:, tl, :], in0=iota_bf[:], scalar1=selT[:, 0, tk:tk + 1],
                                                                   scalar2=selT[:, 2, tk:tk + 1], op0=ALU.is_equal, op1=ALU.mult),
                     reads=["selT"], writes=[rn], nowaw=True)

        def gbuild(gi, extra=()):
            extra = list(extra)
            gt = groups[gi]
            T = len(gt) * 128
            nsub = T // 32
            per_sub = (len(extra) + nsub - 1) // nsub if extra else 0
            io3 = iota[:].unsqueeze(1).to_broadcast([128, 32, 128])
            nev = 0
            for u in range(T // 32):
                t0 = u * 32
                Jm, Rm = JR[u % 2]
                jn, rn = JRn[2 * (u % 2)], JRn[2 * (u % 2) + 1]
                if not (gi in prebuilt and u < 2):
                    gb_dve(gi, u)
                for t4 in range(8):
                    pb, pr = rt[nev % 2], "rt%d" % (nev % 2)
                    for t in range(4):
                        tok = t4 * 4 + t
                        P.op("pe", f_mm(pb[:, t * 128:(t + 1) * 128], Jm[:, tok, :], Rm[:, tok, :], True, True),
                             reads=[jn, rn], writes=[pr], sig=(t == 3))
                    dstv = GT[:, t0 + t4 * 4:t0 + t4 * 4 + 4, :].rearrange("p t i -> p (t i)")
                    P.op("act", f_acp(dstv, pb[:]), reads=[pr], writes=["GT"], nowaw=True)
                    nev += 1
                for _ in range(per_sub):
                    if extra:
                        extra.pop(0)()
            while extra:
                extra.pop(0)()

        def main_loop(gi, thunks, late=()):
            late = list(late)
            gt = groups[gi]
            nt = len(gt)
            T = nt * 128
            hT, hres = hnT[gi % 2], "hnT%d" % (gi % 2)
            per = (len(thunks) + 71) // 72 if thunks else 0
            pos = [0]

            def load(i):
                bu, bv = i % NBU, i % NBV
                P.dma(ub[bu][:], UB[layer, i], reads=["UVB"], writes=["ub%d" % bu])
                P.dma(vb[bv][:], VB[layer, i * 128:(i + 1) * 128, :], reads=["UVB"], writes=["vb%d" % bv])

            def stA(i):
                b = i % NBU
                for kc in range(8):
                    P.op("pe", f_mm(psA[i % 2][:, :T], ub[b][:, kc * 128:(kc + 1) * 128], hT[:, kc, :T], kc == 0, kc == 7),
                         reads=["ub%d" % b, hres], writes=["psA%d" % (i % 2)], sig=(kc == 7))

            def stB(i):
                P.op("act", f_act(gl[i % 2][:, :T], psA[i % 2][:, :T], AF.Gelu), reads=["psA%d" % (i % 2)], writes=["gl%d" % (i % 2)])
                P.op("pool", f_tt(wt[i % 3][:, :T], gl[i % 2][:, :T], GT[:, 0:T, i], ALU.mult),
                     reads=["gl%d" % (i % 2), "GT"], writes=["wt%d" % (i % 3)])

            def stV(i):
                b = i % NBV
                for tt in range(nt):
                    for hf in range(2):
                        a = acc[tt * 2 + hf]
                        P.op("pe", f_mm(a[:], wt[i % 3][:, tt * 128:(tt + 1) * 128], vb[b][:, hf * 512:(hf + 1) * 512], i == 0, i == 127),
                             reads=["wt%d" % (i % 3), "vb%d" % b], writes=["acc%d" % (tt * 2 + hf)], sig=(tt == nt - 1 and hf == 1))

            for i in range(5):
                load(i)
            for i in range(130):
                if i < 128:
                    stA(i)
                if 0 <= i - 1 < 128:
                    stB(i - 1)
                if i - 2 >= 0:
                    stV(i - 2)
                if i + 5 < 128:
                    load(i + 5)
                if i == 118:
                    for ti, t in enumerate(gt):
                        P.dma(xt[ti % 2][:], H[t * 128:(t + 1) * 128, :], reads=["H"], writes=["xp%d" % (ti % 2)])
                for _ in range(per):
                    if pos[0] < len(thunks):
                        thunks[pos[0]]()
                        pos[0] += 1
                if i >= 110:
                    for _ in range(2):
                        if late and pos[0] >= len(thunks):
                            late.pop(0)()
            while pos[0] < len(thunks):
                thunks[pos[0]]()
                pos[0] += 1
            while late:
                late.pop(0)()

        def epilogue(gi):
            gt = groups[gi]
            for ti, t in enumerate(gt):
                xb, xr = xt[ti % 2], "xp%d" % (ti % 2)
                for hf in range(2):
                    P.op("act", f_acp(SX[ti][:, hf * 512:(hf + 1) * 512], acc[ti * 2 + hf][:]),
                         reads=["acc%d" % (ti * 2 + hf)], writes=["SX%d" % ti], nowaw=(hf == 1))
                P.op("pool", f_tt(xb[:], xb[:], SX[ti][:, 0:1024], ALU.add), reads=["SX%d" % ti, xr], writes=[xr])
                P.dma(dst[t * 128:(t + 1) * 128, :], xb[:], reads=[xr], writes=["H" if dst is H else "outd"], nowaw=True)

        for th in burst(0):
            th()
        for th in routing_thunks(0):
            th()
        for th in rtail(0):
            th()
        for gi in range(len(groups)):
            more = gi + 1 < len(groups)
            gbuild(gi, burst(gi + 1) if more else ())
            late = []
            if more:
                late = rtail(gi + 1)
                late.append(lambda g1=gi + 1: (gb_dve(g1, 0), gb_dve(g1, 1), prebuilt.add(g1)))
            main_loop(gi, routing_thunks(gi + 1) if more else [], late)
            epilogue(gi)
        P.barrier()


def emit_pool(nc, P, C):
    H, poolw = C["H"], C["poolw"]
    gain_bc, ident_f = C["gain_bc"], C["ident_f"]
    W = 16 + RUNT
    with ExitStack() as es:
        sb = lambda name, shape, dt: es.enter_context(nc.sbuf_tensor(name, list(shape), dt))
        ps = lambda name, shape, dt: es.enter_context(nc.psum_tensor(name, list(shape), dt))
        wp = sb("wp", [128, 4, 2, 256], BF16)
        P.dma(wp[:], poolw.rearrange("g (k p) d -> p g k d", p=128), writes=["wp"], eng="pool")
        xt = [sb("xq%d" % i, [128, D], F32) for i in range(8)]
        xh = sb("xh", [128, D], F32)
        hn = sb("hnq", [128, D], F32)
        ss = sb("ssq", [128, 1], F32)
        rstd = sb("rstdq", [128, 1], F32)
        haloT = sb("haloT", [128, 8, 128], F32)
        XT = sb("XT", [128, 8, W], F32)
        SAB = [sb("SAB%d" % i, [128, 2, W], F32) for i in range(2)]
        PT = sb("PT", [128, 8, RUNT], BF16)
        psTf = ps("psTf", [128, 8, 128], F32)
        psY = [ps("psY%d" % i, [128, D], F32) for i in range(2)]
        gain = gain_bc[:, 1, :]
        scale = gain_bc[:, 4, :]

        def norm_T(xb, xr, dstv, dres):
            P.op("act", lambda e: e.activation(out=hn[:], in_=xb[:], func=AF.Square, accum_out=ss[:]), reads=[xr], writes=["hn", "ss"])
            P.op("act", lambda e: e.activation(out=ss[:], in_=ss[:], func=AF.Sqrt, bias=EPS, scale=1.0 / D), reads=["ss"], writes=["ss"])
            P.op("dve", lambda e: e.reciprocal(out=rstd[:], in_=ss[:]), reads=["ss"], writes=["rstd"])
            P.op("dve", lambda e: e.scalar_tensor_tensor(out=hn[:], in0=xb[:], scalar=rstd[:], in1=gain, op0=ALU.mult, op1=ALU.mult),
                 reads=[xr, "rstd", "consts"], writes=["hn"])
            for kc in range(8):
                P.op("pe", lambda e, kc=kc: e.transpose(psTf[:, kc, :], hn[:, kc * 128:(kc + 1) * 128], ident_f[:]),
                     reads=["hn", "consts"], writes=["psTf"], sig=(kc == 7))
            P.op("act", lambda e: e.copy(out=dstv, in_=psTf[:]), reads=["psTf"], writes=[dres])

        P.dma(xh[:], H[NOWN:NOWN + 128, :], reads=["H"], writes=["xh"])
        norm_T(xh, "xh", haloT[:], "haloT")
        def load_run(r):
            for ti in range(4):
                t = 4 * r + ti
                bi = (r % 2) * 4 + ti
                P.dma(xt[bi][:], H[t * 128:(t + 1) * 128, :], reads=["H"], writes=["xq%d" % bi])

        load_run(0)
        for r in range(NRUN):
            if r + 1 < NRUN:
                load_run(r + 1)
            P.op("dve", lambda e: e.tensor_copy(out=XT[:, :, 0:16], in_=haloT[:, :, 16 * r:16 * r + 16]), reads=["haloT"], writes=["XT"])
            for ti in range(4):
                t = 4 * r + ti
                bi = (r % 2) * 4 + ti
                norm_T(xt[bi], "xq%d" % bi, XT[:, :, 16 + ti * 128:16 + (ti + 1) * 128], "XT")
            for g in range(4):
                w = 2 << g
                cur, cres = XT[:, 2 * g:2 * g + 2, :], "XT"
                s_, k = 1, 0
                while s_ < w:
                    nxt, nres = SAB[k % 2], "SAB%d" % (k % 2)
                    P.op("dve", lambda e: e.tensor_tensor(out=nxt[:, :, s_:W], in0=cur[:, :, s_:W], in1=cur[:, :, 0:W - s_], op=ALU.add),
                         reads=[cres], writes=[nres])
                    cur, cres = nxt[:], nres
                    s_ *= 2
                    k += 1
                P.op("dve", lambda e: e.scalar_tensor_tensor(out=PT[:, 2 * g:2 * g + 2, :], in0=cur[:, :, 16:W], scalar=1.0 / w,
                                                             in1=XT[:, 2 * g:2 * g + 2, 16:W], op0=ALU.mult, op1=ALU.subtract),
                     reads=[cres, "XT"], writes=["PT"])
            for ti in range(4):
                t = 4 * r + ti
                py, pr = psY[ti % 2], "psY%d" % (ti % 2)
                for g in range(4):
                    for k2 in range(2):
                        P.op("pe", lambda e, g=g, k2=k2: e.matmul(py[:, g * 256:(g + 1) * 256], lhsT=PT[:, 2 * g + k2, ti * 128:(ti + 1) * 128],
                                                                  rhs=wp[:, g, k2, :], start=(k2 == 0), stop=(k2 == 1)),
                             reads=["PT", "wp"], writes=[pr], sig=(g == 3 and k2 == 1))
                P.op("dve", lambda e: e.tensor_tensor(out=hn[:], in0=py[:], in1=scale, op=ALU.mult), reads=[pr, "consts"], writes=["hn"])
                bi = (r % 2) * 4 + ti
                P.op("dve", lambda e: e.tensor_tensor(out=xt[bi][:], in0=hn[:], in1=xt[bi][:], op=ALU.add),
                     reads=["hn", "xq%d" % bi], writes=["xq%d" % bi])
                P.dma(H[t * 128:(t + 1) * 128, :], xt[bi][:], reads=["xq%d" % bi], writes=["H"], nowaw=True)
        P.barrier()


def _consts():
    bf = ml_dtypes.bfloat16
    kk = np.arange(128)
    c = {}
    c["ident_bf"] = np.eye(128, dtype=np.float32).astype(bf)
    c["ident_f"] = np.eye(128, dtype=np.float32)
    c["ntri"] = (-(kk[:, None] >= kk[None, :]).astype(np.float32)).astype(bf)
    c["nones"] = (-np.ones((128, 128), np.float32)).astype(bf)
    c["iota"] = np.tile(np.arange(128, dtype=np.float32)[None, :], (128, 1))
    c["iota_bf"] = c["iota"].astype(bf)
    return c


def _masks(p):
    bf = ml_dtypes.bfloat16
    kk = np.arange(128)[:, None]
    q = np.arange(512)[None, :]
    maskd = np.zeros((128, 9, 512), np.float32)
    maskd[:112, 0, :] = NEG
    for jb in range(8):
        allowed = (128 * jb + kk) < (512 * p + q)
        maskd[:, 1 + jb, :] = np.where(allowed, 0.0, NEG)
    hmask = np.zeros((128, NHB, 128), np.float32)
    for g in range(8):
        qb = 8 * g + 4 * p
        for i in range(16):
            col = g * 16 + i
            for j in range(NHB):
                if j < qb:
                    hmask[:, j, col] = 0.0
                elif j == qb:
                    hmask[:, j, col] = np.where(np.arange(128) < 112 + i, 0.0, NEG)
                else:
                    hmask[:, j, col] = NEG
            hmask[:112, 0, col] = NEG
    return maskd.astype(bf), hmask.astype(bf)


def prepare_in_maps(inputs, cores=range(8)):
    x = np.asarray(inputs["x"], np.float32)
    meta = np.asarray(inputs["meta"], np.float32)
    consts = _consts()
    gains = np.zeros((6, D), np.float32)
    gains[0:2] = np.asarray(inputs["norm_mix"], np.float32)
    gains[2:4] = np.asarray(inputs["norm_ffn"], np.float32)
    gains[4] = np.asarray(inputs["pool_scale"], np.float32)[0]
    gains[5, 0:64] = np.asarray(inputs["sb_q_gain"], np.float32)[0]
    gains[5, 64:128] = np.asarray(inputs["sb_k_gain"], np.float32)[0]
    u = np.asarray(inputs["peer_u"], np.float32)
    uS = np.ascontiguousarray(u.reshape(2, 128, 128, 8, 128).transpose(0, 1, 4, 3, 2)).reshape(2, 2048, 8192)
    shared = dict(
        wqkv=np.ascontiguousarray(np.asarray(inputs["sb_w_qkv"], np.float32)[0]),
        wo=np.ascontiguousarray(np.asarray(inputs["sb_w_o"], np.float32)[0]),
        gains=gains,
        poolw=np.ascontiguousarray(np.asarray(inputs["pool_w"], np.float32)[0]),
        wq_p=np.ascontiguousarray(np.asarray(inputs["peer_w_q"], np.float32)),
        skT=np.ascontiguousarray(np.asarray(inputs["peer_subkeys"], np.float32).transpose(0, 1, 3, 2)),
        uS=uS,
        vS=np.ascontiguousarray(np.asarray(inputs["peer_v"], np.float32)),
        **consts,
    )
    masks = [_masks(0), _masks(1)]
    in_maps = []
    for c in cores:
        b, p = c // 2, c % 2
        xall = np.zeros((LP, D), np.float32)
        xall[112:128] = meta
        xall[128:] = x[b]
        xq = np.zeros((NTOK, D), np.float32)
        for r in range(NRUN):
            k = 2 * r + p
            xq[r * RUNT:(r + 1) * RUNT] = x[b, k * RUNT:(k + 1) * RUNT]
            xq[NOWN + 16 * r: NOWN + 16 * (r + 1)] = meta if k == 0 else x[b, k * RUNT - 16:k * RUNT]
        m = dict(shared)
        m.update(xq=xq, xall=xall, maskd=masks[p][0], hmask=masks[p][1])
        in_maps.append(m)
    return in_maps


def kernel(**inputs):
    nc = build_program()
    in_maps = prepare_in_maps(inputs)
    res = run_bass_kernel_spmd(nc, in_maps, core_ids=list(range(8)))
    out = np.zeros((4, 8192, D), np.float32)
    for c in range(8):
        b, p = c // 2, c % 2
        o = np.asarray(res.results[c]["out"], np.float32)
        for r in range(NRUN):
            k = 2 * r + p
            out[b, k * RUNT:(k + 1) * RUNT] = o[r * RUNT:(r + 1) * RUNT]
    return out
```
